# Optimizing a Trainium2 kernel written in Bass

```python
import jax, jax.numpy as jnp
from jax import lax
import numpy as np

D_MODEL = 1024
BATCH = 16
SEQ = 2048
DEPTH = 4

N_MIXERS = 2
N_ATTN_LAYERS = (DEPTH + N_MIXERS - 1) // N_MIXERS
N_GMLP_LAYERS = DEPTH // N_MIXERS
HEAD_DIM = 64
N_HEADS = D_MODEL // HEAD_DIM
N_KV_HEADS = N_HEADS // 4
KV_GROUP = N_HEADS // N_KV_HEADS
Q_WIDTH = N_HEADS * HEAD_DIM
KV_WIDTH = N_KV_HEADS * HEAD_DIM
QKV_WIDTH = Q_WIDTH + 2 * KV_WIDTH
Q_BLOCK = 128
ROPE_THETA = 10000.0
ROPE_PAIRS_AXIS = HEAD_DIM // 4
GRID_W = 64
GMLP_FFN = 6 * D_MODEL
GMLP_HALF = GMLP_FFN // 2
GMLP_CHUNK = 128
GMLP_GROUPS = 16
GMLP_GROUP_WIDTH = GMLP_HALF // GMLP_GROUPS
MLP_HIDDEN = 4 * D_MODEL
N_MOD = 6
EPS = 1e-6

kernel_name = "hybrid_gqa_gmlp_adaln_encoder"


def rmsnorm(x, g):
    xf = x.astype(jnp.float32)
    y = xf * lax.rsqrt(jnp.mean(xf * xf, axis=-1, keepdims=True) + EPS)
    return (y * g).astype(x.dtype)


def layernorm(x, g, b):
    xf = x.astype(jnp.float32)
    mu = jnp.mean(xf, axis=-1, keepdims=True)
    xc = xf - mu
    y = xc * lax.rsqrt(jnp.mean(xc * xc, axis=-1, keepdims=True) + EPS)
    return (y * g + b).astype(x.dtype)


def modulate(h, shift, scale):
    return h * (1.0 + scale[:, None, :]) + shift[:, None, :]


def axial_rope_tables(seq_len):
    t = jnp.arange(seq_len, dtype=jnp.int32)
    rows = seq_len // GRID_W
    row = (t // GRID_W - rows // 2).astype(jnp.float32)
    col = (t % GRID_W - GRID_W // 2).astype(jnp.float32)
    inv_freq = ROPE_THETA ** (-jnp.arange(ROPE_PAIRS_AXIS, dtype=jnp.float32) / ROPE_PAIRS_AXIS)
    ang = jnp.concatenate([row[:, None] * inv_freq, col[:, None] * inv_freq], axis=-1)
    return jnp.cos(ang), jnp.sin(ang)


def apply_rope(x, cos, sin):
    xp = x.astype(jnp.float32).reshape(*x.shape[:-1], HEAD_DIM // 2, 2)
    x1, x2 = xp[..., 0], xp[..., 1]
    c, s = cos[:, None, :], sin[:, None, :]
    out = jnp.stack([x1 * c - x2 * s, x1 * s + x2 * c], axis=-1)
    return out.reshape(x.shape).astype(x.dtype)


def gqa_attention(h, w_qkv, q_g, k_g, w_o, cos, sin):
    B, T, _ = h.shape
    qkv = h @ w_qkv
    q, k, v = jnp.split(qkv, [Q_WIDTH, Q_WIDTH + KV_WIDTH], axis=-1)
    q = apply_rope(rmsnorm(q.reshape(B, T, N_HEADS, HEAD_DIM), q_g), cos, sin)
    k = apply_rope(rmsnorm(k.reshape(B, T, N_KV_HEADS, HEAD_DIM), k_g), cos, sin)
    v = v.reshape(B, T, N_KV_HEADS, HEAD_DIM)
    n_blocks = T // Q_BLOCK
    qb = q.reshape(B, n_blocks, Q_BLOCK, N_KV_HEADS, KV_GROUP, HEAD_DIM).transpose(1, 0, 2, 3, 4, 5)
    scale = HEAD_DIM ** -0.5

    def one_block(q_blk):
        s = jnp.einsum('bqkgd,bskd->bkgqs', q_blk, k).astype(jnp.float32) * scale
        p = jax.nn.softmax(s, axis=-1).astype(v.dtype)
        return jnp.einsum('bkgqs,bskd->bqkgd', p, v)

    o = lax.map(one_block, qb)
    o = o.transpose(1, 0, 2, 3, 4, 5).reshape(B, T, Q_WIDTH)
    return o @ w_o


def chunked_gmlp(h, w_in, b_in, ln_g, ln_b, w_s, b_s, w_out):
    B, T, _ = h.shape
    z = jax.nn.gelu(h @ w_in + b_in, approximate=False)
    u, v = jnp.split(z, 2, axis=-1)
    v = layernorm(v, ln_g, ln_b)
    v = v.reshape(B, T // GMLP_CHUNK, GMLP_CHUNK, GMLP_GROUPS, GMLP_GROUP_WIDTH)
    v = jnp.einsum('gpq,bnqgc->bnpgc', w_s, v) + b_s.T[None, None, :, :, None]
    v = v.reshape(B, T, GMLP_HALF)
    return (u * v) @ w_out


def sq_relu_mlp(h, w_in, w_out):
    return jnp.square(jax.nn.relu(h @ w_in)) @ w_out


def setup_inputs(seed: int = 0) -> dict:
    key = jax.random.key(seed)
    ks = jax.random.split(key, 20)

    def nrm(k, shape, scale):
        return jax.random.normal(k, shape, jnp.float32) * scale

    return {
        "x": nrm(ks[0], (BATCH, SEQ, D_MODEL), 1.0),
        "c": nrm(ks[1], (BATCH, D_MODEL), 1.0),
        "ada_w": nrm(ks[2], (DEPTH, D_MODEL, N_MOD * D_MODEL), 0.5 * D_MODEL ** -0.5),
        "ada_b": nrm(ks[3], (DEPTH, N_MOD * D_MODEL), 0.01),
        "norm1_g": 1.0 + nrm(ks[4], (DEPTH, D_MODEL), 0.05),
        "norm2_g": 1.0 + nrm(ks[5], (DEPTH, D_MODEL), 0.05),
        "attn_w_qkv": nrm(ks[6], (N_ATTN_LAYERS, D_MODEL, QKV_WIDTH), D_MODEL ** -0.5),
        "attn_q_norm_g": 1.0 + nrm(ks[7], (N_ATTN_LAYERS, HEAD_DIM), 0.05),
        "attn_k_norm_g": 1.0 + nrm(ks[8], (N_ATTN_LAYERS, HEAD_DIM), 0.05),
        "attn_w_o": nrm(ks[9], (N_ATTN_LAYERS, Q_WIDTH, D_MODEL), Q_WIDTH ** -0.5),
        "gmlp_w_in": nrm(ks[10], (N_GMLP_LAYERS, D_MODEL, GMLP_FFN), D_MODEL ** -0.5),
        "gmlp_b_in": nrm(ks[11], (N_GMLP_LAYERS, GMLP_FFN), 0.01),
        "gmlp_ln_g": 1.0 + nrm(ks[12], (N_GMLP_LAYERS, GMLP_HALF), 0.05),
        "gmlp_ln_b": nrm(ks[13], (N_GMLP_LAYERS, GMLP_HALF), 0.01),
        "gmlp_w_s": nrm(ks[14], (N_GMLP_LAYERS, GMLP_GROUPS, GMLP_CHUNK, GMLP_CHUNK), GMLP_CHUNK ** -0.5),
        "gmlp_b_s": 1.0 + nrm(ks[15], (N_GMLP_LAYERS, GMLP_GROUPS, GMLP_CHUNK), 0.05),
        "gmlp_w_out": nrm(ks[16], (N_GMLP_LAYERS, GMLP_HALF, D_MODEL), GMLP_HALF ** -0.5),
        "mlp_w_in": nrm(ks[17], (DEPTH, D_MODEL, MLP_HIDDEN), D_MODEL ** -0.5),
        "mlp_w_out": nrm(ks[18], (DEPTH, MLP_HIDDEN, D_MODEL), MLP_HIDDEN ** -0.5),
    }


def reference(x, c, ada_w, ada_b, norm1_g, norm2_g, attn_w_qkv, attn_q_norm_g,
              attn_k_norm_g, attn_w_o, gmlp_w_in, gmlp_b_in, gmlp_ln_g, gmlp_ln_b,
              gmlp_w_s, gmlp_b_s, gmlp_w_out, mlp_w_in, mlp_w_out):
    cond = jax.nn.silu(c)
    cos, sin = axial_rope_tables(x.shape[1])
    for i in range(DEPTH):
        mod = cond @ ada_w[i] + ada_b[i]
        sh1, sc1, g1, sh2, sc2, g2 = jnp.split(mod, N_MOD, axis=-1)
        h = modulate(rmsnorm(x, norm1_g[i]), sh1, sc1)
        j = i // N_MIXERS
        if i % N_MIXERS == 0:
            y = gqa_attention(h, attn_w_qkv[j], attn_q_norm_g[j], attn_k_norm_g[j],
                              attn_w_o[j], cos, sin)
        else:
            y = chunked_gmlp(h, gmlp_w_in[j], gmlp_b_in[j], gmlp_ln_g[j], gmlp_ln_b[j],
                             gmlp_w_s[j], gmlp_b_s[j], gmlp_w_out[j])
        x = x + g1[:, None, :] * y
        h = modulate(rmsnorm(x, norm2_g[i]), sh2, sc2)
        x = x + g2[:, None, :] * sq_relu_mlp(h, mlp_w_in[i], mlp_w_out[i])
    return x
```

```python
import contextlib
import numpy as np
import concourse.bass as bass
import concourse.mybir as mybir
from concourse.bass_utils import run_bass_kernel_spmd

F32 = mybir.dt.float32
BF16 = mybir.dt.bfloat16
AF = mybir.ActivationFunctionType
ALU = mybir.AluOpType

D = 1024
SEQ = 2048
DEPTH = 4
TT = 512
NT = SEQ // TT
EPS = 1e-6
ENG = ("pe", "act", "dve", "pool", "sp")
import os
STG = float(os.environ.get("KSTAGE", "99"))


class T:
    __slots__ = ("name", "w", "r")

    def __init__(self, name):
        self.name = name
        self.w = None
        self.r = []


class Prog:
    def __init__(self, nc):
        self.nc = nc
        self.ops = {e: [] for e in ENG}
        self.dma_cnt = {}

    def _add(self, deps, d, eng):
        if d is None:
            return
        if d[0] == "e" and d[1] == eng and eng in ("pe", "sp"):
            return
        deps.add(d)

    def op(self, eng, fn, reads=(), writes=(), dma=None):
        deps = set()
        for t in reads:
            self._add(deps, t.w, eng)
        for t in writes:
            self._add(deps, t.w, eng)
            for d in t.r:
                self._add(deps, d, eng)
        idx = len(self.ops[eng])
        if dma is not None:
            c = self.dma_cnt.get(dma, 0) + 1
            self.dma_cnt[dma] = c
            me = ("d", dma, 16 * c)
        else:
            me = ("e", eng, idx)
        self.ops[eng].append([fn, deps, False, dma])
        for t in reads:
            t.r.append(me)
        for t in writes:
            t.w = me
            t.r = []
        return me

    def emit(self, final_waits=()):
        nc = self.nc
        for e in ENG:
            for o in self.ops[e]:
                for d in o[1]:
                    if d[0] == "e":
                        self.ops[d[1]][d[2]][2] = True
        val = {}
        for e in ENG:
            c = 0
            for i, o in enumerate(self.ops[e]):
                if o[3] is None and o[2]:
                    c += 1
                    val[(e, i)] = c
        with contextlib.ExitStack() as st:
            esem = {e: st.enter_context(nc.semaphore("s_" + e)) for e in ENG}
            dsem = {k: st.enter_context(nc.semaphore("d_" + str(k))) for k in self.dma_cnt}
            blk = st.enter_context(nc.Block())
            engobj = {"pe": blk.tensor, "act": blk.scalar, "dve": blk.vector,
                      "pool": blk.gpsimd, "sp": blk.sync}

            def run(e, eo):
                known = {}
                for o in self.ops[e]:
                    fn, deps, sig, dma = o
                    need = {}
                    for d in deps:
                        if d[0] == "e":
                            s, v = esem[d[1]], val[(d[1], d[2])]
                        else:
                            s, v = dsem[d[1]], d[2]
                        k = id(s)
                        if need.get(k, (None, 0))[1] < v:
                            need[k] = (s, v)
                    for k, (s, v) in need.items():
                        if known.get(k, 0) < v:
                            eo.wait_ge(s, v)
                            known[k] = v
                    ins = fn(eo)
                    if dma is not None:
                        ins.then_inc(dsem[dma], 16)
                    elif sig:
                        ins.then_inc(esem[e], 1)
                if e == "sp":
                    for k in final_waits:
                        eo.wait_ge(dsem[k], 16 * self.dma_cnt[k])

            for e in ENG:
                def mk(e):
                    def f(eo):
                        run(e, eo)
                    return f
                engobj[e](mk(e))


DBG = {}


def I(meth, *a, **k):
    return lambda e: getattr(e, meth)(*a, **k)


def _blob_layout():
    off = {}
    o = 0

    def add(name, n):
        nonlocal o
        off[name] = (o, n)
        o += n
    for l in range(DEPTH):
        add(f"ada{l}", 8 * 6144)
        if l % 2 == 0:
            add(f"wkv{l}", 8 * 768)
            add(f"wq{l}", 8 * 2048)
            add(f"wo{l}", 8 * 1024)
        else:
            add(f"gin{l}", 8 * 6144)
            add(f"gout{l}", 24 * 1024)
            add(f"wst{l}", 32 * 128)
        add(f"w1{l}", 8 * 4096)
        add(f"w2{l}", 32 * 1024)
    return off, o


BLOB_OFF, BLOB_N = _blob_layout()
PCH = 4096
assert BLOB_N % PCH == 0

Q_HEADS = [(8 * (c // 4) + c % 4, 8 * (c // 4) + 4 + c % 4) for c in range(8)]

SLOTS = []
for j in range(24):
    if j % 3 != 1:
        SLOTS.append((j, 2 * (j // 3) + (0 if j % 3 == 0 else 1), 0, 128))
for j in range(24):
    if j % 3 == 1:
        SLOTS.append((j, 2 * (j // 3), 0, 64))
        SLOTS.append((j, 2 * (j // 3) + 1, 64, 128))
assert len(SLOTS) == 32


def pk(W):
    K, N = W.shape
    return np.ascontiguousarray(W.reshape(K // 128, 128, N).transpose(1, 0, 2)).reshape(128, -1)


def host_prep(inp):
    blob = np.empty((128, BLOB_N), np.float32)

    def put(name, arr):
        o, n = BLOB_OFF[name]
        assert arr.shape == (128, n), (name, arr.shape, n)
        blob[:, o:o + n] = arr
    sw = np.arange(64) ^ 1
    for l in range(DEPTH):
        put(f"ada{l}", pk(inp["ada_w"][l]))
        j = l // 2
        if l % 2 == 0:
            W = inp["attn_w_qkv"][j]
            Wq, Wk, Wv = W[:, :1024], W[:, 1024:1280], W[:, 1280:1536]
            kcols = np.concatenate([Wk, Wk.reshape(1024, 4, 64)[:, :, sw].reshape(1024, 256), Wv], axis=1)
            put(f"wkv{l}", pk(kcols))
            Wq3 = Wq.reshape(1024, 16, 64)
            cols = []
            for c in range(8):
                ha, hb = Q_HEADS[c]
                cols += [Wq3[:, ha, :], Wq3[:, hb, :], Wq3[:, ha, :][:, sw], Wq3[:, hb, :][:, sw]]
            put(f"wq{l}", pk(np.concatenate(cols, axis=1)))
            Wo3 = inp["attn_w_o"][j].reshape(16, 64, 1024)
            rows = []
            for c in range(8):
                ha, hb = Q_HEADS[c]
                rows += [Wo3[ha], Wo3[hb]]
            put(f"wo{l}", pk(np.concatenate(rows, axis=0)))
        else:
            put(f"gin{l}", pk(inp["gmlp_w_in"][j]))
            put(f"gout{l}", pk(inp["gmlp_w_out"][j]))
            ws = inp["gmlp_w_s"][j]
            wst = np.stack([ws[g].T for (_, g, _, _) in SLOTS], axis=1)
            put(f"wst{l}", np.ascontiguousarray(wst).reshape(128, 32 * 128))
        put(f"w1{l}", pk(inp["mlp_w_in"][l]))
        put(f"w2{l}", pk(inp["mlp_w_out"][l]))

    def colv(v, nch):
        return np.ascontiguousarray(v.reshape(nch, 128).T)
    sm = {}
    sm["ada_b"] = np.ascontiguousarray(np.stack([colv(inp["ada_b"][l], 48) for l in range(DEPTH)], axis=1))
    sm["n1g"] = np.ascontiguousarray(np.stack([colv(inp["norm1_g"][l], 8) for l in range(DEPTH)], axis=1))
    sm["n2g"] = np.ascontiguousarray(np.stack([colv(inp["norm2_g"][l], 8) for l in range(DEPTH)], axis=1))
    pidx = np.arange(128) % 64
    qk = np.zeros((128, 2, 4), np.float32)
    for j in range(2):
        qk[:, j, 0] = inp["attn_q_norm_g"][j][pidx]
        qk[:, j, 1] = inp["attn_q_norm_g"][j][pidx ^ 1]
        qk[:, j, 2] = inp["attn_k_norm_g"][j][pidx]
        qk[:, j, 3] = inp["attn_k_norm_g"][j][pidx ^ 1]
    sm["qkg"] = qk
    sm["binu"] = np.ascontiguousarray(np.stack([colv(inp["gmlp_b_in"][j][:3072], 24) for j in range(2)], axis=1))
    sm["binv"] = np.ascontiguousarray(inp["gmlp_b_in"][:, 3072:]).reshape(2, 1, 3072)
    sm["lng"] = np.ascontiguousarray(inp["gmlp_ln_g"]).reshape(2, 1, 3072)
    sm["lnb"] = np.ascontiguousarray(inp["gmlp_ln_b"]).reshape(2, 1, 3072)
    sm["bsl"] = np.ascontiguousarray(np.stack([np.stack([inp["gmlp_b_s"][j][g] for (_, g, _, _) in SLOTS], 0)
                                                for j in range(2)], 0)).reshape(2, 1, 4096)
    t = np.arange(SEQ)
    row = (t // 64 - (SEQ // 64) // 2).astype(np.float32)
    col = (t % 64 - 32).astype(np.float32)
    inv = (10000.0 ** (-np.arange(16, dtype=np.float32) / 16)).astype(np.float32)
    ang = np.concatenate([row[:, None] * inv, col[:, None] * inv], axis=-1)
    cs, sn = np.cos(ang).astype(np.float32), np.sin(ang).astype(np.float32)
    pr = pidx // 2
    sign = np.where(pidx % 2 == 0, -1.0, 1.0).astype(np.float32)
    sm["ropeC"] = np.ascontiguousarray(cs[:, pr].T)
    sm["ropeS"] = np.ascontiguousarray((sn[:, pr] * sign[None, :]).T)
    return blob, sm


def build(nlayers=DEPTH, nseq=2):
    nc = bass.Bass("TRN2", target_bir_lowering=False)

    def din(name, shape):
        return nc.dram_tensor(name, list(shape), F32, kind="ExternalInput")
    blob = din("blob", [128, BLOB_N])
    xin = din("xT", [2, 128, 8, SEQ])
    cin = din("cT", [128, 8, 2])
    d_adab = din("ada_b", [128, 4, 48])
    d_n1g = din("n1g", [128, 4, 8])
    d_n2g = din("n2g", [128, 4, 8])
    d_qkg = din("qkg", [128, 2, 4])
    d_binu = din("binu", [128, 2, 24])
    d_binv = din("binv", [2, 1, 3072])
    d_lng = din("lng", [2, 1, 3072])
    d_lnb = din("lnb", [2, 1, 3072])
    d_bsl = din("bsl", [2, 1, 4096])
    d_ropeC = din("ropeC", [128, SEQ])
    d_ropeS = din("ropeS", [128, SEQ])
    yout = nc.dram_tensor("yT", [2, 128, 8, SEQ], F32, kind="ExternalOutput")
    wb16 = nc.dram_tensor("wb16", [128, BLOB_N], BF16, kind="Internal")
    KDBG = os.environ.get("KDBG", "") != ""
    DBG.clear()
    if KDBG:
        dbgF = nc.dram_tensor("dbgF", [24, 128, 512], F32, kind="ExternalOutput")
        dbgB = nc.dram_tensor("dbgB", [64, 128, 512], BF16, kind="ExternalOutput")

    P = Prog(nc)
    with contextlib.ExitStack() as st:
        def sb(name, shape, dt):
            return st.enter_context(nc.sbuf_tensor("s_" + name, list(shape), dt))
        x = sb("x", [128, 8, SEQ], F32)
        ring = [sb(f"ring{i}", [128, 8192], BF16) for i in range(2)]
        ringF = [r.bitcast(F32) for r in ring]
        h = sb("h", [128, 8, TT], BF16)
        big = sb("big", [128, 48 * 512], BF16)
        bigF = big.bitcast(F32)
        aux = sb("aux", [128, 25 * 512], BF16)
        auxF = aux.bitcast(F32)
        wst = sb("wst", [128, 32, 128], BF16)
        mod = sb("mod", [128, 4, 48, 2], F32)
        adab = sb("adab", [128, 4, 48], F32)
        n1g = sb("n1g", [128, 4, 8], F32)
        n2g = sb("n2g", [128, 4, 8], F32)
        qkg = sb("qkg", [128, 2, 4], F32)
        binu = sb("binu", [128, 2, 24], F32)
        Amod = sb("Amod", [128, 4, 2, 2, 8], F32)
        cT = sb("cT", [128, 8, 2], F32)
        condb = sb("condb", [128, 8, 2], BF16)
        ones_b = sb("ones_b", [128, 128], BF16)
        ones_f = sb("ones_f", [128, 128], F32)
        avg_d = sb("avg_d", [128, 128], BF16)
        avg_h = sb("avg_h", [128, 128], BF16)
        epst = sb("epst", [128, 1], F32)
        stat = sb("stat", [128, 64], F32)
        rowb = sb("rowb", [1, 8192], BF16)
        ps = [st.enter_context(nc.psum_tensor(f"ps{i}", [128, 512], F32)) for i in range(8)]

        t_x = [[T(f"x{c}_{t}") for t in range(NT)] for c in range(8)]
        t_ring = [T("ring0"), T("ring1")]
        t_h = T("h")
        t_big = [T(f"big{i}") for i in range(48)]
        t_aux = [T(f"aux{i}") for i in range(25)]
        t_wst = T("wst")
        t_mod, t_small, t_amod, t_cond, t_const = T("mod"), T("small"), T("amod"), T("cond"), T("const")
        t_stat, t_rowb = T("stat"), T("rowb")
        t_ps = [T(f"ps{i}") for i in range(8)]
        t_blob = T("blob")
        psi = [0]

        def nps():
            i = psi[0] % 6
            psi[0] += 1
            return i

        pacc = [0]

        def nps_acc():
            i = 6 + pacc[0] % 2
            pacc[0] += 1
            return i

        def dump(name, ap, tiles, f32, p0=0):
            if not KDBG or name in DBG:
                return
            kind = "F" if f32 else "B"
            idx = sum(1 for v in DBG.values() if v[0] == kind)
            np_, ncol = ap.shape[0], ap.shape[1]
            DBG[name] = (kind, idx, p0, np_, ncol)
            dst = (dbgF if f32 else dbgB).ap()[idx, p0:p0 + np_, 0:ncol]
            P.op("sp", I("dma_start", out=dst, in_=ap), reads=tiles, dma="dbg_" + name)

        def slot(i, n=1):
            return big[:, i * 512:(i + n) * 512]

        def slotF(i, n=2):
            return bigF[:, i * 256:(i + n) * 256]

        def axs(i, n=1):
            return aux[:, i * 512:(i + n) * 512]

        def axF(i, n=2):
            return auxF[:, i * 256:(i + n) * 256]

        P.op("pool", I("memset", ones_b[:], 1.0), writes=[t_const])
        P.op("pool", I("memset", ones_f[:], 1.0), writes=[t_const])
        P.op("pool", I("memset", avg_d[:], 1.0 / D), writes=[t_const])
        P.op("pool", I("memset", avg_h[:], 0.0), writes=[t_const])
        P.op("pool", I("memset", avg_h[0:64, 0:64], 1.0 / 64), writes=[t_const])
        P.op("pool", I("memset", avg_h[64:128, 64:128], 1.0 / 64), writes=[t_const])
        P.op("pool", I("memset", epst[:], EPS), writes=[t_const])
        for (dst, src) in ((adab, d_adab), (n1g, d_n1g), (n2g, d_n2g), (qkg, d_qkg), (binu, d_binu), (cT, cin)):
            P.op("sp", I("dma_start", out=dst[:], in_=src.ap()), writes=[t_small], dma="small")

        nch = BLOB_N // PCH
        last_store = None
        for i in range(nch):
            b = i % 2
            sgi = i % 4
            stg = big[:, sgi * 4096:(sgi + 1) * 4096]
            tl = t_big[sgi * 8:(sgi + 1) * 8]
            P.op("sp", I("dma_start", out=ringF[b][:], in_=blob.ap()[:, i * PCH:(i + 1) * PCH]),
                 writes=[t_ring[b]], dma=f"ring{b}")
            ce = "dve" if i % 2 == 0 else "pool"
            P.op(ce, I("tensor_copy", out=stg, in_=ringF[b][:]), reads=[t_ring[b]], writes=tl)
            last_store = P.op("act", I("dma_start", out=wb16.ap()[:, i * PCH:(i + 1) * PCH], in_=stg),
                              reads=tl, dma="st")
        t_blob.w = last_store

        def wview(name, kc):
            o, n = BLOB_OFF[name]
            return wb16.ap()[:, o:o + n].rearrange("p (k n) -> p k n", k=kc)

        rr = [0]

        def wload(src, kc, ncols):
            b = rr[0] % 2
            rr[0] += 1
            dst = ring[b][:, 0:kc * ncols].rearrange("p (k n) -> p k n", k=kc)
            P.op("sp", I("dma_start", out=dst, in_=src), reads=[t_blob], writes=[t_ring[b]], dma=f"ring{b}")
            return dst, t_ring[b]

        P.op("act", I("activation", out=condb[:], in_=cT[:], func=AF.Silu), reads=[t_small], writes=[t_cond])
        for l in range(nlayers):
            pb = nps()
            for pc in range(6):
                wv, wt = wload(wview(f"ada{l}", 8)[:, :, pc * 1024:(pc + 1) * 1024], 8, 1024)
                for jj in range(8):
                    j = pc * 8 + jj
                    for k in range(8):
                        P.op("pe", I("matmul",
                            ps[pb][:, 2 * j:2 * j + 2], lhsT=wv[:, k, jj * 128:(jj + 1) * 128], rhs=condb[:, k, :],
                            start=(k == 0), stop=(k == 7)), reads=[wt, t_cond], writes=[t_ps[pb]])
            P.op("dve", I("tensor_tensor",
                out=mod[:, l, :, :], in0=ps[pb][:, 0:96].rearrange("p (j b) -> p j b", b=2),
                in1=adab[:, l, :].unsqueeze(2).to_broadcast([128, 48, 2]), op=ALU.add),
                reads=[t_ps[pb], t_small], writes=[t_mod])
            for wh, (gt, base) in enumerate(((n1g, 8), (n2g, 32))):
                for b in range(2):
                    P.op("dve", I("scalar_tensor_tensor",
                        out=Amod[:, l, wh, b, :], in0=mod[:, l, base:base + 8, b], scalar=1.0, in1=gt[:, l, :],
                        op0=ALU.add, op1=ALU.mult), reads=[t_mod, t_small], writes=[t_amod])
            dump("mod", mod[:, l, :, :].rearrange("p j b -> p (j b)"), [t_mod], True)

        def norm_tile(l, wh, b, t):
            tsl = slice(t * TT, (t + 1) * TT)
            sh_base = 0 if wh == 0 else 24
            pb = nps()
            for c in range(8):
                P.op("act", I("activation", out=slot(40 + c), in_=x[:, c, tsl], func=AF.Square),
                     reads=[t_x[c][t]], writes=[t_big[40 + c]])
                P.op("pe", I("matmul", ps[pb][:], lhsT=avg_d[:], rhs=slot(40 + c), start=(c == 0), stop=(c == 7)),
                     reads=[t_big[40 + c], t_const], writes=[t_ps[pb]])
            P.op("act", I("activation", out=slotF(36), in_=ps[pb][:], func=AF.Ln, bias=epst[:], scale=1.0),
                 reads=[t_ps[pb], t_const], writes=t_big[36:38])
            P.op("act", I("activation", out=slotF(38), in_=slotF(36), func=AF.Exp, scale=-0.5),
                 reads=t_big[36:38], writes=t_big[38:40])
            for c in range(8):
                tmp = 32 + 2 * (c % 2)
                P.op("dve", I("scalar_tensor_tensor",
                    out=slotF(tmp), in0=x[:, c, tsl], scalar=Amod[:, l, wh, b, c:c + 1], in1=slotF(38),
                    op0=ALU.mult, op1=ALU.mult), reads=[t_x[c][t], t_amod] + t_big[38:40], writes=t_big[tmp:tmp + 2])
                P.op("act", I("activation",
                    out=h[:, c, :], in_=slotF(tmp), func=AF.Identity, bias=mod[:, l, sh_base + c, b:b + 1], scale=1.0),
                    reads=t_big[tmp:tmp + 2] + [t_mod], writes=[t_h])

        def resid_add(l, gbase, b, t, m, pb):
            tsl = slice(t * TT, (t + 1) * TT)
            P.op("dve", I("scalar_tensor_tensor",
                out=x[:, m, tsl], in0=ps[pb][:], scalar=mod[:, l, gbase + m, b:b + 1], in1=x[:, m, tsl],
                op0=ALU.mult, op1=ALU.add), reads=[t_ps[pb], t_mod, t_x[m][t]], writes=[t_x[m][t]])

        def ffn_tile(l, b, t):
            norm_tile(l, 1, b, t)
            for pc in range(4):
                wv, wt = wload(wview(f"w1{l}", 8)[:, :, pc * 1024:(pc + 1) * 1024], 8, 1024)
                for jj in range(8):
                    j = pc * 8 + jj
                    pb = nps()
                    for k in range(8):
                        P.op("pe", I("matmul",
                            ps[pb][:], lhsT=wv[:, k, jj * 128:(jj + 1) * 128], rhs=h[:, k, :], start=(k == 0), stop=(k == 7)),
                            reads=[wt, t_h], writes=[t_ps[pb]])
                    tmp = 32 + 2 * (j % 4)
                    P.op("act", I("activation", out=slotF(tmp), in_=ps[pb][:], func=AF.Relu),
                         reads=[t_ps[pb]], writes=t_big[tmp:tmp + 2])
                    P.op("dve", I("tensor_tensor", out=slot(j), in0=slotF(tmp), in1=slotF(tmp), op=ALU.mult),
                         reads=t_big[tmp:tmp + 2], writes=[t_big[j]])
            for pc in range(4):
                wv, wt = wload(wview(f"w2{l}", 32)[:, :, pc * 256:(pc + 1) * 256], 32, 256)
                for mm in range(2):
                    m = pc * 2 + mm
                    pb = nps()
                    for j in range(32):
                        P.op("pe", I("matmul",
                            ps[pb][:], lhsT=wv[:, j, mm * 128:(mm + 1) * 128], rhs=slot(j), start=(j == 0), stop=(j == 31)),
                            reads=[wt, t_big[j]], writes=[t_ps[pb]])
                    resid_add(l, 40, b, t, m, pb)

        def rope_norm(pq, pqs, gi, j, t, dst, dst_tiles, tb):
            tsl = slice(t * TT, (t + 1) * TT)
            P.op("act", I("activation", out=slot(tb), in_=ps[pq][:], func=AF.Square), reads=[t_ps[pq]], writes=[t_big[tb]])
            pm = nps()
            P.op("pe", I("matmul", ps[pm][:], lhsT=avg_h[:], rhs=slot(tb), start=True, stop=True),
                 reads=[t_big[tb], t_const], writes=[t_ps[pm]])
            P.op("act", I("activation", out=slotF(tb + 2), in_=ps[pm][:], func=AF.Ln, bias=epst[:], scale=1.0),
                 reads=[t_ps[pm], t_const], writes=t_big[tb + 2:tb + 4])
            P.op("act", I("activation", out=slotF(tb + 4), in_=slotF(tb + 2), func=AF.Exp, scale=-0.5),
                 reads=t_big[tb + 2:tb + 4], writes=t_big[tb + 4:tb + 6])
            P.op("dve", I("scalar_tensor_tensor", out=slotF(tb + 6), in0=ps[pq][:], scalar=qkg[:, j, gi:gi + 1],
                                                         in1=axF(21), op0=ALU.mult, op1=ALU.mult),
                 reads=[t_ps[pq], t_small, t_big[tb]] + t_aux[21:23], writes=t_big[tb + 6:tb + 8])
            P.op("dve", I("scalar_tensor_tensor", out=slotF(tb + 8), in0=ps[pqs][:], scalar=qkg[:, j, gi + 1:gi + 2],
                                                         in1=axF(23), op0=ALU.mult, op1=ALU.mult),
                 reads=[t_ps[pqs], t_small] + t_aux[23:25], writes=t_big[tb + 8:tb + 10])
            P.op("dve", I("tensor_tensor", out=slotF(tb + 6), in0=slotF(tb + 6), in1=slotF(tb + 8), op=ALU.add),
                 reads=t_big[tb + 6:tb + 10], writes=t_big[tb + 6:tb + 8])
            P.op("dve", I("tensor_tensor", out=dst, in0=slotF(tb + 6), in1=slotF(tb + 4), op=ALU.mult),
                 reads=t_big[tb + 4:tb + 8], writes=dst_tiles)

        def load_rope(t):
            tsl = slice(t * TT, (t + 1) * TT)
            P.op("sp", I("dma_start", out=axF(21), in_=d_ropeC.ap()[:, tsl]), writes=t_aux[21:23], dma="ropeC")
            P.op("sp", I("dma_start", out=axF(23), in_=d_ropeS.ap()[:, tsl]), writes=t_aux[23:25], dma="ropeS")

        kT = aux[:, 0:4096].rearrange("p (m t) -> p m t", m=2)
        Ve = aux[:, 4096:4096 + 2080].rearrange("p (h t c) -> p h t c", h=2, t=16)
        Vo = aux[:, 4096 + 2080:4096 + 2080 + 4096].rearrange("p (h t c) -> p h t c", h=2, t=16)
        t_kT = t_aux[0:8]
        t_V = t_aux[8:21]

        def attn_layer(l, b):
            j = l // 2
            P.op("pool", I("memset", Vo, 0.0), writes=t_V)
            P.op("pool", I("memset", Vo[:, :, :, 0:1], 1.0), writes=t_V)
            P.op("pool", I("memset", Ve[:, :, :, 64:65], 1.0), writes=t_V)
            for t in range(NT):
                if STG < 2.2:
                    continue
                norm_tile(l, 0, b, t)
                if STG < 2.3:
                    continue
                load_rope(t)
                wv, wt = wload(wview(f"wkv{l}", 8), 8, 768)
                for m in range(2):
                    if STG < 2.26:
                        continue
                    pq, pqs = nps(), nps()
                    for (pp, cb) in ((pq, m * 128), (pqs, 256 + m * 128)):
                        for k in range(8):
                            P.op("pe", I("matmul",
                                ps[pp][:], lhsT=wv[:, k, cb:cb + 128], rhs=h[:, k, :], start=(k == 0), stop=(k == 7)),
                                reads=[wt, t_h], writes=[t_ps[pp]])
                    if STG >= 2.28:
                        rope_norm(pq, pqs, 2, j, t, kT[:, m, t * TT:(t + 1) * TT], t_kT, 0)
                for tc in range(4):
                    if STG < 2.4:
                        continue
                    pv = nps()
                    g = t * 4 + tc
                    for k in range(8):
                        P.op("pe", I("matmul",
                            ps[pv][:, 0:256], lhsT=h[:, k, tc * 128:(tc + 1) * 128], rhs=wv[:, k, 512:768],
                            start=(k == 0), stop=(k == 7)), reads=[wt, t_h], writes=[t_ps[pv]])
                    P.op("act", I("activation",
                        out=Ve[:, :, g, 0:64], in_=ps[pv][:, 0:256].rearrange("p (h two c) -> p h two c", h=2, two=2)[:, :, 0, :],
                        func=AF.Copy), reads=[t_ps[pv]], writes=t_V)
                    P.op("dve", I("tensor_copy",
                        out=Vo[:, :, g, 64:128], in_=ps[pv][:, 0:256].rearrange("p (h two c) -> p h two c", h=2, two=2)[:, :, 1, :]),
                        reads=[t_ps[pv]], writes=t_V)
            if STG < 3:
                return
            for t in range(NT):
                norm_tile(l, 0, b, t)
                dump("h0", h[:, 0, :], [t_h], False)
                dump("kT0", kT[:, 0, 0:512], t_kT, False)
                dump("Ve0", Ve[:, 0, 0, :], t_V, False)
                dump("Vo0", Vo[:, 0, 0, :], t_V, False)
                load_rope(t)
                for half in range(2):
                    wv, wt = wload(wview(f"wq{l}", 8)[:, :, half * 1024:(half + 1) * 1024], 8, 1024)
                    for cc in range(4):
                        c = half * 4 + cc
                        pq, pqs = nps(), nps()
                        for (pp, cb) in ((pq, cc * 256), (pqs, cc * 256 + 128)):
                            for k in range(8):
                                P.op("pe", I("matmul",
                                    ps[pp][:], lhsT=wv[:, k, cb:cb + 128], rhs=h[:, k, :], start=(k == 0), stop=(k == 7)),
                                    reads=[wt, t_h], writes=[t_ps[pp]])
                        rope_norm(pq, pqs, 0, j, t, slot(c), [t_big[c]], 20)
                        dump("qT0", slot(0), [t_big[0]], False)
                if STG < 4:
                    continue
                for c in range(8):
                    m = c // 4
                    for hf in range(2):
                        p0 = hf * 64
                        po = nps_acc()
                        Vl = (lambda kc: Ve[:, m, kc, :]) if hf == 0 else (lambda kc: Vo[:, m, kc, :])
                        M = 65 if hf == 0 else 128
                        sps = {}

                        def emit_s(kc):
                            sp_ = nps()
                            sps[kc] = sp_
                            P.op("pe", I("matmul",
                                ps[sp_][:], lhsT=kT[p0:p0 + 64, m, kc * 128:(kc + 1) * 128], rhs=slot(c)[p0:p0 + 64, :],
                                start=True, stop=True), reads=t_kT + [t_big[c]], writes=[t_ps[sp_]])

                        def emit_pv(kc):
                            sp_ = sps[kc]
                            pslot = 16 + kc % 4
                            P.op("act", I("activation",
                                out=slot(pslot), in_=ps[sp_][:], func=AF.Exp, scale=0.125),
                                reads=[t_ps[sp_]], writes=[t_big[pslot]])
                            dump("P0", slot(16), [t_big[16]], False)
                            P.op("pe", I("matmul",
                                ps[po][0:M, :], lhsT=Vl(kc), rhs=slot(pslot), start=(kc == 0), stop=(kc == 15)),
                                reads=t_V + [t_big[pslot]], writes=[t_ps[po]])
                        emit_s(0)
                        emit_s(1)
                        for kc in range(16):
                            if kc + 2 < 16:
                                emit_s(kc + 2)
                            emit_pv(kc)
                        dp = 64 if hf == 0 else 0
                        rd = slotF(20)[dp:dp + 1, :]
                        P.op("dve", I("reciprocal", out=rd, in_=ps[po][dp:dp + 1, :]),
                             reads=[t_ps[po]], writes=t_big[20:22])
                        dump("rd", rd, t_big[20:22], True, p0=dp)
                        pbc = nps()
                        P.op("pe", I("matmul",
                            ps[pbc][:], lhsT=ones_f[dp:dp + 1, :], rhs=rd, start=True, stop=True),
                            reads=t_big[20:22] + [t_const], writes=[t_ps[pbc]])
                        P.op("act", I("activation", out=slotF(22)[p0:p0 + 64, :], in_=ps[po][p0:p0 + 64, :], func=AF.Copy),
                             reads=[t_ps[po]], writes=t_big[22:24])
                        P.op("dve", I("tensor_tensor",
                            out=slot(8 + c)[p0:p0 + 64, :], in0=slotF(22)[p0:p0 + 64, :], in1=ps[pbc][p0:p0 + 64, :], op=ALU.mult),
                            reads=t_big[22:24] + [t_ps[pbc]], writes=[t_big[8 + c]])
                        if hf == 1:
                            dump("OT0", slot(8), [t_big[8]], False)
                wv, wt = wload(wview(f"wo{l}", 8), 8, 1024)
                for mo in range(8):
                    pb = nps()
                    for c in range(8):
                        P.op("pe", I("matmul",
                            ps[pb][:], lhsT=wv[:, c, mo * 128:(mo + 1) * 128], rhs=slot(8 + c), start=(c == 0), stop=(c == 7)),
                            reads=[wt, t_big[8 + c]], writes=[t_ps[pb]])
                    resid_add(l, 16, b, t, mo, pb)
                    dump("x0", x[:, 0, 0:512], [t_x[0][0]], True)
                if STG >= 5:
                    ffn_tile(l, b, t)

        def gmlp_setup(l):
            j = l // 2
            lngF = auxF[:, 0:3072]
            P.op("sp", I("dma_start", out=lngF, in_=d_lng.ap()[j].to_broadcast([128, 3072])), writes=t_aux[0:12], dma="gs0")
            lnbF = bigF[:, 0:3072]
            P.op("sp", I("dma_start", out=lnbF, in_=d_lnb.ap()[j].to_broadcast([128, 3072])), writes=t_big[0:12], dma="gs1")
            P.op("dve", I("tensor_copy", out=aux[:, 12 * 512:18 * 512], in_=lnbF), reads=t_big[0:12], writes=t_aux[12:18])
            rtmp = bigF[0:1, 3072:3072 + 7168]
            P.op("sp", I("dma_start", out=rtmp[:, 0:3072], in_=d_binv.ap()[j]), writes=t_big[12:40], dma="gs2")
            P.op("sp", I("dma_start", out=rtmp[:, 3072:7168], in_=d_bsl.ap()[j]), writes=t_big[12:40], dma="gs3")
            P.op("dve", I("tensor_copy", out=rowb[:, 0:7168], in_=rtmp), reads=t_big[12:40], writes=[t_rowb])
            o, n = BLOB_OFF[f"wst{l}"]
            P.op("sp", I("dma_start", out=wst[:], in_=wb16.ap()[:, o:o + n].rearrange("p (s q) -> p s q", s=32)),
                 reads=[t_blob], writes=[t_wst], dma="gs4")

        def gmlp_tile(l, b, t):
            j = l // 2
            norm_tile(l, 0, b, t)
            for pc in range(3):
                wv, wt = wload(wview(f"gin{l}", 8)[:, :, pc * 1024:(pc + 1) * 1024], 8, 1024)
                for jj in range(8):
                    ju = pc * 8 + jj
                    pb = nps()
                    for k in range(8):
                        P.op("pe", I("matmul",
                            ps[pb][:], lhsT=wv[:, k, jj * 128:(jj + 1) * 128], rhs=h[:, k, :], start=(k == 0), stop=(k == 7)),
                            reads=[wt, t_h], writes=[t_ps[pb]])
                    P.op("act", I("activation",
                        out=slot(ju), in_=ps[pb][:], func=AF.Gelu, bias=binu[:, j, ju:ju + 1], scale=1.0),
                        reads=[t_ps[pb], t_small], writes=[t_big[ju]])
                    dump("gu0", slot(0), [t_big[0]], False)
            for pc in range(3):
                wv, wt = wload(wview(f"gin{l}", 8)[:, :, 3072 + pc * 1024:3072 + (pc + 1) * 1024], 8, 1024)
                for tc in range(4):
                    for nt in range(2):
                        pb = nps()
                        col0 = pc * 1024 + nt * 512
                        P.op("pe", I("matmul",
                            ps[pb][:], lhsT=ones_b[0:1, :], rhs=rowb[0:1, col0:col0 + 512], start=True, stop=False),
                            reads=[t_const, t_rowb], writes=[t_ps[pb]])
                        for k in range(8):
                            P.op("pe", I("matmul",
                                ps[pb][:], lhsT=h[:, k, tc * 128:(tc + 1) * 128], rhs=wv[:, k, nt * 512:(nt + 1) * 512],
                                start=False, stop=(k == 7)), reads=[wt, t_h], writes=[t_ps[pb]])
                        sl = 24 + 6 * tc + pc * 2 + nt
                        P.op("act", I("activation", out=slot(sl), in_=ps[pb][:], func=AF.Gelu),
                             reads=[t_ps[pb]], writes=[t_big[sl]])
            for tc in range(4):
                vb = 24 + 6 * tc
                vt = t_big[vb:vb + 6]
                so = tc * 16
                for q in range(6):
                    P.op("dve", I("bn_stats", out=stat[:, so * 0 + q * 6:q * 6 + 6], in_=slot(vb + q)),
                         reads=[t_big[vb + q]], writes=[t_stat])
                P.op("dve", I("bn_aggr", out=stat[:, 40:42], in_=stat[:, 0:36]), reads=[t_stat], writes=[t_stat])
                P.op("act", I("activation", out=stat[:, 42:43], in_=stat[:, 41:42], func=AF.Ln, bias=epst[:], scale=1.0),
                     reads=[t_stat, t_const], writes=[t_stat])
                P.op("act", I("activation", out=stat[:, 43:44], in_=stat[:, 42:43], func=AF.Exp, scale=-0.5),
                     reads=[t_stat], writes=[t_stat])
                P.op("dve", I("scalar_tensor_tensor", out=stat[:, 44:45], in0=stat[:, 40:41], scalar=-1.0, in1=stat[:, 43:44],
                                                             op0=ALU.mult, op1=ALU.mult), reads=[t_stat], writes=[t_stat])
                P.op("dve", I("tensor_copy", out=stat[:, 48:50], in_=stat[:, 43:45]), reads=[t_stat], writes=[t_stat])
                for q in range(6):
                    ta = 18 + 2 * (q % 3)
                    P.op("act", I("activation",
                        out=axF(ta), in_=slot(vb + q), func=AF.Identity, bias=stat[:, 49:50], scale=stat[:, 48:49]),
                        reads=[t_big[vb + q], t_stat], writes=t_aux[ta:ta + 2])
                    P.op("dve", I("tensor_tensor",
                        out=axF(ta), in0=axF(ta), in1=auxF[:, q * 512:(q + 1) * 512], op=ALU.mult),
                        reads=t_aux[ta:ta + 2] + t_aux[2 * q:2 * q + 2], writes=t_aux[ta:ta + 2])
                    P.op("dve", I("tensor_tensor",
                        out=slot(vb + q), in0=axF(ta), in1=aux[:, (12 + q) * 512:(13 + q) * 512], op=ALU.add),
                        reads=t_aux[ta:ta + 2] + [t_aux[12 + q]], writes=[t_big[vb + q]])
                    dump("gv0", slot(24), [t_big[24]], False)
                for bk in range(8):
                    pb = nps()
                    P.op("pe", I("matmul",
                        ps[pb][:], lhsT=ones_b[0:1, :], rhs=rowb[0:1, 3072 + bk * 512:3072 + (bk + 1) * 512], start=True, stop=False),
                        reads=[t_const, t_rowb], writes=[t_ps[pb]])
                    for s4 in range(4):
                        s = bk * 4 + s4
                        jf = SLOTS[s][0]
                        P.op("pe", I("matmul",
                            ps[pb][:, s4 * 128:(s4 + 1) * 128], lhsT=slot(vb, 6)[:, jf * 128:(jf + 1) * 128], rhs=wst[:, s, :],
                            start=False, stop=(s4 == 3)), reads=vt + [t_wst], writes=[t_ps[pb]])
                    for s4 in range(4):
                        s = bk * 4 + s4
                        jf, _, p0, p1 = SLOTS[s]
                        P.op("dve", I("tensor_tensor",
                            out=slot(jf)[p0:p1, tc * 128:(tc + 1) * 128], in0=ps[pb][p0:p1, s4 * 128:(s4 + 1) * 128],
                            in1=slot(jf)[p0:p1, tc * 128:(tc + 1) * 128], op=ALU.mult),
                            reads=[t_ps[pb], t_big[jf]], writes=[t_big[jf]])
            for jd in range(24):
                dump(f"guv{jd}", slot(jd), [t_big[jd]], False)
            for pc in range(4):
                wv, wt = wload(wview(f"gout{l}", 24)[:, :, pc * 256:(pc + 1) * 256], 24, 256)
                for mm in range(2):
                    m = pc * 2 + mm
                    pb = nps()
                    for jf in range(24):
                        P.op("pe", I("matmul",
                            ps[pb][:], lhsT=wv[:, jf, mm * 128:(mm + 1) * 128], rhs=slot(jf), start=(jf == 0), stop=(jf == 23)),
                            reads=[wt, t_big[jf]], writes=[t_ps[pb]])
                    resid_add(l, 16, b, t, m, pb)
                    dump("gx0", x[:, 0, 0:512], [t_x[0][0]], True)
            ffn_tile(l, b, t)

        for b in range(nseq):
            for c in range(8):
                P.op("sp", I("dma_start", out=x[:, c, :], in_=xin.ap()[b, :, c, :]),
                     writes=t_x[c], dma=f"xin{c}")
            for l in range(nlayers):
                if STG < 2:
                    continue
                if l % 2 == 0:
                    attn_layer(l, b)
                else:
                    gmlp_setup(l)
                    for t in range(NT):
                        gmlp_tile(l, b, t)
            for c in range(8):
                P.op("sp", I("dma_start", out=yout.ap()[b, :, c, :], in_=x[:, c, :]),
                     reads=t_x[c], dma=f"out{c}")
        P.emit(final_waits=[f"out{c}" for c in range(8)] + [k for k in P.dma_cnt if k.startswith("dbg")])
    return nc


_CACHE = {}


def kernel(**inputs):
    inp = {k: np.asarray(v, dtype=np.float32) for k, v in inputs.items()}
    blob, sm = host_prep(inp)
    if "nc" not in _CACHE:
        _CACHE["nc"] = build()
    nc = _CACHE["nc"]
    x = inp["x"]
    c = inp["c"]
    in_maps = []
    for core in range(8):
        xs = x[2 * core:2 * core + 2]
        xT = np.ascontiguousarray(xs.reshape(2, SEQ, 8, 128).transpose(0, 3, 2, 1))
        cT = np.ascontiguousarray(c[2 * core:2 * core + 2].reshape(2, 8, 128).transpose(2, 1, 0))
        m = {"blob": blob, "xT": xT, "cT": cT}
        m.update(sm)
        in_maps.append(m)
    res = run_bass_kernel_spmd(nc, in_maps, core_ids=list(range(8)))
    out = np.empty((16, SEQ, D), np.float32)
    for core in range(8):
        yT = res.results[core]["yT"]
        out[2 * core:2 * core + 2] = yT.transpose(0, 3, 2, 1).reshape(2, SEQ, D)
    return out
```

```python
import contextlib
import numpy as np
import concourse.bass as bass
import concourse.mybir as mybir
from concourse.bass_utils import run_bass_kernel_spmd

F32 = mybir.dt.float32
BF16 = mybir.dt.bfloat16
AF = mybir.ActivationFunctionType
ALU = mybir.AluOpType

D = 1024
SEQ = 2048
DEPTH = 4
TT = 512
NT = SEQ // TT
EPS = 1e-6
ENG = ("pe", "act", "dve", "pool", "sp")
import os
STG = float(os.environ.get("KSTAGE", "99"))


class T:
    __slots__ = ("name", "w", "r")

    def __init__(self, name):
        self.name = name
        self.w = None
        self.r = []


class Prog:
    def __init__(self, nc):
        self.nc = nc
        self.ops = {e: [] for e in ENG}
        self.dma_cnt = {}

    def _add(self, deps, d, eng):
        if d is None:
            return
        if d[0] == "e" and d[1] == eng and eng in ("pe", "sp"):
            return
        deps.add(d)

    def op(self, eng, fn, reads=(), writes=(), dma=None):
        deps = set()
        for t in reads:
            self._add(deps, t.w, eng)
        for t in writes:
            self._add(deps, t.w, eng)
            for d in t.r:
                self._add(deps, d, eng)
        idx = len(self.ops[eng])
        if dma is not None:
            c = self.dma_cnt.get(dma, 0) + 1
            self.dma_cnt[dma] = c
            me = ("d", dma, 16 * c)
        else:
            me = ("e", eng, idx)
        self.ops[eng].append([fn, deps, False, dma])
        for t in reads:
            t.r.append(me)
        for t in writes:
            t.w = me
            t.r = []
        return me

    def emit(self, final_waits=()):
        nc = self.nc
        for e in ENG:
            for o in self.ops[e]:
                for d in o[1]:
                    if d[0] == "e":
                        self.ops[d[1]][d[2]][2] = True
        val = {}
        for e in ENG:
            c = 0
            for i, o in enumerate(self.ops[e]):
                if o[3] is None and o[2]:
                    c += 1
                    val[(e, i)] = c
        with contextlib.ExitStack() as st:
            esem = {e: st.enter_context(nc.semaphore("s_" + e)) for e in ENG}
            dsem = {k: st.enter_context(nc.semaphore("d_" + str(k))) for k in self.dma_cnt}
            blk = st.enter_context(nc.Block())
            engobj = {"pe": blk.tensor, "act": blk.scalar, "dve": blk.vector,
                      "pool": blk.gpsimd, "sp": blk.sync}

            def run(e, eo):
                known = {}
                for o in self.ops[e]:
                    fn, deps, sig, dma = o
                    need = {}
                    for d in deps:
                        if d[0] == "e":
                            s, v = esem[d[1]], val[(d[1], d[2])]
                        else:
                            s, v = dsem[d[1]], d[2]
                        k = id(s)
                        if need.get(k, (None, 0))[1] < v:
                            need[k] = (s, v)
                    for k, (s, v) in need.items():
                        if known.get(k, 0) < v:
                            eo.wait_ge(s, v)
                            known[k] = v
                    ins = fn(eo)
                    if dma is not None:
                        ins.then_inc(dsem[dma], 16)
                    elif sig:
                        ins.then_inc(esem[e], 1)
                if e == "sp":
                    for k in final_waits:
                        eo.wait_ge(dsem[k], 16 * self.dma_cnt[k])

            for e in ENG:
                def mk(e):
                    def f(eo):
                        run(e, eo)
                    return f
                engobj[e](mk(e))


DBG = {}


def I(meth, *a, **k):
    return lambda e: getattr(e, meth)(*a, **k)


def _blob_layout():
    off = {}
    o = 0

    def add(name, n):
        nonlocal o
        off[name] = (o, n)
        o += n
    for l in range(DEPTH):
        add(f"ada{l}", 8 * 6144)
        if l % 2 == 0:
            add(f"wkv{l}", 8 * 768)
            add(f"wq{l}", 8 * 2048)
            add(f"wo{l}", 8 * 1024)
        else:
            add(f"gin{l}", 8 * 6144)
            add(f"gout{l}", 24 * 1024)
            add(f"wst{l}", 32 * 128)
        add(f"w1{l}", 8 * 4096)
        add(f"w2{l}", 32 * 1024)
    return off, o


BLOB_OFF, BLOB_N = _blob_layout()
PCH = 4096
assert BLOB_N % PCH == 0

Q_HEADS = [(8 * (c // 4) + c % 4, 8 * (c // 4) + 4 + c % 4) for c in range(8)]

SLOTS = []
for j in range(24):
    if j % 3 != 1:
        SLOTS.append((j, 2 * (j // 3) + (0 if j % 3 == 0 else 1), 0, 128))
for j in range(24):
    if j % 3 == 1:
        SLOTS.append((j, 2 * (j // 3), 0, 64))
        SLOTS.append((j, 2 * (j // 3) + 1, 64, 128))
assert len(SLOTS) == 32


def pk(W):
    K, N = W.shape
    return np.ascontiguousarray(W.reshape(K // 128, 128, N).transpose(1, 0, 2)).reshape(128, -1)


def host_prep(inp):
    blob = np.empty((128, BLOB_N), np.float32)

    def put(name, arr):
        o, n = BLOB_OFF[name]
        assert arr.shape == (128, n), (name, arr.shape, n)
        blob[:, o:o + n] = arr
    sw = np.arange(64) ^ 1
    for l in range(DEPTH):
        put(f"ada{l}", pk(inp["ada_w"][l]))
        j = l // 2
        if l % 2 == 0:
            W = inp["attn_w_qkv"][j]
            Wq, Wk, Wv = W[:, :1024], W[:, 1024:1280], W[:, 1280:1536]
            kcols = np.concatenate([Wk, Wk.reshape(1024, 4, 64)[:, :, sw].reshape(1024, 256), Wv], axis=1)
            put(f"wkv{l}", pk(kcols))
            Wq3 = Wq.reshape(1024, 16, 64)
            cols = []
            for c in range(8):
                ha, hb = Q_HEADS[c]
                cols += [Wq3[:, ha, :], Wq3[:, hb, :], Wq3[:, ha, :][:, sw], Wq3[:, hb, :][:, sw]]
            put(f"wq{l}", pk(np.concatenate(cols, axis=1)))
            Wo3 = inp["attn_w_o"][j].reshape(16, 64, 1024)
            rows = []
            for c in range(8):
                ha, hb = Q_HEADS[c]
                rows += [Wo3[ha], Wo3[hb]]
            put(f"wo{l}", pk(np.concatenate(rows, axis=0)))
        else:
            put(f"gin{l}", pk(inp["gmlp_w_in"][j]))
            put(f"gout{l}", pk(inp["gmlp_w_out"][j]))
            ws = inp["gmlp_w_s"][j]
            wst = np.stack([ws[g].T for (_, g, _, _) in SLOTS], axis=1)
            put(f"wst{l}", np.ascontiguousarray(wst).reshape(128, 32 * 128))
        put(f"w1{l}", pk(inp["mlp_w_in"][l]))
        put(f"w2{l}", pk(inp["mlp_w_out"][l]))

    def colv(v, nch):
        return np.ascontiguousarray(v.reshape(nch, 128).T)
    sm = {}
    sm["ada_b"] = np.ascontiguousarray(np.stack([colv(inp["ada_b"][l], 48) for l in range(DEPTH)], axis=1))
    sm["n1g"] = np.ascontiguousarray(np.stack([colv(inp["norm1_g"][l], 8) for l in range(DEPTH)], axis=1))
    sm["n2g"] = np.ascontiguousarray(np.stack([colv(inp["norm2_g"][l], 8) for l in range(DEPTH)], axis=1))
    pidx = np.arange(128) % 64
    qk = np.zeros((128, 2, 4), np.float32)
    for j in range(2):
        qk[:, j, 0] = inp["attn_q_norm_g"][j][pidx]
        qk[:, j, 1] = inp["attn_q_norm_g"][j][pidx ^ 1]
        qk[:, j, 2] = inp["attn_k_norm_g"][j][pidx]
        qk[:, j, 3] = inp["attn_k_norm_g"][j][pidx ^ 1]
    sm["qkg"] = qk
    sm["binu"] = np.ascontiguousarray(np.stack([colv(inp["gmlp_b_in"][j][:3072], 24) for j in range(2)], axis=1))
    sm["binv"] = np.ascontiguousarray(inp["gmlp_b_in"][:, 3072:]).reshape(2, 1, 3072)
    sm["lng"] = np.ascontiguousarray(inp["gmlp_ln_g"]).reshape(2, 1, 3072)
    sm["lnb"] = np.ascontiguousarray(inp["gmlp_ln_b"]).reshape(2, 1, 3072)
    sm["bsl"] = np.ascontiguousarray(np.stack([np.stack([inp["gmlp_b_s"][j][g] for (_, g, _, _) in SLOTS], 0)
                                                for j in range(2)], 0)).reshape(2, 1, 4096)
    t = np.arange(SEQ)
    row = (t // 64 - (SEQ // 64) // 2).astype(np.float32)
    col = (t % 64 - 32).astype(np.float32)
    inv = (10000.0 ** (-np.arange(16, dtype=np.float32) / 16)).astype(np.float32)
    ang = np.concatenate([row[:, None] * inv, col[:, None] * inv], axis=-1)
    cs, sn = np.cos(ang).astype(np.float32), np.sin(ang).astype(np.float32)
    pr = pidx // 2
    sign = np.where(pidx % 2 == 0, -1.0, 1.0).astype(np.float32)
    sm["ropeC"] = np.ascontiguousarray(cs[:, pr].T)
    sm["ropeS"] = np.ascontiguousarray((sn[:, pr] * sign[None, :]).T)
    return blob, sm


def build(nlayers=DEPTH, nseq=2):
    nc = bass.Bass("TRN2", target_bir_lowering=False)

    def din(name, shape):
        return nc.dram_tensor(name, list(shape), F32, kind="ExternalInput")
    blob = din("blob", [128, BLOB_N])
    xin = din("xT", [2, 128, 8, SEQ])
    cin = din("cT", [128, 8, 2])
    d_adab = din("ada_b", [128, 4, 48])
    d_n1g = din("n1g", [128, 4, 8])
    d_n2g = din("n2g", [128, 4, 8])
    d_qkg = din("qkg", [128, 2, 4])
    d_binu = din("binu", [128, 2, 24])
    d_binv = din("binv", [2, 1, 3072])
    d_lng = din("lng", [2, 1, 3072])
    d_lnb = din("lnb", [2, 1, 3072])
    d_bsl = din("bsl", [2, 1, 4096])
    d_ropeC = din("ropeC", [128, SEQ])
    d_ropeS = din("ropeS", [128, SEQ])
    yout = nc.dram_tensor("yT", [2, 128, 8, SEQ], F32, kind="ExternalOutput")
    wb16 = nc.dram_tensor("wb16", [128, BLOB_N], BF16, kind="Internal")
    KDBG = os.environ.get("KDBG", "") != ""
    DBG.clear()
    if KDBG:
        dbgF = nc.dram_tensor("dbgF", [24, 128, 512], F32, kind="ExternalOutput")
        dbgB = nc.dram_tensor("dbgB", [64, 128, 512], BF16, kind="ExternalOutput")

    P = Prog(nc)
    with contextlib.ExitStack() as st:
        def sb(name, shape, dt):
            return st.enter_context(nc.sbuf_tensor("s_" + name, list(shape), dt))
        x = sb("x", [128, 8, SEQ], F32)
        ring = [sb(f"ring{i}", [128, 8192], BF16) for i in range(2)]
        ringF = [r.bitcast(F32) for r in ring]
        h = sb("h", [128, 8, TT], BF16)
        big = sb("big", [128, 48 * 512], BF16)
        bigF = big.bitcast(F32)
        aux = sb("aux", [128, 25 * 512], BF16)
        auxF = aux.bitcast(F32)
        wst = sb("wst", [128, 32, 128], BF16)
        mod = sb("mod", [128, 4, 48, 2], F32)
        adab = sb("adab", [128, 4, 48], F32)
        n1g = sb("n1g", [128, 4, 8], F32)
        n2g = sb("n2g", [128, 4, 8], F32)
        qkg = sb("qkg", [128, 2, 4], F32)
        binu = sb("binu", [128, 2, 24], F32)
        Amod = sb("Amod", [128, 4, 2, 2, 8], F32)
        cT = sb("cT", [128, 8, 2], F32)
        condb = sb("condb", [128, 8, 2], BF16)
        ones_b = sb("ones_b", [128, 128], BF16)
        ones_f = sb("ones_f", [128, 128], F32)
        avg_d = sb("avg_d", [128, 128], BF16)
        avg_h = sb("avg_h", [128, 128], BF16)
        epst = sb("epst", [128, 1], F32)
        stat = sb("stat", [128, 64], F32)
        rowb = sb("rowb", [1, 8192], BF16)
        ps = [st.enter_context(nc.psum_tensor(f"ps{i}", [128, 512], F32)) for i in range(8)]

        t_x = [[T(f"x{c}_{t}") for t in range(NT)] for c in range(8)]
        t_ring = [T("ring0"), T("ring1")]
        t_h = T("h")
        t_big = [T(f"big{i}") for i in range(48)]
        t_aux = [T(f"aux{i}") for i in range(25)]
        t_wst = T("wst")
        t_mod, t_small, t_amod, t_cond, t_const = T("mod"), T("small"), T("amod"), T("cond"), T("const")
        t_stat, t_rowb = T("stat"), T("rowb")
        t_ps = [T(f"ps{i}") for i in range(8)]
        t_blob = T("blob")
        psi = [0]

        def nps():
            i = psi[0] % 6
            psi[0] += 1
            return i

        pacc = [0]

        def nps_acc():
            i = 6 + pacc[0] % 2
            pacc[0] += 1
            return i

        def dump(name, ap, tiles, f32, p0=0):
            if not KDBG or name in DBG:
                return
            kind = "F" if f32 else "B"
            idx = sum(1 for v in DBG.values() if v[0] == kind)
            np_, ncol = ap.shape[0], ap.shape[1]
            DBG[name] = (kind, idx, p0, np_, ncol)
            dst = (dbgF if f32 else dbgB).ap()[idx, p0:p0 + np_, 0:ncol]
            P.op("sp", I("dma_start", out=dst, in_=ap), reads=tiles, dma="dbg_" + name)

        def slot(i, n=1):
            return big[:, i * 512:(i + n) * 512]

        def slotF(i, n=2):
            return bigF[:, i * 256:(i + n) * 256]

        def axs(i, n=1):
            return aux[:, i * 512:(i + n) * 512]

        def axF(i, n=2):
            return auxF[:, i * 256:(i + n) * 256]

        P.op("pool", I("memset", ones_b[:], 1.0), writes=[t_const])
        P.op("pool", I("memset", ones_f[:], 1.0), writes=[t_const])
        P.op("pool", I("memset", avg_d[:], 1.0 / D), writes=[t_const])
        P.op("pool", I("memset", avg_h[:], 0.0), writes=[t_const])
        P.op("pool", I("memset", avg_h[0:64, 0:64], 1.0 / 64), writes=[t_const])
        P.op("pool", I("memset", avg_h[64:128, 64:128], 1.0 / 64), writes=[t_const])
        P.op("pool", I("memset", epst[:], EPS), writes=[t_const])
        for (dst, src) in ((adab, d_adab), (n1g, d_n1g), (n2g, d_n2g), (qkg, d_qkg), (binu, d_binu), (cT, cin)):
            P.op("sp", I("dma_start", out=dst[:], in_=src.ap()), writes=[t_small], dma="small")

        nch = BLOB_N // PCH
        last_store = None
        for i in range(nch):
            b = i % 2
            sgi = i % 4
            stg = big[:, sgi * 4096:(sgi + 1) * 4096]
            tl = t_big[sgi * 8:(sgi + 1) * 8]
            P.op("sp", I("dma_start", out=ringF[b][:], in_=blob.ap()[:, i * PCH:(i + 1) * PCH]),
                 writes=[t_ring[b]], dma=f"ring{b}")
            ce = "dve" if i % 2 == 0 else "pool"
            P.op(ce, I("tensor_copy", out=stg, in_=ringF[b][:]), reads=[t_ring[b]], writes=tl)
            last_store = P.op("act", I("dma_start", out=wb16.ap()[:, i * PCH:(i + 1) * PCH], in_=stg),
                              reads=tl, dma="st")
        t_blob.w = last_store

        def wview(name, kc):
            o, n = BLOB_OFF[name]
            return wb16.ap()[:, o:o + n].rearrange("p (k n) -> p k n", k=kc)

        rr = [0]

        def wload(src, kc, ncols):
            b = rr[0] % 2
            rr[0] += 1
            dst = ring[b][:, 0:kc * ncols].rearrange("p (k n) -> p k n", k=kc)
            P.op("sp", I("dma_start", out=dst, in_=src), reads=[t_blob], writes=[t_ring[b]], dma=f"ring{b}")
            return dst, t_ring[b]

        P.op("act", I("activation", out=condb[:], in_=cT[:], func=AF.Silu), reads=[t_small], writes=[t_cond])
        for l in range(nlayers):
            pb = nps()
            for pc in range(6):
                wv, wt = wload(wview(f"ada{l}", 8)[:, :, pc * 1024:(pc + 1) * 1024], 8, 1024)
                for jj in range(8):
                    j = pc * 8 + jj
                    for k in range(8):
                        P.op("pe", I("matmul",
                            ps[pb][:, 2 * j:2 * j + 2], lhsT=wv[:, k, jj * 128:(jj + 1) * 128], rhs=condb[:, k, :],
                            start=(k == 0), stop=(k == 7)), reads=[wt, t_cond], writes=[t_ps[pb]])
            P.op("dve", I("tensor_tensor",
                out=mod[:, l, :, :], in0=ps[pb][:, 0:96].rearrange("p (j b) -> p j b", b=2),
                in1=adab[:, l, :].unsqueeze(2).to_broadcast([128, 48, 2]), op=ALU.add),
                reads=[t_ps[pb], t_small], writes=[t_mod])
            for wh, (gt, base) in enumerate(((n1g, 8), (n2g, 32))):
                for b in range(2):
                    P.op("dve", I("scalar_tensor_tensor",
                        out=Amod[:, l, wh, b, :], in0=mod[:, l, base:base + 8, b], scalar=1.0, in1=gt[:, l, :],
                        op0=ALU.add, op1=ALU.mult), reads=[t_mod, t_small], writes=[t_amod])
            dump("mod", mod[:, l, :, :].rearrange("p j b -> p (j b)"), [t_mod], True)

        def norm_tile(l, wh, b, t):
            tsl = slice(t * TT, (t + 1) * TT)
            sh_base = 0 if wh == 0 else 24
            pb = nps()
            for c in range(8):
                if c % 2 == 0:
                    P.op("act", I("activation", out=slot(40 + c), in_=x[:, c, tsl], func=AF.Square),
                         reads=[t_x[c][t]], writes=[t_big[40 + c]])
                else:
                    P.op("pool", I("tensor_tensor", out=slot(40 + c), in0=x[:, c, tsl], in1=x[:, c, tsl], op=ALU.mult),
                         reads=[t_x[c][t]], writes=[t_big[40 + c]])
                P.op("pe", I("matmul", ps[pb][:], lhsT=avg_d[:], rhs=slot(40 + c), start=(c == 0), stop=(c == 7)),
                     reads=[t_big[40 + c], t_const], writes=[t_ps[pb]])
            P.op("act", I("activation", out=slotF(36), in_=ps[pb][:], func=AF.Ln, bias=epst[:], scale=1.0),
                 reads=[t_ps[pb], t_const], writes=t_big[36:38])
            P.op("act", I("activation", out=slotF(38), in_=slotF(36), func=AF.Exp, scale=-0.5),
                 reads=t_big[36:38], writes=t_big[38:40])
            for c in range(8):
                tmp = 32 + 2 * (c % 2)
                P.op("dve", I("scalar_tensor_tensor",
                    out=slotF(tmp), in0=x[:, c, tsl], scalar=Amod[:, l, wh, b, c:c + 1], in1=slotF(38),
                    op0=ALU.mult, op1=ALU.mult), reads=[t_x[c][t], t_amod] + t_big[38:40], writes=t_big[tmp:tmp + 2])
                if c % 2 == 0:
                    P.op("act", I("activation",
                        out=h[:, c, :], in_=slotF(tmp), func=AF.Identity, bias=mod[:, l, sh_base + c, b:b + 1], scale=1.0),
                        reads=t_big[tmp:tmp + 2] + [t_mod], writes=[t_h])
                else:
                    P.op("pool", I("tensor_scalar", out=h[:, c, :], in0=slotF(tmp), scalar1=1.0,
                                   scalar2=mod[:, l, sh_base + c, b:b + 1], op0=ALU.mult, op1=ALU.add),
                         reads=t_big[tmp:tmp + 2] + [t_mod], writes=[t_h])

        def resid_add(l, gbase, b, t, m, pb):
            tsl = slice(t * TT, (t + 1) * TT)
            P.op("dve", I("scalar_tensor_tensor",
                out=x[:, m, tsl], in0=ps[pb][:], scalar=mod[:, l, gbase + m, b:b + 1], in1=x[:, m, tsl],
                op0=ALU.mult, op1=ALU.add), reads=[t_ps[pb], t_mod, t_x[m][t]], writes=[t_x[m][t]])

        def ffn_tile(l, b, t):
            norm_tile(l, 1, b, t)
            for pc in range(4):
                wv, wt = wload(wview(f"w1{l}", 8)[:, :, pc * 1024:(pc + 1) * 1024], 8, 1024)
                for jj in range(8):
                    j = pc * 8 + jj
                    pb = nps()
                    for k in range(8):
                        P.op("pe", I("matmul",
                            ps[pb][:], lhsT=wv[:, k, jj * 128:(jj + 1) * 128], rhs=h[:, k, :], start=(k == 0), stop=(k == 7)),
                            reads=[wt, t_h], writes=[t_ps[pb]])
                    tmp = 32 + 2 * (j % 4)
                    P.op("act", I("activation", out=slotF(tmp), in_=ps[pb][:], func=AF.Relu),
                         reads=[t_ps[pb]], writes=t_big[tmp:tmp + 2])
                    P.op("dve", I("tensor_tensor", out=slot(j), in0=slotF(tmp), in1=slotF(tmp), op=ALU.mult),
                         reads=t_big[tmp:tmp + 2], writes=[t_big[j]])
            for pc in range(4):
                wv, wt = wload(wview(f"w2{l}", 32)[:, :, pc * 256:(pc + 1) * 256], 32, 256)
                for mm in range(2):
                    m = pc * 2 + mm
                    pb = nps()
                    for j in range(32):
                        P.op("pe", I("matmul",
                            ps[pb][:], lhsT=wv[:, j, mm * 128:(mm + 1) * 128], rhs=slot(j), start=(j == 0), stop=(j == 31)),
                            reads=[wt, t_big[j]], writes=[t_ps[pb]])
                    resid_add(l, 40, b, t, m, pb)

        def rope_norm(pq, pqs, gi, j, t, dst, dst_tiles, tb):
            tsl = slice(t * TT, (t + 1) * TT)
            P.op("act", I("activation", out=slot(tb), in_=ps[pq][:], func=AF.Square), reads=[t_ps[pq]], writes=[t_big[tb]])
            pm = nps()
            P.op("pe", I("matmul", ps[pm][:], lhsT=avg_h[:], rhs=slot(tb), start=True, stop=True),
                 reads=[t_big[tb], t_const], writes=[t_ps[pm]])
            P.op("act", I("activation", out=slotF(tb + 2), in_=ps[pm][:], func=AF.Ln, bias=epst[:], scale=1.0),
                 reads=[t_ps[pm], t_const], writes=t_big[tb + 2:tb + 4])
            P.op("act", I("activation", out=slotF(tb + 4), in_=slotF(tb + 2), func=AF.Exp, scale=-0.5),
                 reads=t_big[tb + 2:tb + 4], writes=t_big[tb + 4:tb + 6])
            P.op("dve", I("scalar_tensor_tensor", out=slotF(tb + 6), in0=ps[pq][:], scalar=qkg[:, j, gi:gi + 1],
                                                         in1=axF(21), op0=ALU.mult, op1=ALU.mult),
                 reads=[t_ps[pq], t_small, t_big[tb]] + t_aux[21:23], writes=t_big[tb + 6:tb + 8])
            P.op("dve", I("scalar_tensor_tensor", out=slotF(tb + 8), in0=ps[pqs][:], scalar=qkg[:, j, gi + 1:gi + 2],
                                                         in1=axF(23), op0=ALU.mult, op1=ALU.mult),
                 reads=[t_ps[pqs], t_small] + t_aux[23:25], writes=t_big[tb + 8:tb + 10])
            P.op("dve", I("tensor_tensor", out=slotF(tb + 6), in0=slotF(tb + 6), in1=slotF(tb + 8), op=ALU.add),
                 reads=t_big[tb + 6:tb + 10], writes=t_big[tb + 6:tb + 8])
            P.op("dve", I("tensor_tensor", out=dst, in0=slotF(tb + 6), in1=slotF(tb + 4), op=ALU.mult),
                 reads=t_big[tb + 4:tb + 8], writes=dst_tiles)

        def load_rope(t):
            tsl = slice(t * TT, (t + 1) * TT)
            P.op("sp", I("dma_start", out=axF(21), in_=d_ropeC.ap()[:, tsl]), writes=t_aux[21:23], dma="ropeC")
            P.op("sp", I("dma_start", out=axF(23), in_=d_ropeS.ap()[:, tsl]), writes=t_aux[23:25], dma="ropeS")

        kT = aux[:, 0:4096].rearrange("p (m t) -> p m t", m=2)
        Ve = aux[:, 4096:4096 + 2080].rearrange("p (h t c) -> p h t c", h=2, t=16)
        Vo = aux[:, 4096 + 2080:4096 + 2080 + 4096].rearrange("p (h t c) -> p h t c", h=2, t=16)
        t_kT = t_aux[0:8]
        t_V = t_aux[8:21]

        def attn_layer(l, b):
            j = l // 2
            P.op("pool", I("memset", Vo, 0.0), writes=t_V)
            P.op("pool", I("memset", Vo[:, :, :, 0:1], 1.0), writes=t_V)
            P.op("pool", I("memset", Ve[:, :, :, 64:65], 1.0), writes=t_V)
            for t in range(NT):
                if STG < 2.2:
                    continue
                norm_tile(l, 0, b, t)
                if STG < 2.3:
                    continue
                load_rope(t)
                wv, wt = wload(wview(f"wkv{l}", 8), 8, 768)
                for m in range(2):
                    if STG < 2.26:
                        continue
                    pq, pqs = nps(), nps()
                    for (pp, cb) in ((pq, m * 128), (pqs, 256 + m * 128)):
                        for k in range(8):
                            P.op("pe", I("matmul",
                                ps[pp][:], lhsT=wv[:, k, cb:cb + 128], rhs=h[:, k, :], start=(k == 0), stop=(k == 7)),
                                reads=[wt, t_h], writes=[t_ps[pp]])
                    if STG >= 2.28:
                        rope_norm(pq, pqs, 2, j, t, kT[:, m, t * TT:(t + 1) * TT], t_kT, 0)
                for tc in range(4):
                    if STG < 2.4:
                        continue
                    pv = nps()
                    g = t * 4 + tc
                    for k in range(8):
                        P.op("pe", I("matmul",
                            ps[pv][:, 0:256], lhsT=h[:, k, tc * 128:(tc + 1) * 128], rhs=wv[:, k, 512:768],
                            start=(k == 0), stop=(k == 7)), reads=[wt, t_h], writes=[t_ps[pv]])
                    P.op("act", I("activation",
                        out=Ve[:, :, g, 0:64], in_=ps[pv][:, 0:256].rearrange("p (h two c) -> p h two c", h=2, two=2)[:, :, 0, :],
                        func=AF.Copy), reads=[t_ps[pv]], writes=t_V)
                    P.op("dve", I("tensor_copy",
                        out=Vo[:, :, g, 64:128], in_=ps[pv][:, 0:256].rearrange("p (h two c) -> p h two c", h=2, two=2)[:, :, 1, :]),
                        reads=[t_ps[pv]], writes=t_V)
            if STG < 3:
                return
            for t in range(NT):
                norm_tile(l, 0, b, t)
                dump("h0", h[:, 0, :], [t_h], False)
                dump("kT0", kT[:, 0, 0:512], t_kT, False)
                dump("Ve0", Ve[:, 0, 0, :], t_V, False)
                dump("Vo0", Vo[:, 0, 0, :], t_V, False)
                load_rope(t)
                for half in range(2):
                    wv, wt = wload(wview(f"wq{l}", 8)[:, :, half * 1024:(half + 1) * 1024], 8, 1024)
                    for cc in range(4):
                        c = half * 4 + cc
                        pq, pqs = nps(), nps()
                        for (pp, cb) in ((pq, cc * 256), (pqs, cc * 256 + 128)):
                            for k in range(8):
                                P.op("pe", I("matmul",
                                    ps[pp][:], lhsT=wv[:, k, cb:cb + 128], rhs=h[:, k, :], start=(k == 0), stop=(k == 7)),
                                    reads=[wt, t_h], writes=[t_ps[pp]])
                        rope_norm(pq, pqs, 0, j, t, slot(c), [t_big[c]], 20)
                        dump("qT0", slot(0), [t_big[0]], False)
                if STG < 4:
                    continue
                pending = [None]
                for c in range(8):
                    m = c // 4
                    for hf in range(2):
                        p0 = hf * 64
                        po = nps_acc()
                        Vl = (lambda kc: Ve[:, m, kc, :]) if hf == 0 else (lambda kc: Vo[:, m, kc, :])
                        M = 65 if hf == 0 else 128
                        sps = {}

                        def emit_s(kc):
                            sp_ = nps()
                            sps[kc] = sp_
                            P.op("pe", I("matmul",
                                ps[sp_][:], lhsT=kT[p0:p0 + 64, m, kc * 128:(kc + 1) * 128], rhs=slot(c)[p0:p0 + 64, :],
                                start=True, stop=True), reads=t_kT + [t_big[c]], writes=[t_ps[sp_]])

                        def emit_pv(kc):
                            sp_ = sps[kc]
                            pslot = 16 + kc % 4
                            P.op("act", I("activation",
                                out=slot(pslot), in_=ps[sp_][:], func=AF.Exp, scale=0.125),
                                reads=[t_ps[sp_]], writes=[t_big[pslot]])
                            dump("P0", slot(16), [t_big[16]], False)
                            P.op("pe", I("matmul",
                                ps[po][0:M, :], lhsT=Vl(kc), rhs=slot(pslot), start=(kc == 0), stop=(kc == 15)),
                                reads=t_V + [t_big[pslot]], writes=[t_ps[po]])
                        emit_s(0)
                        emit_s(1)
                        for kc in range(16):
                            if kc + 2 < 16:
                                emit_s(kc + 2)
                            emit_pv(kc)
                        def mk_norm(c=c, hf=hf, p0=p0, po=po):
                            def f():
                                dp = 64 if hf == 0 else 0
                                rd = slotF(20)[dp:dp + 1, :]
                                P.op("dve", I("reciprocal", out=rd, in_=ps[po][dp:dp + 1, :]),
                                     reads=[t_ps[po]], writes=t_big[20:22])
                                dump("rd", rd, t_big[20:22], True, p0=dp)
                                pbc = nps()
                                P.op("pe", I("matmul",
                                    ps[pbc][:], lhsT=ones_f[dp:dp + 1, :], rhs=rd, start=True, stop=True),
                                    reads=t_big[20:22] + [t_const], writes=[t_ps[pbc]])
                                P.op("act", I("activation", out=slotF(22)[p0:p0 + 64, :], in_=ps[po][p0:p0 + 64, :], func=AF.Copy),
                                     reads=[t_ps[po]], writes=t_big[22:24])
                                P.op("dve", I("tensor_tensor",
                                    out=slot(8 + c)[p0:p0 + 64, :], in0=slotF(22)[p0:p0 + 64, :], in1=ps[pbc][p0:p0 + 64, :], op=ALU.mult),
                                    reads=t_big[22:24] + [t_ps[pbc]], writes=[t_big[8 + c]])
                                if hf == 1:
                                    dump("OT0", slot(8), [t_big[8]], False)
                            return f
                        if pending[0] is not None:
                            pending[0]()
                        pending[0] = mk_norm()
                if pending[0] is not None:
                    pending[0]()
                    pending[0] = None
                wv, wt = wload(wview(f"wo{l}", 8), 8, 1024)
                for mo in range(8):
                    pb = nps()
                    for c in range(8):
                        P.op("pe", I("matmul",
                            ps[pb][:], lhsT=wv[:, c, mo * 128:(mo + 1) * 128], rhs=slot(8 + c), start=(c == 0), stop=(c == 7)),
                            reads=[wt, t_big[8 + c]], writes=[t_ps[pb]])
                    resid_add(l, 16, b, t, mo, pb)
                    dump("x0", x[:, 0, 0:512], [t_x[0][0]], True)
                if STG >= 5:
                    ffn_tile(l, b, t)

        def gmlp_setup(l):
            j = l // 2
            lngF = auxF[:, 0:3072]
            P.op("sp", I("dma_start", out=lngF, in_=d_lng.ap()[j].to_broadcast([128, 3072])), writes=t_aux[0:12], dma="gs0")
            lnbF = bigF[:, 0:3072]
            P.op("sp", I("dma_start", out=lnbF, in_=d_lnb.ap()[j].to_broadcast([128, 3072])), writes=t_big[0:12], dma="gs1")
            P.op("dve", I("tensor_copy", out=aux[:, 12 * 512:18 * 512], in_=lnbF), reads=t_big[0:12], writes=t_aux[12:18])
            rtmp = bigF[0:1, 3072:3072 + 7168]
            P.op("sp", I("dma_start", out=rtmp[:, 0:3072], in_=d_binv.ap()[j]), writes=t_big[12:40], dma="gs2")
            P.op("sp", I("dma_start", out=rtmp[:, 3072:7168], in_=d_bsl.ap()[j]), writes=t_big[12:40], dma="gs3")
            P.op("dve", I("tensor_copy", out=rowb[:, 0:7168], in_=rtmp), reads=t_big[12:40], writes=[t_rowb])
            o, n = BLOB_OFF[f"wst{l}"]
            P.op("sp", I("dma_start", out=wst[:], in_=wb16.ap()[:, o:o + n].rearrange("p (s q) -> p s q", s=32)),
                 reads=[t_blob], writes=[t_wst], dma="gs4")

        def gmlp_tile(l, b, t):
            j = l // 2
            norm_tile(l, 0, b, t)
            for pc in range(3):
                wv, wt = wload(wview(f"gin{l}", 8)[:, :, pc * 1024:(pc + 1) * 1024], 8, 1024)
                for jj in range(8):
                    ju = pc * 8 + jj
                    pb = nps()
                    for k in range(8):
                        P.op("pe", I("matmul",
                            ps[pb][:], lhsT=wv[:, k, jj * 128:(jj + 1) * 128], rhs=h[:, k, :], start=(k == 0), stop=(k == 7)),
                            reads=[wt, t_h], writes=[t_ps[pb]])
                    P.op("act", I("activation",
                        out=slot(ju), in_=ps[pb][:], func=AF.Gelu, bias=binu[:, j, ju:ju + 1], scale=1.0),
                        reads=[t_ps[pb], t_small], writes=[t_big[ju]])
                    dump("gu0", slot(0), [t_big[0]], False)
            for pc in range(3):
                wv, wt = wload(wview(f"gin{l}", 8)[:, :, 3072 + pc * 1024:3072 + (pc + 1) * 1024], 8, 1024)
                for tc in range(4):
                    for nt in range(2):
                        pb = nps()
                        col0 = pc * 1024 + nt * 512
                        P.op("pe", I("matmul",
                            ps[pb][:], lhsT=ones_b[0:1, :], rhs=rowb[0:1, col0:col0 + 512], start=True, stop=False),
                            reads=[t_const, t_rowb], writes=[t_ps[pb]])
                        for k in range(8):
                            P.op("pe", I("matmul",
                                ps[pb][:], lhsT=h[:, k, tc * 128:(tc + 1) * 128], rhs=wv[:, k, nt * 512:(nt + 1) * 512],
                                start=False, stop=(k == 7)), reads=[wt, t_h], writes=[t_ps[pb]])
                        sl = 24 + 6 * tc + pc * 2 + nt
                        P.op("act", I("activation", out=slot(sl), in_=ps[pb][:], func=AF.Gelu),
                             reads=[t_ps[pb]], writes=[t_big[sl]])
            for tc in range(4):
                vb = 24 + 6 * tc
                vt = t_big[vb:vb + 6]
                so = tc * 16
                for q in range(6):
                    P.op("dve", I("bn_stats", out=stat[:, so * 0 + q * 6:q * 6 + 6], in_=slot(vb + q)),
                         reads=[t_big[vb + q]], writes=[t_stat])
                P.op("dve", I("bn_aggr", out=stat[:, 40:42], in_=stat[:, 0:36]), reads=[t_stat], writes=[t_stat])
                P.op("act", I("activation", out=stat[:, 42:43], in_=stat[:, 41:42], func=AF.Ln, bias=epst[:], scale=1.0),
                     reads=[t_stat, t_const], writes=[t_stat])
                P.op("act", I("activation", out=stat[:, 43:44], in_=stat[:, 42:43], func=AF.Exp, scale=-0.5),
                     reads=[t_stat], writes=[t_stat])
                P.op("dve", I("scalar_tensor_tensor", out=stat[:, 44:45], in0=stat[:, 40:41], scalar=-1.0, in1=stat[:, 43:44],
                                                             op0=ALU.mult, op1=ALU.mult), reads=[t_stat], writes=[t_stat])
                P.op("dve", I("tensor_copy", out=stat[:, 48:50], in_=stat[:, 43:45]), reads=[t_stat], writes=[t_stat])
                for q in range(6):
                    ta = 18 + 2 * (q % 3)
                    P.op("act", I("activation",
                        out=axF(ta), in_=slot(vb + q), func=AF.Identity, bias=stat[:, 49:50], scale=stat[:, 48:49]),
                        reads=[t_big[vb + q], t_stat], writes=t_aux[ta:ta + 2])
                    P.op("dve", I("tensor_tensor",
                        out=axF(ta), in0=axF(ta), in1=auxF[:, q * 512:(q + 1) * 512], op=ALU.mult),
                        reads=t_aux[ta:ta + 2] + t_aux[2 * q:2 * q + 2], writes=t_aux[ta:ta + 2])
                    P.op("dve", I("tensor_tensor",
                        out=slot(vb + q), in0=axF(ta), in1=aux[:, (12 + q) * 512:(13 + q) * 512], op=ALU.add),
                        reads=t_aux[ta:ta + 2] + [t_aux[12 + q]], writes=[t_big[vb + q]])
                    dump("gv0", slot(24), [t_big[24]], False)
                for bk in range(8):
                    pb = nps()
                    P.op("pe", I("matmul",
                        ps[pb][:], lhsT=ones_b[0:1, :], rhs=rowb[0:1, 3072 + bk * 512:3072 + (bk + 1) * 512], start=True, stop=False),
                        reads=[t_const, t_rowb], writes=[t_ps[pb]])
                    for s4 in range(4):
                        s = bk * 4 + s4
                        jf = SLOTS[s][0]
                        P.op("pe", I("matmul",
                            ps[pb][:, s4 * 128:(s4 + 1) * 128], lhsT=slot(vb, 6)[:, jf * 128:(jf + 1) * 128], rhs=wst[:, s, :],
                            start=False, stop=(s4 == 3)), reads=vt + [t_wst], writes=[t_ps[pb]])
                    for s4 in range(4):
                        s = bk * 4 + s4
                        jf, _, p0, p1 = SLOTS[s]
                        P.op("dve", I("tensor_tensor",
                            out=slot(jf)[p0:p1, tc * 128:(tc + 1) * 128], in0=ps[pb][p0:p1, s4 * 128:(s4 + 1) * 128],
                            in1=slot(jf)[p0:p1, tc * 128:(tc + 1) * 128], op=ALU.mult),
                            reads=[t_ps[pb], t_big[jf]], writes=[t_big[jf]])
            for jd in range(24):
                dump(f"guv{jd}", slot(jd), [t_big[jd]], False)
            for pc in range(4):
                wv, wt = wload(wview(f"gout{l}", 24)[:, :, pc * 256:(pc + 1) * 256], 24, 256)
                for mm in range(2):
                    m = pc * 2 + mm
                    pb = nps()
                    for jf in range(24):
                        P.op("pe", I("matmul",
                            ps[pb][:], lhsT=wv[:, jf, mm * 128:(mm + 1) * 128], rhs=slot(jf), start=(jf == 0), stop=(jf == 23)),
                            reads=[wt, t_big[jf]], writes=[t_ps[pb]])
                    resid_add(l, 16, b, t, m, pb)
                    dump("gx0", x[:, 0, 0:512], [t_x[0][0]], True)
            ffn_tile(l, b, t)

        for b in range(nseq):
            for c in range(8):
                P.op("sp", I("dma_start", out=x[:, c, :], in_=xin.ap()[b, :, c, :]),
                     writes=t_x[c], dma=f"xin{c}")
            for l in range(nlayers):
                if STG < 2:
                    continue
                if l % 2 == 0:
                    attn_layer(l, b)
                else:
                    gmlp_setup(l)
                    for t in range(NT):
                        gmlp_tile(l, b, t)
            for c in range(8):
                P.op("sp", I("dma_start", out=yout.ap()[b, :, c, :], in_=x[:, c, :]),
                     reads=t_x[c], dma=f"out{c}")
        P.emit(final_waits=[f"out{c}" for c in range(8)] + [k for k in P.dma_cnt if k.startswith("dbg")])
    return nc


_CACHE = {}


def kernel(**inputs):
    inp = {k: np.asarray(v, dtype=np.float32) for k, v in inputs.items()}
    blob, sm = host_prep(inp)
    if "nc" not in _CACHE:
        _CACHE["nc"] = build()
    nc = _CACHE["nc"]
    x = inp["x"]
    c = inp["c"]
    in_maps = []
    for core in range(8):
        xs = x[2 * core:2 * core + 2]
        xT = np.ascontiguousarray(xs.reshape(2, SEQ, 8, 128).transpose(0, 3, 2, 1))
        cT = np.ascontiguousarray(c[2 * core:2 * core + 2].reshape(2, 8, 128).transpose(2, 1, 0))
        m = {"blob": blob, "xT": xT, "cT": cT}
        m.update(sm)
        in_maps.append(m)
    res = run_bass_kernel_spmd(nc, in_maps, core_ids=list(range(8)))
    out = np.empty((16, SEQ, D), np.float32)
    for core in range(8):
        yT = res.results[core]["yT"]
        out[2 * core:2 * core + 2] = yT.transpose(0, 3, 2, 1).reshape(2, SEQ, D)
    return out
```

```python
import contextlib
import numpy as np
import concourse.bass as bass
import concourse.mybir as mybir
from concourse.bass_utils import run_bass_kernel_spmd

F32 = mybir.dt.float32
BF16 = mybir.dt.bfloat16
AF = mybir.ActivationFunctionType
ALU = mybir.AluOpType

D = 1024
SEQ = 2048
DEPTH = 4
TT = 512
NT = SEQ // TT
EPS = 1e-6
ENG = ("pe", "act", "dve", "pool", "sp")
import os
STG = float(os.environ.get("KSTAGE", "99"))


class T:
    __slots__ = ("name", "w", "r")

    def __init__(self, name):
        self.name = name
        self.w = None
        self.r = []


class Prog:
    def __init__(self, nc):
        self.nc = nc
        self.ops = {e: [] for e in ENG}
        self.dma_cnt = {}

    def _add(self, deps, d, eng):
        if d is None:
            return
        if d[0] == "e" and d[1] == eng and eng in ("pe", "sp"):
            return
        deps.add(d)

    def op(self, eng, fn, reads=(), writes=(), dma=None):
        deps = set()
        for t in reads:
            self._add(deps, t.w, eng)
        for t in writes:
            self._add(deps, t.w, eng)
            for d in t.r:
                self._add(deps, d, eng)
        idx = len(self.ops[eng])
        if dma is not None:
            c = self.dma_cnt.get(dma, 0) + 1
            self.dma_cnt[dma] = c
            me = ("d", dma, 16 * c)
        else:
            me = ("e", eng, idx)
        self.ops[eng].append([fn, deps, False, dma])
        for t in reads:
            t.r.append(me)
        for t in writes:
            t.w = me
            t.r = []
        return me

    def emit(self, final_waits=()):
        nc = self.nc
        for e in ENG:
            for o in self.ops[e]:
                for d in o[1]:
                    if d[0] == "e":
                        self.ops[d[1]][d[2]][2] = True
        val = {}
        for e in ENG:
            c = 0
            for i, o in enumerate(self.ops[e]):
                if o[3] is None and o[2]:
                    c += 1
                    val[(e, i)] = c
        with contextlib.ExitStack() as st:
            esem = {e: st.enter_context(nc.semaphore("s_" + e)) for e in ENG}
            dsem = {k: st.enter_context(nc.semaphore("d_" + str(k))) for k in self.dma_cnt}
            blk = st.enter_context(nc.Block())
            engobj = {"pe": blk.tensor, "act": blk.scalar, "dve": blk.vector,
                      "pool": blk.gpsimd, "sp": blk.sync}

            def run(e, eo):
                known = {}
                for o in self.ops[e]:
                    fn, deps, sig, dma = o
                    need = {}
                    for d in deps:
                        if d[0] == "e":
                            s, v = esem[d[1]], val[(d[1], d[2])]
                        else:
                            s, v = dsem[d[1]], d[2]
                        k = id(s)
                        if need.get(k, (None, 0))[1] < v:
                            need[k] = (s, v)
                    for k, (s, v) in need.items():
                        if known.get(k, 0) < v:
                            eo.wait_ge(s, v)
                            known[k] = v
                    ins = fn(eo)
                    if dma is not None:
                        ins.then_inc(dsem[dma], 16)
                    elif sig:
                        ins.then_inc(esem[e], 1)
                if e == "sp":
                    for k in final_waits:
                        eo.wait_ge(dsem[k], 16 * self.dma_cnt[k])

            for e in ENG:
                def mk(e):
                    def f(eo):
                        run(e, eo)
                    return f
                engobj[e](mk(e))


DBG = {}


def I(meth, *a, **k):
    return lambda e: getattr(e, meth)(*a, **k)


def _blob_layout():
    off = {}
    o = 0

    def add(name, n):
        nonlocal o
        off[name] = (o, n)
        o += n
    for l in range(DEPTH):
        add(f"ada{l}", 8 * 6144)
        if l % 2 == 0:
            add(f"wkv{l}", 8 * 768)
            add(f"wq{l}", 8 * 2048)
            add(f"wo{l}", 8 * 1024)
        else:
            add(f"gin{l}", 8 * 6144)
            add(f"gout{l}", 24 * 1024)
            add(f"wst{l}", 32 * 128)
        add(f"w1{l}", 8 * 4096)
        add(f"w2{l}", 32 * 1024)
    return off, o


BLOB_OFF, BLOB_N = _blob_layout()
PCH = 4096
assert BLOB_N % PCH == 0

Q_HEADS = [(8 * (c // 4) + c % 4, 8 * (c // 4) + 4 + c % 4) for c in range(8)]

SLOTS = []
for j in range(24):
    if j % 3 != 1:
        SLOTS.append((j, 2 * (j // 3) + (0 if j % 3 == 0 else 1), 0, 128))
for j in range(24):
    if j % 3 == 1:
        SLOTS.append((j, 2 * (j // 3), 0, 64))
        SLOTS.append((j, 2 * (j // 3) + 1, 64, 128))
assert len(SLOTS) == 32


def pk(W):
    K, N = W.shape
    return np.ascontiguousarray(W.reshape(K // 128, 128, N).transpose(1, 0, 2)).reshape(128, -1)


def host_prep(inp):
    blob = np.empty((128, BLOB_N), np.float32)

    def put(name, arr):
        o, n = BLOB_OFF[name]
        assert arr.shape == (128, n), (name, arr.shape, n)
        blob[:, o:o + n] = arr
    sw = np.arange(64) ^ 1
    for l in range(DEPTH):
        put(f"ada{l}", pk(inp["ada_w"][l]))
        j = l // 2
        if l % 2 == 0:
            W = inp["attn_w_qkv"][j]
            Wq, Wk, Wv = W[:, :1024], W[:, 1024:1280], W[:, 1280:1536]
            kcols = np.concatenate([Wk, Wk.reshape(1024, 4, 64)[:, :, sw].reshape(1024, 256), Wv], axis=1)
            put(f"wkv{l}", pk(kcols))
            Wq3 = Wq.reshape(1024, 16, 64)
            cols = []
            for c in range(8):
                ha, hb = Q_HEADS[c]
                cols += [Wq3[:, ha, :], Wq3[:, hb, :], Wq3[:, ha, :][:, sw], Wq3[:, hb, :][:, sw]]
            put(f"wq{l}", pk(np.concatenate(cols, axis=1)))
            Wo3 = inp["attn_w_o"][j].reshape(16, 64, 1024)
            rows = []
            for c in range(8):
                ha, hb = Q_HEADS[c]
                rows += [Wo3[ha], Wo3[hb]]
            put(f"wo{l}", pk(np.concatenate(rows, axis=0)))
        else:
            put(f"gin{l}", pk(inp["gmlp_w_in"][j]))
            put(f"gout{l}", pk(inp["gmlp_w_out"][j]))
            ws = inp["gmlp_w_s"][j]
            wst = np.stack([ws[g].T for (_, g, _, _) in SLOTS], axis=1)
            put(f"wst{l}", np.ascontiguousarray(wst).reshape(128, 32 * 128))
        put(f"w1{l}", pk(inp["mlp_w_in"][l]))
        put(f"w2{l}", pk(inp["mlp_w_out"][l]))

    def colv(v, nch):
        return np.ascontiguousarray(v.reshape(nch, 128).T)
    sm = {}
    sm["ada_b"] = np.ascontiguousarray(np.stack([colv(inp["ada_b"][l], 48) for l in range(DEPTH)], axis=1))
    sm["n1g"] = np.ascontiguousarray(np.stack([colv(inp["norm1_g"][l], 8) for l in range(DEPTH)], axis=1))
    sm["n2g"] = np.ascontiguousarray(np.stack([colv(inp["norm2_g"][l], 8) for l in range(DEPTH)], axis=1))
    pidx = np.arange(128) % 64
    qk = np.zeros((128, 2, 4), np.float32)
    for j in range(2):
        qk[:, j, 0] = inp["attn_q_norm_g"][j][pidx]
        qk[:, j, 1] = inp["attn_q_norm_g"][j][pidx ^ 1]
        qk[:, j, 2] = inp["attn_k_norm_g"][j][pidx]
        qk[:, j, 3] = inp["attn_k_norm_g"][j][pidx ^ 1]
    sm["qkg"] = qk
    sm["binu"] = np.ascontiguousarray(np.stack([colv(inp["gmlp_b_in"][j][:3072], 24) for j in range(2)], axis=1))
    sm["binv"] = np.ascontiguousarray(inp["gmlp_b_in"][:, 3072:]).reshape(2, 1, 3072)
    sm["lng"] = np.ascontiguousarray(inp["gmlp_ln_g"]).reshape(2, 1, 3072)
    sm["lnb"] = np.ascontiguousarray(inp["gmlp_ln_b"]).reshape(2, 1, 3072)
    sm["bsl"] = np.ascontiguousarray(np.stack([np.stack([inp["gmlp_b_s"][j][g] for (_, g, _, _) in SLOTS], 0)
                                                for j in range(2)], 0)).reshape(2, 1, 4096)
    t = np.arange(SEQ)
    row = (t // 64 - (SEQ // 64) // 2).astype(np.float32)
    col = (t % 64 - 32).astype(np.float32)
    inv = (10000.0 ** (-np.arange(16, dtype=np.float32) / 16)).astype(np.float32)
    ang = np.concatenate([row[:, None] * inv, col[:, None] * inv], axis=-1)
    cs, sn = np.cos(ang).astype(np.float32), np.sin(ang).astype(np.float32)
    pr = pidx // 2
    sign = np.where(pidx % 2 == 0, -1.0, 1.0).astype(np.float32)
    sm["ropeC"] = np.ascontiguousarray(cs[:, pr].T)
    sm["ropeS"] = np.ascontiguousarray((sn[:, pr] * sign[None, :]).T)
    return blob, sm


def build(nlayers=DEPTH, nseq=2):
    nc = bass.Bass("TRN2", target_bir_lowering=False)

    def din(name, shape):
        return nc.dram_tensor(name, list(shape), F32, kind="ExternalInput")
    blob = din("blob", [128, BLOB_N])
    xin = din("xT", [2, 128, 8, SEQ])
    cin = din("cT", [128, 8, 2])
    d_adab = din("ada_b", [128, 4, 48])
    d_n1g = din("n1g", [128, 4, 8])
    d_n2g = din("n2g", [128, 4, 8])
    d_qkg = din("qkg", [128, 2, 4])
    d_binu = din("binu", [128, 2, 24])
    d_binv = din("binv", [2, 1, 3072])
    d_lng = din("lng", [2, 1, 3072])
    d_lnb = din("lnb", [2, 1, 3072])
    d_bsl = din("bsl", [2, 1, 4096])
    d_ropeC = din("ropeC", [128, SEQ])
    d_ropeS = din("ropeS", [128, SEQ])
    yout = nc.dram_tensor("yT", [2, 128, 8, SEQ], F32, kind="ExternalOutput")
    wb16 = nc.dram_tensor("wb16", [128, BLOB_N], BF16, kind="Internal")
    KDBG = os.environ.get("KDBG", "") != ""
    DBG.clear()
    if KDBG:
        dbgF = nc.dram_tensor("dbgF", [24, 128, 512], F32, kind="ExternalOutput")
        dbgB = nc.dram_tensor("dbgB", [64, 128, 512], BF16, kind="ExternalOutput")

    P = Prog(nc)
    with contextlib.ExitStack() as st:
        def sb(name, shape, dt):
            return st.enter_context(nc.sbuf_tensor("s_" + name, list(shape), dt))
        x = sb("x", [128, 8, SEQ], F32)
        ring = [sb(f"ring{i}", [128, 8192], BF16) for i in range(2)]
        ringF = [r.bitcast(F32) for r in ring]
        h = sb("h", [128, 8, TT], BF16)
        big = sb("big", [128, 48 * 512], BF16)
        bigF = big.bitcast(F32)
        aux = sb("aux", [128, 25 * 512], BF16)
        auxF = aux.bitcast(F32)
        wst = sb("wst", [128, 32, 128], BF16)
        mod = sb("mod", [128, 4, 48, 2], F32)
        adab = sb("adab", [128, 4, 48], F32)
        n1g = sb("n1g", [128, 4, 8], F32)
        n2g = sb("n2g", [128, 4, 8], F32)
        qkg = sb("qkg", [128, 2, 4], F32)
        binu = sb("binu", [128, 2, 24], F32)
        Amod = sb("Amod", [128, 4, 2, 2, 8], F32)
        cT = sb("cT", [128, 8, 2], F32)
        condb = sb("condb", [128, 8, 2], BF16)
        ones_b = sb("ones_b", [128, 128], BF16)
        selA = sb("selA", [128, 128], F32)
        selB = sb("selB", [128, 128], F32)
        avg_d = sb("avg_d", [128, 128], BF16)
        avg_h = sb("avg_h", [128, 128], BF16)
        epst = sb("epst", [128, 1], F32)
        stat = sb("stat", [128, 64], F32)
        rowb = sb("rowb", [128, 7168], BF16)
        ps = [st.enter_context(nc.psum_tensor(f"ps{i}", [128, 512], F32)) for i in range(8)]

        t_x = [[T(f"x{c}_{t}") for t in range(NT)] for c in range(8)]
        t_ring = [T("ring0"), T("ring1")]
        t_h = T("h")
        t_big = [T(f"big{i}") for i in range(48)]
        t_aux = [T(f"aux{i}") for i in range(25)]
        t_wst = T("wst")
        t_mod, t_small, t_amod, t_cond, t_const = T("mod"), T("small"), T("amod"), T("cond"), T("const")
        t_stat, t_rowb = T("stat"), T("rowb")
        t_ps = [T(f"ps{i}") for i in range(8)]
        t_blob = T("blob")
        psi = [0]

        def nps():
            i = psi[0] % 6
            psi[0] += 1
            return i

        pacc = [0]

        def nps_acc():
            i = 6 + pacc[0] % 2
            pacc[0] += 1
            return i

        def dump(name, ap, tiles, f32, p0=0):
            if not KDBG or name in DBG:
                return
            kind = "F" if f32 else "B"
            idx = sum(1 for v in DBG.values() if v[0] == kind)
            np_, ncol = ap.shape[0], ap.shape[1]
            DBG[name] = (kind, idx, p0, np_, ncol)
            dst = (dbgF if f32 else dbgB).ap()[idx, p0:p0 + np_, 0:ncol]
            P.op("sp", I("dma_start", out=dst, in_=ap), reads=tiles, dma="dbg_" + name)

        def slot(i, n=1):
            return big[:, i * 512:(i + n) * 512]

        def slotF(i, n=2):
            return bigF[:, i * 256:(i + n) * 256]

        def axs(i, n=1):
            return aux[:, i * 512:(i + n) * 512]

        def axF(i, n=2):
            return auxF[:, i * 256:(i + n) * 256]

        P.op("pool", I("memset", ones_b[:], 0.0), writes=[t_const])
        P.op("pool", I("memset", ones_b[0:1, :], 1.0), writes=[t_const])
        P.op("pool", I("memset", selA[:], 0.0), writes=[t_const])
        P.op("pool", I("memset", selA[64:65, :], 1.0), writes=[t_const])
        P.op("pool", I("memset", selB[:], 0.0), writes=[t_const])
        P.op("pool", I("memset", selB[0:1, :], 1.0), writes=[t_const])
        P.op("pool", I("memset", rowb[:], 0.0), writes=[t_rowb])
        P.op("pool", I("memset", avg_d[:], 1.0 / D), writes=[t_const])
        P.op("pool", I("memset", avg_h[:], 0.0), writes=[t_const])
        P.op("pool", I("memset", avg_h[0:64, 0:64], 1.0 / 64), writes=[t_const])
        P.op("pool", I("memset", avg_h[64:128, 64:128], 1.0 / 64), writes=[t_const])
        P.op("pool", I("memset", epst[:], EPS), writes=[t_const])
        for (dst, src) in ((adab, d_adab), (n1g, d_n1g), (n2g, d_n2g), (qkg, d_qkg), (binu, d_binu), (cT, cin)):
            P.op("sp", I("dma_start", out=dst[:], in_=src.ap()), writes=[t_small], dma="small")

        nch = BLOB_N // PCH
        last_store = None
        for i in range(nch):
            b = i % 2
            sgi = i % 4
            stg = big[:, sgi * 4096:(sgi + 1) * 4096]
            tl = t_big[sgi * 8:(sgi + 1) * 8]
            P.op("sp", I("dma_start", out=ringF[b][:], in_=blob.ap()[:, i * PCH:(i + 1) * PCH]),
                 writes=[t_ring[b]], dma=f"ring{b}")
            ce = "dve" if i % 2 == 0 else "pool"
            P.op(ce, I("tensor_copy", out=stg, in_=ringF[b][:]), reads=[t_ring[b]], writes=tl)
            last_store = P.op("act", I("dma_start", out=wb16.ap()[:, i * PCH:(i + 1) * PCH], in_=stg),
                              reads=tl, dma="st")
        t_blob.w = last_store

        def wview(name, kc):
            o, n = BLOB_OFF[name]
            return wb16.ap()[:, o:o + n].rearrange("p (k n) -> p k n", k=kc)

        rr = [0]

        def wload(src, kc, ncols):
            b = rr[0] % 2
            rr[0] += 1
            dst = ring[b][:, 0:kc * ncols].rearrange("p (k n) -> p k n", k=kc)
            P.op("sp", I("dma_start", out=dst, in_=src), reads=[t_blob], writes=[t_ring[b]], dma=f"ring{b}")
            return dst, t_ring[b]

        P.op("act", I("activation", out=condb[:], in_=cT[:], func=AF.Silu), reads=[t_small], writes=[t_cond])
        for l in range(nlayers):
            pb = nps()
            for pc in range(6):
                wv, wt = wload(wview(f"ada{l}", 8)[:, :, pc * 1024:(pc + 1) * 1024], 8, 1024)
                for jj in range(8):
                    j = pc * 8 + jj
                    for k in range(8):
                        P.op("pe", I("matmul",
                            ps[pb][:, 2 * j:2 * j + 2], lhsT=wv[:, k, jj * 128:(jj + 1) * 128], rhs=condb[:, k, :],
                            start=(k == 0), stop=(k == 7)), reads=[wt, t_cond], writes=[t_ps[pb]])
            P.op("dve", I("tensor_tensor",
                out=mod[:, l, :, :], in0=ps[pb][:, 0:96].rearrange("p (j b) -> p j b", b=2),
                in1=adab[:, l, :].unsqueeze(2).to_broadcast([128, 48, 2]), op=ALU.add),
                reads=[t_ps[pb], t_small], writes=[t_mod])
            for wh, (gt, base) in enumerate(((n1g, 8), (n2g, 32))):
                for b in range(2):
                    P.op("dve", I("scalar_tensor_tensor",
                        out=Amod[:, l, wh, b, :], in0=mod[:, l, base:base + 8, b], scalar=1.0, in1=gt[:, l, :],
                        op0=ALU.add, op1=ALU.mult), reads=[t_mod, t_small], writes=[t_amod])
            dump("mod", mod[:, l, :, :].rearrange("p j b -> p (j b)"), [t_mod], True)

        def norm_tile(l, wh, b, t):
            tsl = slice(t * TT, (t + 1) * TT)
            sh_base = 0 if wh == 0 else 24
            pb = nps()
            for c in range(8):
                if c % 2 == 0:
                    P.op("act", I("activation", out=slot(40 + c), in_=x[:, c, tsl], func=AF.Square),
                         reads=[t_x[c][t]], writes=[t_big[40 + c]])
                else:
                    P.op("pool", I("tensor_tensor", out=slot(40 + c), in0=x[:, c, tsl], in1=x[:, c, tsl], op=ALU.mult),
                         reads=[t_x[c][t]], writes=[t_big[40 + c]])
                P.op("pe", I("matmul", ps[pb][:], lhsT=avg_d[:], rhs=slot(40 + c), start=(c == 0), stop=(c == 7)),
                     reads=[t_big[40 + c], t_const], writes=[t_ps[pb]])
            P.op("act", I("activation", out=slotF(36), in_=ps[pb][:], func=AF.Ln, bias=epst[:], scale=1.0),
                 reads=[t_ps[pb], t_const], writes=t_big[36:38])
            P.op("act", I("activation", out=slotF(38), in_=slotF(36), func=AF.Exp, scale=-0.5),
                 reads=t_big[36:38], writes=t_big[38:40])
            for c in range(8):
                tmp = 32 + 2 * (c % 2)
                P.op("dve", I("scalar_tensor_tensor",
                    out=slotF(tmp), in0=x[:, c, tsl], scalar=Amod[:, l, wh, b, c:c + 1], in1=slotF(38),
                    op0=ALU.mult, op1=ALU.mult), reads=[t_x[c][t], t_amod] + t_big[38:40], writes=t_big[tmp:tmp + 2])
                if c % 2 == 0:
                    P.op("act", I("activation",
                        out=h[:, c, :], in_=slotF(tmp), func=AF.Identity, bias=mod[:, l, sh_base + c, b:b + 1], scale=1.0),
                        reads=t_big[tmp:tmp + 2] + [t_mod], writes=[t_h])
                else:
                    P.op("pool", I("tensor_scalar", out=h[:, c, :], in0=slotF(tmp), scalar1=1.0,
                                   scalar2=mod[:, l, sh_base + c, b:b + 1], op0=ALU.mult, op1=ALU.add),
                         reads=t_big[tmp:tmp + 2] + [t_mod], writes=[t_h])

        def resid_add(l, gbase, b, t, m, pb):
            tsl = slice(t * TT, (t + 1) * TT)
            P.op("dve", I("scalar_tensor_tensor",
                out=x[:, m, tsl], in0=ps[pb][:], scalar=mod[:, l, gbase + m, b:b + 1], in1=x[:, m, tsl],
                op0=ALU.mult, op1=ALU.add), reads=[t_ps[pb], t_mod, t_x[m][t]], writes=[t_x[m][t]])

        def ffn_tile(l, b, t):
            norm_tile(l, 1, b, t)
            for pc in range(4):
                wv, wt = wload(wview(f"w1{l}", 8)[:, :, pc * 1024:(pc + 1) * 1024], 8, 1024)
                for jj in range(8):
                    j = pc * 8 + jj
                    pb = nps()
                    for k in range(8):
                        P.op("pe", I("matmul",
                            ps[pb][:], lhsT=wv[:, k, jj * 128:(jj + 1) * 128], rhs=h[:, k, :], start=(k == 0), stop=(k == 7)),
                            reads=[wt, t_h], writes=[t_ps[pb]])
                    tmp = 32 + 2 * (j % 4)
                    P.op("act", I("activation", out=slotF(tmp), in_=ps[pb][:], func=AF.Relu),
                         reads=[t_ps[pb]], writes=t_big[tmp:tmp + 2])
                    P.op("dve", I("tensor_tensor", out=slot(j), in0=slotF(tmp), in1=slotF(tmp), op=ALU.mult),
                         reads=t_big[tmp:tmp + 2], writes=[t_big[j]])
            for pc in range(4):
                wv, wt = wload(wview(f"w2{l}", 32)[:, :, pc * 256:(pc + 1) * 256], 32, 256)
                for mm in range(2):
                    m = pc * 2 + mm
                    pb = nps()
                    for j in range(32):
                        P.op("pe", I("matmul",
                            ps[pb][:], lhsT=wv[:, j, mm * 128:(mm + 1) * 128], rhs=slot(j), start=(j == 0), stop=(j == 31)),
                            reads=[wt, t_big[j]], writes=[t_ps[pb]])
                    resid_add(l, 40, b, t, m, pb)

        def rope_norm(pq, pqs, gi, j, t, dst, dst_tiles, tb):
            tsl = slice(t * TT, (t + 1) * TT)
            P.op("act", I("activation", out=slot(tb), in_=ps[pq][:], func=AF.Square), reads=[t_ps[pq]], writes=[t_big[tb]])
            pm = nps()
            P.op("pe", I("matmul", ps[pm][:], lhsT=avg_h[:], rhs=slot(tb), start=True, stop=True),
                 reads=[t_big[tb], t_const], writes=[t_ps[pm]])
            P.op("act", I("activation", out=slotF(tb + 2), in_=ps[pm][:], func=AF.Ln, bias=epst[:], scale=1.0),
                 reads=[t_ps[pm], t_const], writes=t_big[tb + 2:tb + 4])
            P.op("act", I("activation", out=slotF(tb + 4), in_=slotF(tb + 2), func=AF.Exp, scale=-0.5),
                 reads=t_big[tb + 2:tb + 4], writes=t_big[tb + 4:tb + 6])
            P.op("dve", I("scalar_tensor_tensor", out=slotF(tb + 6), in0=ps[pq][:], scalar=qkg[:, j, gi:gi + 1],
                                                         in1=axF(21), op0=ALU.mult, op1=ALU.mult),
                 reads=[t_ps[pq], t_small, t_big[tb]] + t_aux[21:23], writes=t_big[tb + 6:tb + 8])
            P.op("dve", I("scalar_tensor_tensor", out=slotF(tb + 8), in0=ps[pqs][:], scalar=qkg[:, j, gi + 1:gi + 2],
                                                         in1=axF(23), op0=ALU.mult, op1=ALU.mult),
                 reads=[t_ps[pqs], t_small] + t_aux[23:25], writes=t_big[tb + 8:tb + 10])
            P.op("dve", I("tensor_tensor", out=slotF(tb + 6), in0=slotF(tb + 6), in1=slotF(tb + 8), op=ALU.add),
                 reads=t_big[tb + 6:tb + 10], writes=t_big[tb + 6:tb + 8])
            if isinstance(dst, list):
                for (dap, q0, q1, dtl) in dst:
                    P.op("dve", I("tensor_tensor", out=dap[q0:q1, :], in0=slotF(tb + 6)[q0:q1, :], in1=slotF(tb + 4)[q0:q1, :], op=ALU.mult),
                         reads=t_big[tb + 4:tb + 8], writes=dtl)
            else:
                P.op("dve", I("tensor_tensor", out=dst, in0=slotF(tb + 6), in1=slotF(tb + 4), op=ALU.mult),
                     reads=t_big[tb + 4:tb + 8], writes=dst_tiles)

        def load_rope(t):
            tsl = slice(t * TT, (t + 1) * TT)
            P.op("sp", I("dma_start", out=axF(21), in_=d_ropeC.ap()[:, tsl]), writes=t_aux[21:23], dma="ropeC")
            P.op("sp", I("dma_start", out=axF(23), in_=d_ropeS.ap()[:, tsl]), writes=t_aux[23:25], dma="ropeS")

        kT = aux[:, 0:4096].rearrange("p (m t) -> p m t", m=2)
        Ve = aux[:, 4096:4096 + 2080].rearrange("p (h t c) -> p h t c", h=2, t=16)
        Vo = aux[:, 4096 + 2080:4096 + 2080 + 4096].rearrange("p (h t c) -> p h t c", h=2, t=16)
        t_kT = t_aux[0:8]
        t_V = t_aux[8:21]

        def attn_layer(l, b):
            j = l // 2
            P.op("pool", I("memset", Vo, 0.0), writes=t_V)
            P.op("pool", I("memset", Vo[:, :, :, 0:1], 1.0), writes=t_V)
            P.op("pool", I("memset", Ve[:, :, :, 64:65], 1.0), writes=t_V)
            for t in range(NT):
                if STG < 2.2:
                    continue
                norm_tile(l, 0, b, t)
                if STG < 2.3:
                    continue
                load_rope(t)
                wv, wt = wload(wview(f"wkv{l}", 8), 8, 768)
                for m in range(2):
                    if STG < 2.26:
                        continue
                    pq, pqs = nps(), nps()
                    for (pp, cb) in ((pq, m * 128), (pqs, 256 + m * 128)):
                        for k in range(8):
                            P.op("pe", I("matmul",
                                ps[pp][:], lhsT=wv[:, k, cb:cb + 128], rhs=h[:, k, :], start=(k == 0), stop=(k == 7)),
                                reads=[wt, t_h], writes=[t_ps[pp]])
                    if STG >= 2.28:
                        rope_norm(pq, pqs, 2, j, t, kT[:, m, t * TT:(t + 1) * TT], t_kT, 0)
                for tc in range(4):
                    if STG < 2.4:
                        continue
                    pv = nps()
                    g = t * 4 + tc
                    for k in range(8):
                        P.op("pe", I("matmul",
                            ps[pv][:, 0:256], lhsT=h[:, k, tc * 128:(tc + 1) * 128], rhs=wv[:, k, 512:768],
                            start=(k == 0), stop=(k == 7)), reads=[wt, t_h], writes=[t_ps[pv]])
                    P.op("act", I("activation",
                        out=Ve[:, :, g, 0:64], in_=ps[pv][:, 0:256].rearrange("p (h two c) -> p h two c", h=2, two=2)[:, :, 0, :],
                        func=AF.Copy), reads=[t_ps[pv]], writes=t_V)
                    P.op("dve", I("tensor_copy",
                        out=Vo[:, :, g, 64:128], in_=ps[pv][:, 0:256].rearrange("p (h two c) -> p h two c", h=2, two=2)[:, :, 1, :]),
                        reads=[t_ps[pv]], writes=t_V)
            if STG < 3:
                return
            for t in range(NT):
                norm_tile(l, 0, b, t)
                dump("h0", h[:, 0, :], [t_h], False)
                dump("kT0", kT[:, 0, 0:512], t_kT, False)
                dump("Ve0", Ve[:, 0, 0, :], t_V, False)
                dump("Vo0", Vo[:, 0, 0, :], t_V, False)
                load_rope(t)
                for c in range(8):
                    P.op("pool", I("memset", slot(32 + 2 * c)[64:128, :], 0.0), writes=[t_big[32 + 2 * c]])
                    P.op("pool", I("memset", slot(33 + 2 * c)[0:64, :], 0.0), writes=[t_big[33 + 2 * c]])
                for half in range(2):
                    wv, wt = wload(wview(f"wq{l}", 8)[:, :, half * 1024:(half + 1) * 1024], 8, 1024)
                    for cc in range(4):
                        c = half * 4 + cc
                        pq, pqs = nps(), nps()
                        for (pp, cb) in ((pq, cc * 256), (pqs, cc * 256 + 128)):
                            for k in range(8):
                                P.op("pe", I("matmul",
                                    ps[pp][:], lhsT=wv[:, k, cb:cb + 128], rhs=h[:, k, :], start=(k == 0), stop=(k == 7)),
                                    reads=[wt, t_h], writes=[t_ps[pp]])
                        rope_norm(pq, pqs, 0, j, t, [(slot(32 + 2 * c), 0, 64, [t_big[32 + 2 * c]]),
                                                      (slot(33 + 2 * c), 64, 128, [t_big[33 + 2 * c]])], None, 20)
                if STG < 4:
                    continue
                P.op("pool", I("memset", slotF(20), 0.0), writes=t_big[20:22])
                pending = [None]
                for c in range(8):
                    m = c // 4
                    for hf in range(2):
                        p0 = hf * 64
                        po = nps_acc()
                        Vl = (lambda kc: Ve[:, m, kc, :]) if hf == 0 else (lambda kc: Vo[:, m, kc, :])
                        M = 65 if hf == 0 else 128
                        sps = {}

                        def emit_s(kc):
                            sp_ = nps()
                            sps[kc] = sp_
                            P.op("pe", I("matmul",
                                ps[sp_][:], lhsT=kT[:, m, kc * 128:(kc + 1) * 128], rhs=slot(32 + 2 * c + hf),
                                start=True, stop=True), reads=t_kT + [t_big[32 + 2 * c + hf]], writes=[t_ps[sp_]])

                        def emit_pv(kc):
                            sp_ = sps[kc]
                            pslot = 16 + kc % 4
                            P.op("act", I("activation",
                                out=slot(pslot), in_=ps[sp_][:], func=AF.Exp, scale=0.125),
                                reads=[t_ps[sp_]], writes=[t_big[pslot]])
                            dump("P0", slot(16), [t_big[16]], False)
                            P.op("pe", I("matmul",
                                ps[po][0:M, :], lhsT=Vl(kc), rhs=slot(pslot), start=(kc == 0), stop=(kc == 15)),
                                reads=t_V + [t_big[pslot]], writes=[t_ps[po]])
                        emit_s(0)
                        emit_s(1)
                        for kc in range(16):
                            if kc + 2 < 16:
                                emit_s(kc + 2)
                            emit_pv(kc)
                        def mk_norm(c=c, hf=hf, p0=p0, po=po):
                            def f():
                                dp = 64 if hf == 0 else 0
                                rd = slotF(20)[dp:dp + 1, :]
                                P.op("dve", I("reciprocal", out=rd, in_=ps[po][dp:dp + 1, :]),
                                     reads=[t_ps[po]], writes=t_big[20:22])
                                dump("rd", rd, t_big[20:22], True, p0=dp)
                                pbc = nps()
                                P.op("pe", I("matmul",
                                    ps[pbc][:], lhsT=(selA if hf == 0 else selB)[:], rhs=slotF(20), start=True, stop=True),
                                    reads=t_big[20:22] + [t_const], writes=[t_ps[pbc]])
                                P.op("act", I("activation", out=slotF(22)[p0:p0 + 64, :], in_=ps[po][p0:p0 + 64, :], func=AF.Copy),
                                     reads=[t_ps[po]], writes=t_big[22:24])
                                P.op("dve", I("tensor_tensor",
                                    out=slot(8 + c)[p0:p0 + 64, :], in0=slotF(22)[p0:p0 + 64, :], in1=ps[pbc][p0:p0 + 64, :], op=ALU.mult),
                                    reads=t_big[22:24] + [t_ps[pbc]], writes=[t_big[8 + c]])
                                if hf == 1:
                                    dump("OT0", slot(8), [t_big[8]], False)
                            return f
                        if pending[0] is not None:
                            pending[0]()
                        pending[0] = mk_norm()
                if pending[0] is not None:
                    pending[0]()
                    pending[0] = None
                wv, wt = wload(wview(f"wo{l}", 8), 8, 1024)
                for mo in range(8):
                    pb = nps()
                    for c in range(8):
                        P.op("pe", I("matmul",
                            ps[pb][:], lhsT=wv[:, c, mo * 128:(mo + 1) * 128], rhs=slot(8 + c), start=(c == 0), stop=(c == 7)),
                            reads=[wt, t_big[8 + c]], writes=[t_ps[pb]])
                    resid_add(l, 16, b, t, mo, pb)
                    dump("x0", x[:, 0, 0:512], [t_x[0][0]], True)
                if STG >= 5:
                    ffn_tile(l, b, t)

        def gmlp_setup(l):
            j = l // 2
            lngF = auxF[:, 0:3072]
            P.op("sp", I("dma_start", out=lngF, in_=d_lng.ap()[j].to_broadcast([128, 3072])), writes=t_aux[0:12], dma="gs0")
            lnbF = bigF[:, 0:3072]
            P.op("sp", I("dma_start", out=lnbF, in_=d_lnb.ap()[j].to_broadcast([128, 3072])), writes=t_big[0:12], dma="gs1")
            P.op("dve", I("tensor_copy", out=aux[:, 12 * 512:18 * 512], in_=lnbF), reads=t_big[0:12], writes=t_aux[12:18])
            rtmp = bigF[0:1, 3072:3072 + 7168]
            P.op("sp", I("dma_start", out=rtmp[:, 0:3072], in_=d_binv.ap()[j]), writes=t_big[12:40], dma="gs2")
            P.op("sp", I("dma_start", out=rtmp[:, 3072:7168], in_=d_bsl.ap()[j]), writes=t_big[12:40], dma="gs3")
            P.op("dve", I("tensor_copy", out=rowb[0:1, 0:7168], in_=rtmp), reads=t_big[12:40], writes=[t_rowb])
            o, n = BLOB_OFF[f"wst{l}"]
            P.op("sp", I("dma_start", out=wst[:], in_=wb16.ap()[:, o:o + n].rearrange("p (s q) -> p s q", s=32)),
                 reads=[t_blob], writes=[t_wst], dma="gs4")

        def gmlp_tile(l, b, t):
            j = l // 2
            norm_tile(l, 0, b, t)
            for pc in range(3):
                wv, wt = wload(wview(f"gin{l}", 8)[:, :, pc * 1024:(pc + 1) * 1024], 8, 1024)
                for jj in range(8):
                    ju = pc * 8 + jj
                    pb = nps()
                    for k in range(8):
                        P.op("pe", I("matmul",
                            ps[pb][:], lhsT=wv[:, k, jj * 128:(jj + 1) * 128], rhs=h[:, k, :], start=(k == 0), stop=(k == 7)),
                            reads=[wt, t_h], writes=[t_ps[pb]])
                    P.op("act", I("activation",
                        out=slot(ju), in_=ps[pb][:], func=AF.Gelu, bias=binu[:, j, ju:ju + 1], scale=1.0),
                        reads=[t_ps[pb], t_small], writes=[t_big[ju]])
                    dump("gu0", slot(0), [t_big[0]], False)
            for pc in range(3):
                wv, wt = wload(wview(f"gin{l}", 8)[:, :, 3072 + pc * 1024:3072 + (pc + 1) * 1024], 8, 1024)
                for tc in range(4):
                    for nt in range(2):
                        pb = nps()
                        col0 = pc * 1024 + nt * 512
                        P.op("pe", I("matmul",
                            ps[pb][:], lhsT=ones_b[:], rhs=rowb[:, col0:col0 + 512], start=True, stop=False),
                            reads=[t_const, t_rowb], writes=[t_ps[pb]])
                        for k in range(8):
                            P.op("pe", I("matmul",
                                ps[pb][:], lhsT=h[:, k, tc * 128:(tc + 1) * 128], rhs=wv[:, k, nt * 512:(nt + 1) * 512],
                                start=False, stop=(k == 7)), reads=[wt, t_h], writes=[t_ps[pb]])
                        sl = 24 + 6 * tc + pc * 2 + nt
                        P.op("act", I("activation", out=slot(sl), in_=ps[pb][:], func=AF.Gelu),
                             reads=[t_ps[pb]], writes=[t_big[sl]])
            for tc in range(4):
                vb = 24 + 6 * tc
                vt = t_big[vb:vb + 6]
                so = tc * 16
                for q in range(6):
                    P.op("dve", I("bn_stats", out=stat[:, so * 0 + q * 6:q * 6 + 6], in_=slot(vb + q)),
                         reads=[t_big[vb + q]], writes=[t_stat])
                P.op("dve", I("bn_aggr", out=stat[:, 40:42], in_=stat[:, 0:36]), reads=[t_stat], writes=[t_stat])
                P.op("act", I("activation", out=stat[:, 42:43], in_=stat[:, 41:42], func=AF.Ln, bias=epst[:], scale=1.0),
                     reads=[t_stat, t_const], writes=[t_stat])
                P.op("act", I("activation", out=stat[:, 43:44], in_=stat[:, 42:43], func=AF.Exp, scale=-0.5),
                     reads=[t_stat], writes=[t_stat])
                P.op("dve", I("scalar_tensor_tensor", out=stat[:, 44:45], in0=stat[:, 40:41], scalar=-1.0, in1=stat[:, 43:44],
                                                             op0=ALU.mult, op1=ALU.mult), reads=[t_stat], writes=[t_stat])
                P.op("dve", I("tensor_copy", out=stat[:, 48:50], in_=stat[:, 43:45]), reads=[t_stat], writes=[t_stat])
                for q in range(6):
                    ta = 18 + 2 * (q % 3)
                    P.op("act", I("activation",
                        out=axF(ta), in_=slot(vb + q), func=AF.Identity, bias=stat[:, 49:50], scale=stat[:, 48:49]),
                        reads=[t_big[vb + q], t_stat], writes=t_aux[ta:ta + 2])
                    P.op("dve", I("tensor_tensor",
                        out=axF(ta), in0=axF(ta), in1=auxF[:, q * 512:(q + 1) * 512], op=ALU.mult),
                        reads=t_aux[ta:ta + 2] + t_aux[2 * q:2 * q + 2], writes=t_aux[ta:ta + 2])
                    P.op("dve", I("tensor_tensor",
                        out=slot(vb + q), in0=axF(ta), in1=aux[:, (12 + q) * 512:(13 + q) * 512], op=ALU.add),
                        reads=t_aux[ta:ta + 2] + [t_aux[12 + q]], writes=[t_big[vb + q]])
                    dump("gv0", slot(24), [t_big[24]], False)
                for bk in range(8):
                    pb = nps()
                    P.op("pe", I("matmul",
                        ps[pb][:], lhsT=ones_b[:], rhs=rowb[:, 3072 + bk * 512:3072 + (bk + 1) * 512], start=True, stop=False),
                        reads=[t_const, t_rowb], writes=[t_ps[pb]])
                    for s4 in range(4):
                        s = bk * 4 + s4
                        jf = SLOTS[s][0]
                        P.op("pe", I("matmul",
                            ps[pb][:, s4 * 128:(s4 + 1) * 128], lhsT=slot(vb, 6)[:, jf * 128:(jf + 1) * 128], rhs=wst[:, s, :],
                            start=False, stop=(s4 == 3)), reads=vt + [t_wst], writes=[t_ps[pb]])
                    for s4 in range(4):
                        s = bk * 4 + s4
                        jf, _, p0, p1 = SLOTS[s]
                        P.op("dve", I("tensor_tensor",
                            out=slot(jf)[p0:p1, tc * 128:(tc + 1) * 128], in0=ps[pb][p0:p1, s4 * 128:(s4 + 1) * 128],
                            in1=slot(jf)[p0:p1, tc * 128:(tc + 1) * 128], op=ALU.mult),
                            reads=[t_ps[pb], t_big[jf]], writes=[t_big[jf]])
            for jd in range(24):
                dump(f"guv{jd}", slot(jd), [t_big[jd]], False)
            for pc in range(4):
                wv, wt = wload(wview(f"gout{l}", 24)[:, :, pc * 256:(pc + 1) * 256], 24, 256)
                for mm in range(2):
                    m = pc * 2 + mm
                    pb = nps()
                    for jf in range(24):
                        P.op("pe", I("matmul",
                            ps[pb][:], lhsT=wv[:, jf, mm * 128:(mm + 1) * 128], rhs=slot(jf), start=(jf == 0), stop=(jf == 23)),
                            reads=[wt, t_big[jf]], writes=[t_ps[pb]])
                    resid_add(l, 16, b, t, m, pb)
                    dump("gx0", x[:, 0, 0:512], [t_x[0][0]], True)
            ffn_tile(l, b, t)

        for b in range(nseq):
            for c in range(8):
                P.op("sp", I("dma_start", out=x[:, c, :], in_=xin.ap()[b, :, c, :]),
                     writes=t_x[c], dma=f"xin{c}")
            for l in range(nlayers):
                if STG < 2:
                    continue
                if l % 2 == 0:
                    attn_layer(l, b)
                else:
                    gmlp_setup(l)
                    for t in range(NT):
                        gmlp_tile(l, b, t)
            for c in range(8):
                P.op("sp", I("dma_start", out=yout.ap()[b, :, c, :], in_=x[:, c, :]),
                     reads=t_x[c], dma=f"out{c}")
        P.emit(final_waits=[f"out{c}" for c in range(8)] + [k for k in P.dma_cnt if k.startswith("dbg")])
    return nc


_CACHE = {}


def kernel(**inputs):
    inp = {k: np.asarray(v, dtype=np.float32) for k, v in inputs.items()}
    blob, sm = host_prep(inp)
    if "nc" not in _CACHE:
        _CACHE["nc"] = build()
    nc = _CACHE["nc"]
    x = inp["x"]
    c = inp["c"]
    in_maps = []
    for core in range(8):
        xs = x[2 * core:2 * core + 2]
        xT = np.ascontiguousarray(xs.reshape(2, SEQ, 8, 128).transpose(0, 3, 2, 1))
        cT = np.ascontiguousarray(c[2 * core:2 * core + 2].reshape(2, 8, 128).transpose(2, 1, 0))
        m = {"blob": blob, "xT": xT, "cT": cT}
        m.update(sm)
        in_maps.append(m)
    res = run_bass_kernel_spmd(nc, in_maps, core_ids=list(range(8)))
    out = np.empty((16, SEQ, D), np.float32)
    for core in range(8):
        yT = res.results[core]["yT"]
        out[2 * core:2 * core + 2] = yT.transpose(0, 3, 2, 1).reshape(2, SEQ, D)
    return out
```

```python
import contextlib
import numpy as np
import concourse.bass as bass
import concourse.mybir as mybir
from concourse.bass_utils import run_bass_kernel_spmd

F32 = mybir.dt.float32
BF16 = mybir.dt.bfloat16
AF = mybir.ActivationFunctionType
ALU = mybir.AluOpType

D = 1024
SEQ = 2048
DEPTH = 4
TT = 512
NT = SEQ // TT
EPS = 1e-6
ENG = ("pe", "act", "dve", "pool", "sp")
import os
STG = float(os.environ.get("KSTAGE", "99"))


class T:
    __slots__ = ("name", "w", "r")

    def __init__(self, name):
        self.name = name
        self.w = None
        self.r = []


class Prog:
    def __init__(self, nc):
        self.nc = nc
        self.ops = {e: [] for e in ENG}
        self.dma_cnt = {}

    def _add(self, deps, d, eng):
        if d is None:
            return
        if d[0] == "e" and d[1] == eng and eng in ("pe", "sp"):
            return
        deps.add(d)

    def op(self, eng, fn, reads=(), writes=(), dma=None):
        deps = set()
        for t in reads:
            self._add(deps, t.w, eng)
        for t in writes:
            self._add(deps, t.w, eng)
            for d in t.r:
                self._add(deps, d, eng)
        idx = len(self.ops[eng])
        if dma is not None:
            c = self.dma_cnt.get(dma, 0) + 1
            self.dma_cnt[dma] = c
            me = ("d", dma, 16 * c)
        else:
            me = ("e", eng, idx)
        self.ops[eng].append([fn, deps, False, dma])
        for t in reads:
            t.r.append(me)
        for t in writes:
            t.w = me
            t.r = []
        return me

    def emit(self, final_waits=()):
        nc = self.nc
        for e in ENG:
            for o in self.ops[e]:
                for d in o[1]:
                    if d[0] == "e":
                        self.ops[d[1]][d[2]][2] = True
        val = {}
        for e in ENG:
            c = 0
            for i, o in enumerate(self.ops[e]):
                if o[3] is None and o[2]:
                    c += 1
                    val[(e, i)] = c
        with contextlib.ExitStack() as st:
            esem = {e: st.enter_context(nc.semaphore("s_" + e)) for e in ENG}
            dsem = {k: st.enter_context(nc.semaphore("d_" + str(k))) for k in self.dma_cnt}
            blk = st.enter_context(nc.Block())
            engobj = {"pe": blk.tensor, "act": blk.scalar, "dve": blk.vector,
                      "pool": blk.gpsimd, "sp": blk.sync}

            def run(e, eo):
                known = {}
                for o in self.ops[e]:
                    fn, deps, sig, dma = o
                    need = {}
                    for d in deps:
                        if d[0] == "e":
                            s, v = esem[d[1]], val[(d[1], d[2])]
                        else:
                            s, v = dsem[d[1]], d[2]
                        k = id(s)
                        if need.get(k, (None, 0))[1] < v:
                            need[k] = (s, v)
                    for k, (s, v) in need.items():
                        if known.get(k, 0) < v:
                            eo.wait_ge(s, v)
                            known[k] = v
                    ins = fn(eo)
                    if dma is not None:
                        ins.then_inc(dsem[dma], 16)
                    elif sig:
                        ins.then_inc(esem[e], 1)
                if e == "sp":
                    for k in final_waits:
                        eo.wait_ge(dsem[k], 16 * self.dma_cnt[k])

            for e in ENG:
                def mk(e):
                    def f(eo):
                        run(e, eo)
                    return f
                engobj[e](mk(e))


DBG = {}


def I(meth, *a, **k):
    return lambda e: getattr(e, meth)(*a, **k)


def _blob_layout():
    off = {}
    o = 0

    def add(name, n):
        nonlocal o
        off[name] = (o, n)
        o += n
    for l in range(DEPTH):
        add(f"ada{l}", 8 * 6144)
        if l % 2 == 0:
            add(f"wkv{l}", 8 * 768)
            add(f"wq{l}", 8 * 2048)
            add(f"wo{l}", 8 * 1024)
        else:
            add(f"gin{l}", 8 * 6144)
            add(f"gout{l}", 24 * 1024)
            add(f"wst{l}", 32 * 128)
        add(f"w1{l}", 8 * 4096)
        add(f"w2{l}", 32 * 1024)
    return off, o


BLOB_OFF, BLOB_N = _blob_layout()
PCH = 4096
assert BLOB_N % PCH == 0

Q_HEADS = [(8 * (c // 4) + c % 4, 8 * (c // 4) + 4 + c % 4) for c in range(8)]

SLOTS = []
for j in range(24):
    if j % 3 != 1:
        SLOTS.append((j, 2 * (j // 3) + (0 if j % 3 == 0 else 1), 0, 128))
for j in range(24):
    if j % 3 == 1:
        SLOTS.append((j, 2 * (j // 3), 0, 64))
        SLOTS.append((j, 2 * (j // 3) + 1, 64, 128))
assert len(SLOTS) == 32


def pk(W):
    K, N = W.shape
    return np.ascontiguousarray(W.reshape(K // 128, 128, N).transpose(1, 0, 2)).reshape(128, -1)


def host_prep(inp):
    blob = np.empty((128, BLOB_N), np.float32)

    def put(name, arr):
        o, n = BLOB_OFF[name]
        assert arr.shape == (128, n), (name, arr.shape, n)
        blob[:, o:o + n] = arr
    sw = np.arange(64) ^ 1
    for l in range(DEPTH):
        put(f"ada{l}", pk(inp["ada_w"][l]))
        j = l // 2
        if l % 2 == 0:
            W = inp["attn_w_qkv"][j]
            Wq, Wk, Wv = W[:, :1024], W[:, 1024:1280], W[:, 1280:1536]
            kcols = np.concatenate([Wk, Wk.reshape(1024, 4, 64)[:, :, sw].reshape(1024, 256), Wv], axis=1)
            put(f"wkv{l}", pk(kcols))
            Wq3 = Wq.reshape(1024, 16, 64)
            cols = []
            for c in range(8):
                ha, hb = Q_HEADS[c]
                cols += [Wq3[:, ha, :], Wq3[:, hb, :], Wq3[:, ha, :][:, sw], Wq3[:, hb, :][:, sw]]
            put(f"wq{l}", pk(np.concatenate(cols, axis=1)))
            Wo3 = inp["attn_w_o"][j].reshape(16, 64, 1024)
            rows = []
            for c in range(8):
                ha, hb = Q_HEADS[c]
                rows += [Wo3[ha], Wo3[hb]]
            put(f"wo{l}", pk(np.concatenate(rows, axis=0)))
        else:
            put(f"gin{l}", pk(inp["gmlp_w_in"][j]))
            put(f"gout{l}", pk(inp["gmlp_w_out"][j]))
            ws = inp["gmlp_w_s"][j]
            wst = np.stack([ws[g].T for (_, g, _, _) in SLOTS], axis=1)
            put(f"wst{l}", np.ascontiguousarray(wst).reshape(128, 32 * 128))
        put(f"w1{l}", pk(inp["mlp_w_in"][l]))
        put(f"w2{l}", pk(inp["mlp_w_out"][l]))

    def colv(v, nch):
        return np.ascontiguousarray(v.reshape(nch, 128).T)
    sm = {}
    sm["ada_b"] = np.ascontiguousarray(np.stack([colv(inp["ada_b"][l], 48) for l in range(DEPTH)], axis=1))
    sm["n1g"] = np.ascontiguousarray(np.stack([colv(inp["norm1_g"][l], 8) for l in range(DEPTH)], axis=1))
    sm["n2g"] = np.ascontiguousarray(np.stack([colv(inp["norm2_g"][l], 8) for l in range(DEPTH)], axis=1))
    pidx = np.arange(128) % 64
    qk = np.zeros((128, 2, 4), np.float32)
    for j in range(2):
        qk[:, j, 0] = inp["attn_q_norm_g"][j][pidx]
        qk[:, j, 1] = inp["attn_q_norm_g"][j][pidx ^ 1]
        qk[:, j, 2] = inp["attn_k_norm_g"][j][pidx]
        qk[:, j, 3] = inp["attn_k_norm_g"][j][pidx ^ 1]
    sm["qkg"] = qk
    sm["binu"] = np.ascontiguousarray(np.stack([colv(inp["gmlp_b_in"][j][:3072], 24) for j in range(2)], axis=1))
    sm["binv"] = np.ascontiguousarray(inp["gmlp_b_in"][:, 3072:]).reshape(2, 1, 3072)
    sm["lng"] = np.ascontiguousarray(inp["gmlp_ln_g"]).reshape(2, 1, 3072)
    sm["lnb"] = np.ascontiguousarray(inp["gmlp_ln_b"]).reshape(2, 1, 3072)
    sm["bsl"] = np.ascontiguousarray(np.stack([np.stack([inp["gmlp_b_s"][j][g] for (_, g, _, _) in SLOTS], 0)
                                                for j in range(2)], 0)).reshape(2, 1, 4096)
    t = np.arange(SEQ)
    row = (t // 64 - (SEQ // 64) // 2).astype(np.float32)
    col = (t % 64 - 32).astype(np.float32)
    inv = (10000.0 ** (-np.arange(16, dtype=np.float32) / 16)).astype(np.float32)
    ang = np.concatenate([row[:, None] * inv, col[:, None] * inv], axis=-1)
    cs, sn = np.cos(ang).astype(np.float32), np.sin(ang).astype(np.float32)
    pr = pidx // 2
    sign = np.where(pidx % 2 == 0, -1.0, 1.0).astype(np.float32)
    sm["ropeC"] = np.ascontiguousarray(cs[:, pr].T)
    sm["ropeS"] = np.ascontiguousarray((sn[:, pr] * sign[None, :]).T)
    return blob, sm


def build(nlayers=DEPTH, nseq=2):
    nc = bass.Bass("TRN2", target_bir_lowering=False)

    def din(name, shape):
        return nc.dram_tensor(name, list(shape), F32, kind="ExternalInput")
    blob = din("blob", [128, BLOB_N])
    xin = din("xT", [2, 128, 8, SEQ])
    cin = din("cT", [128, 8, 2])
    d_adab = din("ada_b", [128, 4, 48])
    d_n1g = din("n1g", [128, 4, 8])
    d_n2g = din("n2g", [128, 4, 8])
    d_qkg = din("qkg", [128, 2, 4])
    d_binu = din("binu", [128, 2, 24])
    d_binv = din("binv", [2, 1, 3072])
    d_lng = din("lng", [2, 1, 3072])
    d_lnb = din("lnb", [2, 1, 3072])
    d_bsl = din("bsl", [2, 1, 4096])
    d_ropeC = din("ropeC", [128, SEQ])
    d_ropeS = din("ropeS", [128, SEQ])
    yout = nc.dram_tensor("yT", [2, 128, 8, SEQ], F32, kind="ExternalOutput")
    wb16 = nc.dram_tensor("wb16", [128, BLOB_N], BF16, kind="Internal")
    KDBG = os.environ.get("KDBG", "") != ""
    DBG.clear()
    if KDBG:
        dbgF = nc.dram_tensor("dbgF", [24, 128, 512], F32, kind="ExternalOutput")
        dbgB = nc.dram_tensor("dbgB", [64, 128, 512], BF16, kind="ExternalOutput")

    P = Prog(nc)
    with contextlib.ExitStack() as st:
        def sb(name, shape, dt):
            return st.enter_context(nc.sbuf_tensor("s_" + name, list(shape), dt))
        x = sb("x", [128, 8, SEQ], F32)
        ring = [sb(f"ring{i}", [128, 8192], BF16) for i in range(2)]
        ringF = [r.bitcast(F32) for r in ring]
        h = sb("h", [128, 8, TT], BF16)
        big = sb("big", [128, 48 * 512], BF16)
        bigF = big.bitcast(F32)
        aux = sb("aux", [128, 25 * 512], BF16)
        auxF = aux.bitcast(F32)
        wst = sb("wst", [128, 32, 128], BF16)
        mod = sb("mod", [128, 4, 48, 2], F32)
        adab = sb("adab", [128, 4, 48], F32)
        n1g = sb("n1g", [128, 4, 8], F32)
        n2g = sb("n2g", [128, 4, 8], F32)
        qkg = sb("qkg", [128, 2, 4], F32)
        binu = sb("binu", [128, 2, 24], F32)
        Amod = sb("Amod", [128, 4, 2, 2, 8], F32)
        cT = sb("cT", [128, 8, 2], F32)
        condb = sb("condb", [128, 8, 2], BF16)
        ones_b = sb("ones_b", [128, 128], BF16)
        selA = sb("selA", [128, 128], F32)
        selB = sb("selB", [128, 128], F32)
        avg_d = sb("avg_d", [128, 128], BF16)
        avg_h = sb("avg_h", [128, 128], BF16)
        epst = sb("epst", [128, 1], F32)
        stat = sb("stat", [128, 64], F32)
        rowb = sb("rowb", [128, 7168], BF16)
        ps = [st.enter_context(nc.psum_tensor(f"ps{i}", [128, 512], F32)) for i in range(8)]

        t_x = [[T(f"x{c}_{t}") for t in range(NT)] for c in range(8)]
        t_ring = [T("ring0"), T("ring1")]
        t_h = T("h")
        t_big = [T(f"big{i}") for i in range(48)]
        t_aux = [T(f"aux{i}") for i in range(25)]
        t_wst = T("wst")
        t_mod, t_small, t_amod, t_cond, t_const = T("mod"), T("small"), T("amod"), T("cond"), T("const")
        t_stat, t_rowb = T("stat"), T("rowb")
        t_stat2 = [T(f"stat2_{i}") for i in range(4)]
        t_ps = [T(f"ps{i}") for i in range(8)]
        t_blob = T("blob")
        psi = [0]

        def nps():
            i = psi[0] % 6
            psi[0] += 1
            return i

        pacc = [0]

        def nps_acc():
            i = 6 + pacc[0] % 2
            pacc[0] += 1
            return i

        def dump(name, ap, tiles, f32, p0=0):
            if not KDBG or name in DBG:
                return
            kind = "F" if f32 else "B"
            idx = sum(1 for v in DBG.values() if v[0] == kind)
            np_, ncol = ap.shape[0], ap.shape[1]
            DBG[name] = (kind, idx, p0, np_, ncol)
            dst = (dbgF if f32 else dbgB).ap()[idx, p0:p0 + np_, 0:ncol]
            P.op("sp", I("dma_start", out=dst, in_=ap), reads=tiles, dma="dbg_" + name)

        def slot(i, n=1):
            return big[:, i * 512:(i + n) * 512]

        def slotF(i, n=2):
            return bigF[:, i * 256:(i + n) * 256]

        def axs(i, n=1):
            return aux[:, i * 512:(i + n) * 512]

        def axF(i, n=2):
            return auxF[:, i * 256:(i + n) * 256]

        P.op("pool", I("memset", ones_b[:], 0.0), writes=[t_const])
        P.op("pool", I("memset", ones_b[0:1, :], 1.0), writes=[t_const])
        P.op("pool", I("memset", selA[:], 0.0), writes=[t_const])
        P.op("pool", I("memset", selA[64:65, :], 1.0), writes=[t_const])
        P.op("pool", I("memset", selB[:], 0.0), writes=[t_const])
        P.op("pool", I("memset", selB[0:1, :], 1.0), writes=[t_const])
        P.op("pool", I("memset", rowb[:], 0.0), writes=[t_rowb])
        P.op("pool", I("memset", avg_d[:], 1.0 / D), writes=[t_const])
        P.op("pool", I("memset", avg_h[:], 0.0), writes=[t_const])
        P.op("pool", I("memset", avg_h[0:64, 0:64], 1.0 / 64), writes=[t_const])
        P.op("pool", I("memset", avg_h[64:128, 64:128], 1.0 / 64), writes=[t_const])
        P.op("pool", I("memset", epst[:], EPS), writes=[t_const])
        for (dst, src) in ((adab, d_adab), (n1g, d_n1g), (n2g, d_n2g), (qkg, d_qkg), (binu, d_binu), (cT, cin)):
            P.op("sp", I("dma_start", out=dst[:], in_=src.ap()), writes=[t_small], dma="small")

        nch = BLOB_N // PCH
        last_store = None
        for i in range(nch):
            b = i % 2
            sgi = i % 4
            stg = big[:, sgi * 4096:(sgi + 1) * 4096]
            tl = t_big[sgi * 8:(sgi + 1) * 8]
            P.op("sp", I("dma_start", out=ringF[b][:], in_=blob.ap()[:, i * PCH:(i + 1) * PCH]),
                 writes=[t_ring[b]], dma=f"ring{b}")
            ce = "dve" if i % 2 == 0 else "pool"
            P.op(ce, I("tensor_copy", out=stg, in_=ringF[b][:]), reads=[t_ring[b]], writes=tl)
            last_store = P.op("act", I("dma_start", out=wb16.ap()[:, i * PCH:(i + 1) * PCH], in_=stg),
                              reads=tl, dma="st")
        t_blob.w = last_store

        def wview(name, kc):
            o, n = BLOB_OFF[name]
            return wb16.ap()[:, o:o + n].rearrange("p (k n) -> p k n", k=kc)

        rr = [0]

        def wload(src, kc, ncols):
            b = rr[0] % 2
            rr[0] += 1
            dst = ring[b][:, 0:kc * ncols].rearrange("p (k n) -> p k n", k=kc)
            P.op("sp", I("dma_start", out=dst, in_=src), reads=[t_blob], writes=[t_ring[b]], dma=f"ring{b}")
            return dst, t_ring[b]

        P.op("act", I("activation", out=condb[:], in_=cT[:], func=AF.Silu), reads=[t_small], writes=[t_cond])
        for l in range(nlayers):
            pb = nps()
            for pc in range(6):
                wv, wt = wload(wview(f"ada{l}", 8)[:, :, pc * 1024:(pc + 1) * 1024], 8, 1024)
                for jj in range(8):
                    j = pc * 8 + jj
                    for k in range(8):
                        P.op("pe", I("matmul",
                            ps[pb][:, 2 * j:2 * j + 2], lhsT=wv[:, k, jj * 128:(jj + 1) * 128], rhs=condb[:, k, :],
                            start=(k == 0), stop=(k == 7)), reads=[wt, t_cond], writes=[t_ps[pb]])
            P.op("dve", I("tensor_tensor",
                out=mod[:, l, :, :], in0=ps[pb][:, 0:96].rearrange("p (j b) -> p j b", b=2),
                in1=adab[:, l, :].unsqueeze(2).to_broadcast([128, 48, 2]), op=ALU.add),
                reads=[t_ps[pb], t_small], writes=[t_mod])
            for wh, (gt, base) in enumerate(((n1g, 8), (n2g, 32))):
                for b in range(2):
                    P.op("dve", I("scalar_tensor_tensor",
                        out=Amod[:, l, wh, b, :], in0=mod[:, l, base:base + 8, b], scalar=1.0, in1=gt[:, l, :],
                        op0=ALU.add, op1=ALU.mult), reads=[t_mod, t_small], writes=[t_amod])
            dump("mod", mod[:, l, :, :].rearrange("p j b -> p (j b)"), [t_mod], True)

        def norm_tile(l, wh, b, t):
            tsl = slice(t * TT, (t + 1) * TT)
            sh_base = 0 if wh == 0 else 24
            pb = nps()
            for c in range(8):
                if c % 2 == 0:
                    P.op("act", I("activation", out=slot(40 + c), in_=x[:, c, tsl], func=AF.Square),
                         reads=[t_x[c][t]], writes=[t_big[40 + c]])
                else:
                    P.op("pool", I("tensor_tensor", out=slot(40 + c), in0=x[:, c, tsl], in1=x[:, c, tsl], op=ALU.mult),
                         reads=[t_x[c][t]], writes=[t_big[40 + c]])
                P.op("pe", I("matmul", ps[pb][:], lhsT=avg_d[:], rhs=slot(40 + c), start=(c == 0), stop=(c == 7)),
                     reads=[t_big[40 + c], t_const], writes=[t_ps[pb]])
            P.op("act", I("activation", out=slotF(36), in_=ps[pb][:], func=AF.Ln, bias=epst[:], scale=1.0),
                 reads=[t_ps[pb], t_const], writes=t_big[36:38])
            P.op("act", I("activation", out=slotF(38), in_=slotF(36), func=AF.Exp, scale=-0.5),
                 reads=t_big[36:38], writes=t_big[38:40])
            for c in range(8):
                tmp = 32 + 2 * (c % 2)
                P.op("dve", I("scalar_tensor_tensor",
                    out=slotF(tmp), in0=x[:, c, tsl], scalar=Amod[:, l, wh, b, c:c + 1], in1=slotF(38),
                    op0=ALU.mult, op1=ALU.mult), reads=[t_x[c][t], t_amod] + t_big[38:40], writes=t_big[tmp:tmp + 2])
                if c % 2 == 0:
                    P.op("act", I("activation",
                        out=h[:, c, :], in_=slotF(tmp), func=AF.Identity, bias=mod[:, l, sh_base + c, b:b + 1], scale=1.0),
                        reads=t_big[tmp:tmp + 2] + [t_mod], writes=[t_h])
                else:
                    P.op("pool", I("tensor_scalar", out=h[:, c, :], in0=slotF(tmp), scalar1=1.0,
                                   scalar2=mod[:, l, sh_base + c, b:b + 1], op0=ALU.mult, op1=ALU.add),
                         reads=t_big[tmp:tmp + 2] + [t_mod], writes=[t_h])

        def resid_add(l, gbase, b, t, m, pb):
            tsl = slice(t * TT, (t + 1) * TT)
            P.op("dve", I("scalar_tensor_tensor",
                out=x[:, m, tsl], in0=ps[pb][:], scalar=mod[:, l, gbase + m, b:b + 1], in1=x[:, m, tsl],
                op0=ALU.mult, op1=ALU.add), reads=[t_ps[pb], t_mod, t_x[m][t]], writes=[t_x[m][t]])

        def ffn_tile(l, b, t):
            norm_tile(l, 1, b, t)
            for pc in range(4):
                wv, wt = wload(wview(f"w1{l}", 8)[:, :, pc * 1024:(pc + 1) * 1024], 8, 1024)
                for jj in range(8):
                    j = pc * 8 + jj
                    pb = nps()
                    for k in range(8):
                        P.op("pe", I("matmul",
                            ps[pb][:], lhsT=wv[:, k, jj * 128:(jj + 1) * 128], rhs=h[:, k, :], start=(k == 0), stop=(k == 7)),
                            reads=[wt, t_h], writes=[t_ps[pb]])
                    tmp = 32 + 2 * (j % 4)
                    P.op("act", I("activation", out=slotF(tmp), in_=ps[pb][:], func=AF.Relu),
                         reads=[t_ps[pb]], writes=t_big[tmp:tmp + 2])
                    P.op("dve", I("tensor_tensor", out=slot(j), in0=slotF(tmp), in1=slotF(tmp), op=ALU.mult),
                         reads=t_big[tmp:tmp + 2], writes=[t_big[j]])
            for pc in range(4):
                wv, wt = wload(wview(f"w2{l}", 32)[:, :, pc * 256:(pc + 1) * 256], 32, 256)
                for mm in range(2):
                    m = pc * 2 + mm
                    pb = nps()
                    for j in range(32):
                        P.op("pe", I("matmul",
                            ps[pb][:], lhsT=wv[:, j, mm * 128:(mm + 1) * 128], rhs=slot(j), start=(j == 0), stop=(j == 31)),
                            reads=[wt, t_big[j]], writes=[t_ps[pb]])
                    resid_add(l, 40, b, t, m, pb)

        def rope_norm(pq, pqs, gi, j, t, dst, dst_tiles, tb):
            tsl = slice(t * TT, (t + 1) * TT)
            P.op("act", I("activation", out=slot(tb), in_=ps[pq][:], func=AF.Square), reads=[t_ps[pq]], writes=[t_big[tb]])
            pm = nps()
            P.op("pe", I("matmul", ps[pm][:], lhsT=avg_h[:], rhs=slot(tb), start=True, stop=True),
                 reads=[t_big[tb], t_const], writes=[t_ps[pm]])
            P.op("act", I("activation", out=slotF(tb + 2), in_=ps[pm][:], func=AF.Ln, bias=epst[:], scale=1.0),
                 reads=[t_ps[pm], t_const], writes=t_big[tb + 2:tb + 4])
            P.op("act", I("activation", out=slotF(tb + 4), in_=slotF(tb + 2), func=AF.Exp, scale=-0.5),
                 reads=t_big[tb + 2:tb + 4], writes=t_big[tb + 4:tb + 6])
            P.op("dve", I("scalar_tensor_tensor", out=slotF(tb + 6), in0=ps[pq][:], scalar=qkg[:, j, gi:gi + 1],
                                                         in1=axF(21), op0=ALU.mult, op1=ALU.mult),
                 reads=[t_ps[pq], t_small, t_big[tb]] + t_aux[21:23], writes=t_big[tb + 6:tb + 8])
            P.op("dve", I("scalar_tensor_tensor", out=slotF(tb + 8), in0=ps[pqs][:], scalar=qkg[:, j, gi + 1:gi + 2],
                                                         in1=axF(23), op0=ALU.mult, op1=ALU.mult),
                 reads=[t_ps[pqs], t_small] + t_aux[23:25], writes=t_big[tb + 8:tb + 10])
            P.op("dve", I("tensor_tensor", out=slotF(tb + 6), in0=slotF(tb + 6), in1=slotF(tb + 8), op=ALU.add),
                 reads=t_big[tb + 6:tb + 10], writes=t_big[tb + 6:tb + 8])
            if isinstance(dst, list):
                for (dap, q0, q1, dtl) in dst:
                    P.op("dve", I("tensor_tensor", out=dap[q0:q1, :], in0=slotF(tb + 6)[q0:q1, :], in1=slotF(tb + 4)[q0:q1, :], op=ALU.mult),
                         reads=t_big[tb + 4:tb + 8], writes=dtl)
            else:
                P.op("dve", I("tensor_tensor", out=dst, in0=slotF(tb + 6), in1=slotF(tb + 4), op=ALU.mult),
                     reads=t_big[tb + 4:tb + 8], writes=dst_tiles)

        def load_rope(t):
            tsl = slice(t * TT, (t + 1) * TT)
            P.op("sp", I("dma_start", out=axF(21), in_=d_ropeC.ap()[:, tsl]), writes=t_aux[21:23], dma="ropeC")
            P.op("sp", I("dma_start", out=axF(23), in_=d_ropeS.ap()[:, tsl]), writes=t_aux[23:25], dma="ropeS")

        kT = aux[:, 0:4096].rearrange("p (m t) -> p m t", m=2)
        Ve = aux[:, 4096:4096 + 2080].rearrange("p (h t c) -> p h t c", h=2, t=16)
        Vo = aux[:, 4096 + 2080:4096 + 2080 + 4096].rearrange("p (h t c) -> p h t c", h=2, t=16)
        t_kT = t_aux[0:8]
        t_V = t_aux[8:21]

        def attn_layer(l, b):
            j = l // 2
            P.op("pool", I("memset", Vo, 0.0), writes=t_V)
            P.op("pool", I("memset", Vo[:, :, :, 0:1], 1.0), writes=t_V)
            P.op("pool", I("memset", Ve[:, :, :, 64:65], 1.0), writes=t_V)
            for t in range(NT):
                if STG < 2.2:
                    continue
                norm_tile(l, 0, b, t)
                if STG < 2.3:
                    continue
                load_rope(t)
                wv, wt = wload(wview(f"wkv{l}", 8), 8, 768)
                for m in range(2):
                    if STG < 2.26:
                        continue
                    pq, pqs = nps(), nps()
                    for (pp, cb) in ((pq, m * 128), (pqs, 256 + m * 128)):
                        for k in range(8):
                            P.op("pe", I("matmul",
                                ps[pp][:], lhsT=wv[:, k, cb:cb + 128], rhs=h[:, k, :], start=(k == 0), stop=(k == 7)),
                                reads=[wt, t_h], writes=[t_ps[pp]])
                    if STG >= 2.28:
                        rope_norm(pq, pqs, 2, j, t, kT[:, m, t * TT:(t + 1) * TT], t_kT, 0)
                for tc in range(4):
                    if STG < 2.4:
                        continue
                    pv = nps()
                    g = t * 4 + tc
                    for k in range(8):
                        P.op("pe", I("matmul",
                            ps[pv][:, 0:256], lhsT=h[:, k, tc * 128:(tc + 1) * 128], rhs=wv[:, k, 512:768],
                            start=(k == 0), stop=(k == 7)), reads=[wt, t_h], writes=[t_ps[pv]])
                    P.op("act", I("activation",
                        out=Ve[:, :, g, 0:64], in_=ps[pv][:, 0:256].rearrange("p (h two c) -> p h two c", h=2, two=2)[:, :, 0, :],
                        func=AF.Copy), reads=[t_ps[pv]], writes=t_V)
                    P.op("dve", I("tensor_copy",
                        out=Vo[:, :, g, 64:128], in_=ps[pv][:, 0:256].rearrange("p (h two c) -> p h two c", h=2, two=2)[:, :, 1, :]),
                        reads=[t_ps[pv]], writes=t_V)
            if STG < 3:
                return
            for t in range(NT):
                norm_tile(l, 0, b, t)
                dump("h0", h[:, 0, :], [t_h], False)
                dump("kT0", kT[:, 0, 0:512], t_kT, False)
                dump("Ve0", Ve[:, 0, 0, :], t_V, False)
                dump("Vo0", Vo[:, 0, 0, :], t_V, False)
                load_rope(t)
                for c in range(8):
                    P.op("pool", I("memset", slot(32 + 2 * c)[64:128, :], 0.0), writes=[t_big[32 + 2 * c]])
                    P.op("pool", I("memset", slot(33 + 2 * c)[0:64, :], 0.0), writes=[t_big[33 + 2 * c]])
                for half in range(2):
                    wv, wt = wload(wview(f"wq{l}", 8)[:, :, half * 1024:(half + 1) * 1024], 8, 1024)
                    for cc in range(4):
                        c = half * 4 + cc
                        pq, pqs = nps(), nps()
                        for (pp, cb) in ((pq, cc * 256), (pqs, cc * 256 + 128)):
                            for k in range(8):
                                P.op("pe", I("matmul",
                                    ps[pp][:], lhsT=wv[:, k, cb:cb + 128], rhs=h[:, k, :], start=(k == 0), stop=(k == 7)),
                                    reads=[wt, t_h], writes=[t_ps[pp]])
                        rope_norm(pq, pqs, 0, j, t, [(slot(32 + 2 * c), 0, 64, [t_big[32 + 2 * c]]),
                                                      (slot(33 + 2 * c), 64, 128, [t_big[33 + 2 * c]])], None, 20)
                if STG < 4:
                    continue
                P.op("pool", I("memset", slotF(20), 0.0), writes=t_big[20:22])
                pending = [None]
                for c in range(8):
                    m = c // 4
                    for hf in range(2):
                        p0 = hf * 64
                        po = nps_acc()
                        Vl = (lambda kc: Ve[:, m, kc, :]) if hf == 0 else (lambda kc: Vo[:, m, kc, :])
                        M = 65 if hf == 0 else 128
                        sps = {}

                        def emit_s(kc):
                            sp_ = nps()
                            sps[kc] = sp_
                            P.op("pe", I("matmul",
                                ps[sp_][:], lhsT=kT[:, m, kc * 128:(kc + 1) * 128], rhs=slot(32 + 2 * c + hf),
                                start=True, stop=True), reads=t_kT + [t_big[32 + 2 * c + hf]], writes=[t_ps[sp_]])

                        def emit_pv(kc):
                            sp_ = sps[kc]
                            pslot = 16 + kc % 4
                            P.op("act", I("activation",
                                out=slot(pslot), in_=ps[sp_][:], func=AF.Exp, scale=0.125),
                                reads=[t_ps[sp_]], writes=[t_big[pslot]])
                            dump("P0", slot(16), [t_big[16]], False)
                            P.op("pe", I("matmul",
                                ps[po][0:M, :], lhsT=Vl(kc), rhs=slot(pslot), start=(kc == 0), stop=(kc == 15)),
                                reads=t_V + [t_big[pslot]], writes=[t_ps[po]])
                        emit_s(0)
                        emit_s(1)
                        for kc in range(16):
                            if kc + 2 < 16:
                                emit_s(kc + 2)
                            emit_pv(kc)
                        def mk_norm(c=c, hf=hf, p0=p0, po=po):
                            def f():
                                dp = 64 if hf == 0 else 0
                                rd = slotF(20)[dp:dp + 1, :]
                                P.op("dve", I("reciprocal", out=rd, in_=ps[po][dp:dp + 1, :]),
                                     reads=[t_ps[po]], writes=t_big[20:22])
                                dump("rd", rd, t_big[20:22], True, p0=dp)
                                pbc = nps()
                                P.op("pe", I("matmul",
                                    ps[pbc][:], lhsT=(selA if hf == 0 else selB)[:], rhs=slotF(20), start=True, stop=True),
                                    reads=t_big[20:22] + [t_const], writes=[t_ps[pbc]])
                                P.op("act", I("activation", out=slotF(22)[p0:p0 + 64, :], in_=ps[po][p0:p0 + 64, :], func=AF.Copy),
                                     reads=[t_ps[po]], writes=t_big[22:24])
                                P.op("dve", I("tensor_tensor",
                                    out=slot(8 + c)[p0:p0 + 64, :], in0=slotF(22)[p0:p0 + 64, :], in1=ps[pbc][p0:p0 + 64, :], op=ALU.mult),
                                    reads=t_big[22:24] + [t_ps[pbc]], writes=[t_big[8 + c]])
                                if hf == 1:
                                    dump("OT0", slot(8), [t_big[8]], False)
                            return f
                        if pending[0] is not None:
                            pending[0]()
                        pending[0] = mk_norm()
                if pending[0] is not None:
                    pending[0]()
                    pending[0] = None
                wv, wt = wload(wview(f"wo{l}", 8), 8, 1024)
                for mo in range(8):
                    pb = nps()
                    for c in range(8):
                        P.op("pe", I("matmul",
                            ps[pb][:], lhsT=wv[:, c, mo * 128:(mo + 1) * 128], rhs=slot(8 + c), start=(c == 0), stop=(c == 7)),
                            reads=[wt, t_big[8 + c]], writes=[t_ps[pb]])
                    resid_add(l, 16, b, t, mo, pb)
                    dump("x0", x[:, 0, 0:512], [t_x[0][0]], True)
                if STG >= 5:
                    ffn_tile(l, b, t)

        def gmlp_setup(l):
            j = l // 2
            lngF = auxF[:, 0:3072]
            P.op("sp", I("dma_start", out=lngF, in_=d_lng.ap()[j].to_broadcast([128, 3072])), writes=t_aux[0:12], dma="gs0")
            lnbF = bigF[:, 0:3072]
            P.op("sp", I("dma_start", out=lnbF, in_=d_lnb.ap()[j].to_broadcast([128, 3072])), writes=t_big[0:12], dma="gs1")
            P.op("dve", I("tensor_copy", out=aux[:, 12 * 512:18 * 512], in_=lnbF), reads=t_big[0:12], writes=t_aux[12:18])
            rtmp = bigF[0:1, 3072:3072 + 7168]
            P.op("sp", I("dma_start", out=rtmp[:, 0:3072], in_=d_binv.ap()[j]), writes=t_big[12:40], dma="gs2")
            P.op("sp", I("dma_start", out=rtmp[:, 3072:7168], in_=d_bsl.ap()[j]), writes=t_big[12:40], dma="gs3")
            P.op("dve", I("tensor_copy", out=rowb[0:1, 0:7168], in_=rtmp), reads=t_big[12:40], writes=[t_rowb])
            o, n = BLOB_OFF[f"wst{l}"]
            P.op("sp", I("dma_start", out=wst[:], in_=wb16.ap()[:, o:o + n].rearrange("p (s q) -> p s q", s=32)),
                 reads=[t_blob], writes=[t_wst], dma="gs4")

        def gmlp_tile(l, b, t):
            j = l // 2
            norm_tile(l, 0, b, t)
            for pc in range(3):
                wv, wt = wload(wview(f"gin{l}", 8)[:, :, 3072 + pc * 1024:3072 + (pc + 1) * 1024], 8, 1024)
                for tc in range(4):
                    for nt in range(2):
                        pb = nps()
                        col0 = pc * 1024 + nt * 512
                        P.op("pe", I("matmul",
                            ps[pb][:], lhsT=ones_b[:], rhs=rowb[:, col0:col0 + 512], start=True, stop=False),
                            reads=[t_const, t_rowb], writes=[t_ps[pb]])
                        for k in range(8):
                            P.op("pe", I("matmul",
                                ps[pb][:], lhsT=h[:, k, tc * 128:(tc + 1) * 128], rhs=wv[:, k, nt * 512:(nt + 1) * 512],
                                start=False, stop=(k == 7)), reads=[wt, t_h], writes=[t_ps[pb]])
                        sl = 24 + 6 * tc + pc * 2 + nt
                        P.op("act", I("activation", out=slot(sl), in_=ps[pb][:], func=AF.Gelu),
                             reads=[t_ps[pb]], writes=[t_big[sl]])

            def ln_tc(tc):
                vb = 24 + 6 * tc
                for q in range(6):
                    P.op("dve", I("bn_stats", out=stat[:, q * 6:q * 6 + 6], in_=slot(vb + q)),
                         reads=[t_big[vb + q]], writes=[t_stat])
                P.op("dve", I("bn_aggr", out=stat[:, 40:42], in_=stat[:, 0:36]), reads=[t_stat], writes=[t_stat])
                P.op("act", I("activation", out=stat[:, 42:43], in_=stat[:, 41:42], func=AF.Ln, bias=epst[:], scale=1.0),
                     reads=[t_stat, t_const], writes=[t_stat])
                P.op("act", I("activation", out=stat[:, 43:44], in_=stat[:, 42:43], func=AF.Exp, scale=-0.5),
                     reads=[t_stat], writes=[t_stat])
                P.op("dve", I("scalar_tensor_tensor", out=stat[:, 44:45], in0=stat[:, 40:41], scalar=-1.0, in1=stat[:, 43:44],
                                                             op0=ALU.mult, op1=ALU.mult), reads=[t_stat], writes=[t_stat])
                P.op("dve", I("tensor_copy", out=stat[:, 48 + 2 * tc:50 + 2 * tc], in_=stat[:, 43:45]), reads=[t_stat], writes=[t_stat2[tc]])
                for q in range(6):
                    ta = 18 + 2 * (q % 3)
                    P.op("act", I("activation",
                        out=axF(ta), in_=slot(vb + q), func=AF.Identity, bias=stat[:, 49 + 2 * tc:50 + 2 * tc], scale=stat[:, 48 + 2 * tc:49 + 2 * tc]),
                        reads=[t_big[vb + q], t_stat2[tc]], writes=t_aux[ta:ta + 2])
                    P.op("dve", I("tensor_tensor",
                        out=axF(ta), in0=axF(ta), in1=auxF[:, q * 512:(q + 1) * 512], op=ALU.mult),
                        reads=t_aux[ta:ta + 2] + t_aux[2 * q:2 * q + 2], writes=t_aux[ta:ta + 2])
                    P.op("pool", I("tensor_tensor",
                        out=slot(vb + q), in0=axF(ta), in1=aux[:, (12 + q) * 512:(13 + q) * 512], op=ALU.add),
                        reads=t_aux[ta:ta + 2] + [t_aux[12 + q]], writes=[t_big[vb + q]])
                    dump("gv0", slot(24), [t_big[24]], False)

            for pc in range(3):
                wv, wt = wload(wview(f"gin{l}", 8)[:, :, pc * 1024:(pc + 1) * 1024], 8, 1024)
                for jj in range(8):
                    ju = pc * 8 + jj
                    pb = nps()
                    for k in range(8):
                        P.op("pe", I("matmul",
                            ps[pb][:], lhsT=wv[:, k, jj * 128:(jj + 1) * 128], rhs=h[:, k, :], start=(k == 0), stop=(k == 7)),
                            reads=[wt, t_h], writes=[t_ps[pb]])
                    P.op("act", I("activation",
                        out=slot(ju), in_=ps[pb][:], func=AF.Gelu, bias=binu[:, j, ju:ju + 1], scale=1.0),
                        reads=[t_ps[pb], t_small], writes=[t_big[ju]])
                    dump("gu0", slot(0), [t_big[0]], False)
                    if ju % 6 == 2:
                        ln_tc(ju // 6)
            for tc in range(4):
                vb = 24 + 6 * tc
                vt = t_big[vb:vb + 6]
                for bk in range(8):
                    pb = nps()
                    P.op("pe", I("matmul",
                        ps[pb][:], lhsT=ones_b[:], rhs=rowb[:, 3072 + bk * 512:3072 + (bk + 1) * 512], start=True, stop=False),
                        reads=[t_const, t_rowb], writes=[t_ps[pb]])
                    for s4 in range(4):
                        s_ = bk * 4 + s4
                        jf = SLOTS[s_][0]
                        P.op("pe", I("matmul",
                            ps[pb][:, s4 * 128:(s4 + 1) * 128], lhsT=slot(vb, 6)[:, jf * 128:(jf + 1) * 128], rhs=wst[:, s_, :],
                            start=False, stop=(s4 == 3)), reads=vt + [t_wst], writes=[t_ps[pb]])
                    for s4 in range(4):
                        s_ = bk * 4 + s4
                        jf, _, p0, p1 = SLOTS[s_]
                        P.op("dve", I("tensor_tensor",
                            out=slot(jf)[p0:p1, tc * 128:(tc + 1) * 128], in0=ps[pb][p0:p1, s4 * 128:(s4 + 1) * 128],
                            in1=slot(jf)[p0:p1, tc * 128:(tc + 1) * 128], op=ALU.mult),
                            reads=[t_ps[pb], t_big[jf]], writes=[t_big[jf]])
            for pc in range(4):
                wv, wt = wload(wview(f"gout{l}", 24)[:, :, pc * 256:(pc + 1) * 256], 24, 256)
                for mm in range(2):
                    m = pc * 2 + mm
                    pb = nps()
                    for jf in range(24):
                        P.op("pe", I("matmul",
                            ps[pb][:], lhsT=wv[:, jf, mm * 128:(mm + 1) * 128], rhs=slot(jf), start=(jf == 0), stop=(jf == 23)),
                            reads=[wt, t_big[jf]], writes=[t_ps[pb]])
                    resid_add(l, 16, b, t, m, pb)
                    dump("gx0", x[:, 0, 0:512], [t_x[0][0]], True)
            ffn_tile(l, b, t)

        for b in range(nseq):
            for c in range(8):
                P.op("sp", I("dma_start", out=x[:, c, :], in_=xin.ap()[b, :, c, :]),
                     writes=t_x[c], dma=f"xin{c}")
            for l in range(nlayers):
                if STG < 2:
                    continue
                if l % 2 == 0:
                    attn_layer(l, b)
                else:
                    gmlp_setup(l)
                    for t in range(NT):
                        gmlp_tile(l, b, t)
            for c in range(8):
                P.op("sp", I("dma_start", out=yout.ap()[b, :, c, :], in_=x[:, c, :]),
                     reads=t_x[c], dma=f"out{c}")
        P.emit(final_waits=[f"out{c}" for c in range(8)] + [k for k in P.dma_cnt if k.startswith("dbg")])
    return nc


_CACHE = {}


def kernel(**inputs):
    inp = {k: np.asarray(v, dtype=np.float32) for k, v in inputs.items()}
    blob, sm = host_prep(inp)
    if "nc" not in _CACHE:
        _CACHE["nc"] = build()
    nc = _CACHE["nc"]
    x = inp["x"]
    c = inp["c"]
    in_maps = []
    for core in range(8):
        xs = x[2 * core:2 * core + 2]
        xT = np.ascontiguousarray(xs.reshape(2, SEQ, 8, 128).transpose(0, 3, 2, 1))
        cT = np.ascontiguousarray(c[2 * core:2 * core + 2].reshape(2, 8, 128).transpose(2, 1, 0))
        m = {"blob": blob, "xT": xT, "cT": cT}
        m.update(sm)
        in_maps.append(m)
    res = run_bass_kernel_spmd(nc, in_maps, core_ids=list(range(8)))
    out = np.empty((16, SEQ, D), np.float32)
    for core in range(8):
        yT = res.results[core]["yT"]
        out[2 * core:2 * core + 2] = yT.transpose(0, 3, 2, 1).reshape(2, SEQ, D)
    return out
```

```python
import contextlib
import numpy as np
import concourse.bass as bass
import concourse.mybir as mybir
from concourse.bass_utils import run_bass_kernel_spmd

F32 = mybir.dt.float32
BF16 = mybir.dt.bfloat16
AF = mybir.ActivationFunctionType
ALU = mybir.AluOpType

D = 1024
SEQ = 2048
DEPTH = 4
TT = 512
NT = SEQ // TT
EPS = 1e-6
ENG = ("pe", "act", "dve", "pool", "sp")
import os
STG = float(os.environ.get("KSTAGE", "99"))


class T:
    __slots__ = ("name", "w", "r")

    def __init__(self, name):
        self.name = name
        self.w = None
        self.r = []


class Prog:
    def __init__(self, nc):
        self.nc = nc
        self.ops = {e: [] for e in ENG}
        self.dma_cnt = {}

    def _add(self, deps, d, eng):
        if d is None:
            return
        if d[0] == "e" and d[1] == eng and eng in ("pe", "sp"):
            return
        deps.add(d)

    def op(self, eng, fn, reads=(), writes=(), dma=None):
        deps = set()
        for t in reads:
            self._add(deps, t.w, eng)
        for t in writes:
            self._add(deps, t.w, eng)
            for d in t.r:
                self._add(deps, d, eng)
        idx = len(self.ops[eng])
        if dma is not None:
            c = self.dma_cnt.get(dma, 0) + 1
            self.dma_cnt[dma] = c
            me = ("d", dma, 16 * c)
        else:
            me = ("e", eng, idx)
        self.ops[eng].append([fn, deps, False, dma])
        for t in reads:
            t.r.append(me)
        for t in writes:
            t.w = me
            t.r = []
        return me

    def emit(self, final_waits=()):
        nc = self.nc
        for e in ENG:
            for o in self.ops[e]:
                for d in o[1]:
                    if d[0] == "e":
                        self.ops[d[1]][d[2]][2] = True
        val = {}
        for e in ENG:
            c = 0
            for i, o in enumerate(self.ops[e]):
                if o[3] is None and o[2]:
                    c += 1
                    val[(e, i)] = c
        with contextlib.ExitStack() as st:
            esem = {e: st.enter_context(nc.semaphore("s_" + e)) for e in ENG}
            dsem = {k: st.enter_context(nc.semaphore("d_" + str(k))) for k in self.dma_cnt}
            blk = st.enter_context(nc.Block())
            engobj = {"pe": blk.tensor, "act": blk.scalar, "dve": blk.vector,
                      "pool": blk.gpsimd, "sp": blk.sync}

            def run(e, eo):
                known = {}
                for o in self.ops[e]:
                    fn, deps, sig, dma = o
                    need = {}
                    for d in deps:
                        if d[0] == "e":
                            s, v = esem[d[1]], val[(d[1], d[2])]
                        else:
                            s, v = dsem[d[1]], d[2]
                        k = id(s)
                        if need.get(k, (None, 0))[1] < v:
                            need[k] = (s, v)
                    for k, (s, v) in need.items():
                        if known.get(k, 0) < v:
                            eo.wait_ge(s, v)
                            known[k] = v
                    ins = fn(eo)
                    if dma is not None:
                        ins.then_inc(dsem[dma], 16)
                    elif sig:
                        ins.then_inc(esem[e], 1)
                if e == "sp":
                    for k in final_waits:
                        eo.wait_ge(dsem[k], 16 * self.dma_cnt[k])

            for e in ENG:
                def mk(e):
                    def f(eo):
                        run(e, eo)
                    return f
                engobj[e](mk(e))


DBG = {}


def I(meth, *a, **k):
    return lambda e: getattr(e, meth)(*a, **k)


def _blob_layout():
    off = {}
    o = 0

    def add(name, n):
        nonlocal o
        off[name] = (o, n)
        o += n
    for l in range(DEPTH):
        add(f"ada{l}", 8 * 6144)
        if l % 2 == 0:
            add(f"wkv{l}", 8 * 768)
            add(f"wq{l}", 8 * 2048)
            add(f"wo{l}", 8 * 1024)
        else:
            add(f"gin{l}", 8 * 6144)
            add(f"gout{l}", 24 * 1024)
            add(f"wst{l}", 32 * 128)
        add(f"w1{l}", 8 * 4096)
        add(f"w2{l}", 32 * 1024)
    return off, o


BLOB_OFF, BLOB_N = _blob_layout()
PCH = 4096
assert BLOB_N % PCH == 0

Q_HEADS = [(8 * (c // 4) + c % 4, 8 * (c // 4) + 4 + c % 4) for c in range(8)]

SLOTS = []
for j in range(24):
    if j % 3 != 1:
        SLOTS.append((j, 2 * (j // 3) + (0 if j % 3 == 0 else 1), 0, 128))
for j in range(24):
    if j % 3 == 1:
        SLOTS.append((j, 2 * (j // 3), 0, 64))
        SLOTS.append((j, 2 * (j // 3) + 1, 64, 128))
assert len(SLOTS) == 32


def pk(W):
    K, N = W.shape
    return np.ascontiguousarray(W.reshape(K // 128, 128, N).transpose(1, 0, 2)).reshape(128, -1)


def host_prep(inp):
    blob = np.empty((128, BLOB_N), np.float32)

    def put(name, arr):
        o, n = BLOB_OFF[name]
        assert arr.shape == (128, n), (name, arr.shape, n)
        blob[:, o:o + n] = arr
    sw = np.arange(64) ^ 1
    for l in range(DEPTH):
        put(f"ada{l}", pk(inp["ada_w"][l]))
        j = l // 2
        if l % 2 == 0:
            W = inp["attn_w_qkv"][j]
            Wq, Wk, Wv = W[:, :1024], W[:, 1024:1280], W[:, 1280:1536]
            kcols = np.concatenate([Wk, Wk.reshape(1024, 4, 64)[:, :, sw].reshape(1024, 256), Wv], axis=1)
            put(f"wkv{l}", pk(kcols))
            Wq3 = Wq.reshape(1024, 16, 64)
            cols = []
            for c in range(8):
                ha, hb = Q_HEADS[c]
                cols += [Wq3[:, ha, :], Wq3[:, hb, :], Wq3[:, ha, :][:, sw], Wq3[:, hb, :][:, sw]]
            put(f"wq{l}", pk(np.concatenate(cols, axis=1)))
            Wo3 = inp["attn_w_o"][j].reshape(16, 64, 1024)
            rows = []
            for c in range(8):
                ha, hb = Q_HEADS[c]
                rows += [Wo3[ha], Wo3[hb]]
            put(f"wo{l}", pk(np.concatenate(rows, axis=0)))
        else:
            put(f"gin{l}", pk(inp["gmlp_w_in"][j]))
            put(f"gout{l}", pk(inp["gmlp_w_out"][j]))
            ws = inp["gmlp_w_s"][j]
            wst = np.stack([ws[g].T for (_, g, _, _) in SLOTS], axis=1)
            put(f"wst{l}", np.ascontiguousarray(wst).reshape(128, 32 * 128))
        put(f"w1{l}", pk(inp["mlp_w_in"][l]))
        put(f"w2{l}", pk(inp["mlp_w_out"][l]))

    def colv(v, nch):
        return np.ascontiguousarray(v.reshape(nch, 128).T)
    sm = {}
    sm["ada_b"] = np.ascontiguousarray(np.stack([colv(inp["ada_b"][l], 48) for l in range(DEPTH)], axis=1))
    sm["n1g"] = np.ascontiguousarray(np.stack([colv(inp["norm1_g"][l], 8) for l in range(DEPTH)], axis=1))
    sm["n2g"] = np.ascontiguousarray(np.stack([colv(inp["norm2_g"][l], 8) for l in range(DEPTH)], axis=1))
    pidx = np.arange(128) % 64
    qk = np.zeros((128, 2, 4), np.float32)
    for j in range(2):
        qk[:, j, 0] = inp["attn_q_norm_g"][j][pidx]
        qk[:, j, 1] = inp["attn_q_norm_g"][j][pidx ^ 1]
        qk[:, j, 2] = inp["attn_k_norm_g"][j][pidx]
        qk[:, j, 3] = inp["attn_k_norm_g"][j][pidx ^ 1]
    sm["qkg"] = qk
    sm["binu"] = np.ascontiguousarray(np.stack([colv(inp["gmlp_b_in"][j][:3072], 24) for j in range(2)], axis=1))
    sm["binv"] = np.ascontiguousarray(inp["gmlp_b_in"][:, 3072:]).reshape(2, 1, 3072)
    sm["lng"] = np.ascontiguousarray(inp["gmlp_ln_g"]).reshape(2, 1, 3072)
    sm["lnb"] = np.ascontiguousarray(inp["gmlp_ln_b"]).reshape(2, 1, 3072)
    sm["bsl"] = np.ascontiguousarray(np.stack([np.stack([inp["gmlp_b_s"][j][g] for (_, g, _, _) in SLOTS], 0)
                                                for j in range(2)], 0)).reshape(2, 1, 4096)
    t = np.arange(SEQ)
    row = (t // 64 - (SEQ // 64) // 2).astype(np.float32)
    col = (t % 64 - 32).astype(np.float32)
    inv = (10000.0 ** (-np.arange(16, dtype=np.float32) / 16)).astype(np.float32)
    ang = np.concatenate([row[:, None] * inv, col[:, None] * inv], axis=-1)
    cs, sn = np.cos(ang).astype(np.float32), np.sin(ang).astype(np.float32)
    pr = pidx // 2
    sign = np.where(pidx % 2 == 0, -1.0, 1.0).astype(np.float32)
    sm["ropeC"] = np.ascontiguousarray(cs[:, pr].T)
    sm["ropeS"] = np.ascontiguousarray((sn[:, pr] * sign[None, :]).T)
    return blob, sm


def build(nlayers=DEPTH, nseq=2):
    nc = bass.Bass("TRN2", target_bir_lowering=False)

    def din(name, shape):
        return nc.dram_tensor(name, list(shape), F32, kind="ExternalInput")
    blob = din("blob", [128, BLOB_N])
    xin = din("xT", [2, 128, 8, SEQ])
    cin = din("cT", [128, 8, 2])
    d_adab = din("ada_b", [128, 4, 48])
    d_n1g = din("n1g", [128, 4, 8])
    d_n2g = din("n2g", [128, 4, 8])
    d_qkg = din("qkg", [128, 2, 4])
    d_binu = din("binu", [128, 2, 24])
    d_binv = din("binv", [2, 1, 3072])
    d_lng = din("lng", [2, 1, 3072])
    d_lnb = din("lnb", [2, 1, 3072])
    d_bsl = din("bsl", [2, 1, 4096])
    d_ropeC = din("ropeC", [128, SEQ])
    d_ropeS = din("ropeS", [128, SEQ])
    yout = nc.dram_tensor("yT", [2, 128, 8, SEQ], F32, kind="ExternalOutput")
    wb16 = nc.dram_tensor("wb16", [128, BLOB_N], BF16, kind="Internal")
    KDBG = os.environ.get("KDBG", "") != ""
    DBG.clear()
    if KDBG:
        dbgF = nc.dram_tensor("dbgF", [24, 128, 512], F32, kind="ExternalOutput")
        dbgB = nc.dram_tensor("dbgB", [64, 128, 512], BF16, kind="ExternalOutput")

    P = Prog(nc)
    with contextlib.ExitStack() as st:
        def sb(name, shape, dt):
            return st.enter_context(nc.sbuf_tensor("s_" + name, list(shape), dt))
        x = sb("x", [128, 8, SEQ], F32)
        ring = [sb(f"ring{i}", [128, 8192], BF16) for i in range(2)]
        ringF = [r.bitcast(F32) for r in ring]
        h = sb("h", [128, 8, TT], BF16)
        big = sb("big", [128, 48 * 512], BF16)
        bigF = big.bitcast(F32)
        aux = sb("aux", [128, 25 * 512], BF16)
        auxF = aux.bitcast(F32)
        wst = sb("wst", [128, 32, 128], BF16)
        mod = sb("mod", [128, 4, 48, 2], F32)
        adab = sb("adab", [128, 4, 48], F32)
        n1g = sb("n1g", [128, 4, 8], F32)
        n2g = sb("n2g", [128, 4, 8], F32)
        qkg = sb("qkg", [128, 2, 4], F32)
        binu = sb("binu", [128, 2, 24], F32)
        Amod = sb("Amod", [128, 4, 2, 2, 8], F32)
        cT = sb("cT", [128, 8, 2], F32)
        condb = sb("condb", [128, 8, 2], BF16)
        ones_b = sb("ones_b", [128, 128], BF16)
        selA = sb("selA", [128, 128], F32)
        selB = sb("selB", [128, 128], F32)
        avg_d = sb("avg_d", [128, 128], BF16)
        avg_h = sb("avg_h", [128, 128], BF16)
        epst = sb("epst", [128, 1], F32)
        stat = sb("stat", [128, 64], F32)
        rowb = sb("rowb", [128, 7168], BF16)
        ps = [st.enter_context(nc.psum_tensor(f"ps{i}", [128, 512], F32)) for i in range(8)]

        t_x = [[T(f"x{c}_{t}") for t in range(NT)] for c in range(8)]
        t_ring = [T("ring0"), T("ring1")]
        t_h = T("h")
        t_big = [T(f"big{i}") for i in range(48)]
        t_aux = [T(f"aux{i}") for i in range(25)]
        t_wst = T("wst")
        t_mod, t_small, t_amod, t_cond, t_const = T("mod"), T("small"), T("amod"), T("cond"), T("const")
        t_stat, t_rowb = T("stat"), T("rowb")
        t_stat2 = [T(f"stat2_{i}") for i in range(4)]
        t_ps = [T(f"ps{i}") for i in range(8)]
        t_blob = T("blob")
        psi = [0]

        def nps():
            i = psi[0] % 6
            psi[0] += 1
            return i

        pacc = [0]

        def nps_acc():
            i = 6 + pacc[0] % 2
            pacc[0] += 1
            return i

        def dump(name, ap, tiles, f32, p0=0):
            if not KDBG or name in DBG:
                return
            kind = "F" if f32 else "B"
            idx = sum(1 for v in DBG.values() if v[0] == kind)
            np_, ncol = ap.shape[0], ap.shape[1]
            DBG[name] = (kind, idx, p0, np_, ncol)
            dst = (dbgF if f32 else dbgB).ap()[idx, p0:p0 + np_, 0:ncol]
            P.op("sp", I("dma_start", out=dst, in_=ap), reads=tiles, dma="dbg_" + name)

        def slot(i, n=1):
            return big[:, i * 512:(i + n) * 512]

        def slotF(i, n=2):
            return bigF[:, i * 256:(i + n) * 256]

        def axs(i, n=1):
            return aux[:, i * 512:(i + n) * 512]

        def axF(i, n=2):
            return auxF[:, i * 256:(i + n) * 256]

        P.op("pool", I("memset", ones_b[:], 0.0), writes=[t_const])
        P.op("pool", I("memset", ones_b[0:1, :], 1.0), writes=[t_const])
        P.op("pool", I("memset", selA[:], 0.0), writes=[t_const])
        P.op("pool", I("memset", selA[64:65, :], 1.0), writes=[t_const])
        P.op("pool", I("memset", selB[:], 0.0), writes=[t_const])
        P.op("pool", I("memset", selB[0:1, :], 1.0), writes=[t_const])
        P.op("pool", I("memset", rowb[:], 0.0), writes=[t_rowb])
        P.op("pool", I("memset", avg_d[:], 1.0 / D), writes=[t_const])
        P.op("pool", I("memset", avg_h[:], 0.0), writes=[t_const])
        P.op("pool", I("memset", avg_h[0:64, 0:64], 1.0 / 64), writes=[t_const])
        P.op("pool", I("memset", avg_h[64:128, 64:128], 1.0 / 64), writes=[t_const])
        P.op("pool", I("memset", epst[:], EPS), writes=[t_const])
        for (dst, src) in ((adab, d_adab), (n1g, d_n1g), (n2g, d_n2g), (qkg, d_qkg), (binu, d_binu), (cT, cin)):
            P.op("sp", I("dma_start", out=dst[:], in_=src.ap()), writes=[t_small], dma="small")

        nch = BLOB_N // PCH
        last_store = None
        fstage = [(ringF[0][:], [t_ring[0]], "ring0"), (ringF[1][:], [t_ring[1]], "ring1")]
        for qd in range(4):
            fstage.append((x[:, 2 * qd:2 * qd + 2, :].rearrange("p c t -> p (c t)"),
                           t_x[2 * qd] + t_x[2 * qd + 1], f"xin{2 * qd}"))
        cast_pat = ["dve", "act", "dve", "act", "pool"]
        for i in range(nch):
            fbuf, ftl, fkey = fstage[i % 6]
            sgi = i % 6
            stg = big[:, sgi * 4096:(sgi + 1) * 4096]
            tl = t_big[sgi * 8:(sgi + 1) * 8]
            P.op("sp", I("dma_start", out=fbuf, in_=blob.ap()[:, i * PCH:(i + 1) * PCH]), writes=ftl, dma=fkey)
            ce = cast_pat[i % 5]
            if ce == "act":
                P.op("act", I("activation", out=stg, in_=fbuf, func=AF.Copy), reads=ftl, writes=tl)
            else:
                P.op(ce, I("tensor_copy", out=stg, in_=fbuf), reads=ftl, writes=tl)
            last_store = P.op("act", I("dma_start", out=wb16.ap()[:, i * PCH:(i + 1) * PCH], in_=stg),
                              reads=tl, dma="st")
        t_blob.w = last_store

        def wview(name, kc):
            o, n = BLOB_OFF[name]
            return wb16.ap()[:, o:o + n].rearrange("p (k n) -> p k n", k=kc)

        rr = [0]

        def wload(src, kc, ncols):
            b = rr[0] % 2
            rr[0] += 1
            dst = ring[b][:, 0:kc * ncols].rearrange("p (k n) -> p k n", k=kc)
            P.op("sp", I("dma_start", out=dst, in_=src), reads=[t_blob], writes=[t_ring[b]], dma=f"ring{b}")
            return dst, t_ring[b]

        P.op("act", I("activation", out=condb[:], in_=cT[:], func=AF.Silu), reads=[t_small], writes=[t_cond])
        for l in range(nlayers):
            pb = nps()
            for pc in range(6):
                wv, wt = wload(wview(f"ada{l}", 8)[:, :, pc * 1024:(pc + 1) * 1024], 8, 1024)
                for jj in range(8):
                    j = pc * 8 + jj
                    for k in range(8):
                        P.op("pe", I("matmul",
                            ps[pb][:, 2 * j:2 * j + 2], lhsT=wv[:, k, jj * 128:(jj + 1) * 128], rhs=condb[:, k, :],
                            start=(k == 0), stop=(k == 7)), reads=[wt, t_cond], writes=[t_ps[pb]])
            P.op("dve", I("tensor_tensor",
                out=mod[:, l, :, :], in0=ps[pb][:, 0:96].rearrange("p (j b) -> p j b", b=2),
                in1=adab[:, l, :].unsqueeze(2).to_broadcast([128, 48, 2]), op=ALU.add),
                reads=[t_ps[pb], t_small], writes=[t_mod])
            for wh, (gt, base) in enumerate(((n1g, 8), (n2g, 32))):
                for b in range(2):
                    P.op("dve", I("scalar_tensor_tensor",
                        out=Amod[:, l, wh, b, :], in0=mod[:, l, base:base + 8, b], scalar=1.0, in1=gt[:, l, :],
                        op0=ALU.add, op1=ALU.mult), reads=[t_mod, t_small], writes=[t_amod])
            dump("mod", mod[:, l, :, :].rearrange("p j b -> p (j b)"), [t_mod], True)

        def norm_tile(l, wh, b, t):
            tsl = slice(t * TT, (t + 1) * TT)
            sh_base = 0 if wh == 0 else 24
            pb = nps()
            for c in range(8):
                if c % 2 == 0:
                    P.op("act", I("activation", out=slot(40 + c), in_=x[:, c, tsl], func=AF.Square),
                         reads=[t_x[c][t]], writes=[t_big[40 + c]])
                else:
                    P.op("pool", I("tensor_tensor", out=slot(40 + c), in0=x[:, c, tsl], in1=x[:, c, tsl], op=ALU.mult),
                         reads=[t_x[c][t]], writes=[t_big[40 + c]])
                P.op("pe", I("matmul", ps[pb][:], lhsT=avg_d[:], rhs=slot(40 + c), start=(c == 0), stop=(c == 7)),
                     reads=[t_big[40 + c], t_const], writes=[t_ps[pb]])
            P.op("act", I("activation", out=slotF(36), in_=ps[pb][:], func=AF.Ln, bias=epst[:], scale=1.0),
                 reads=[t_ps[pb], t_const], writes=t_big[36:38])
            P.op("act", I("activation", out=slotF(38), in_=slotF(36), func=AF.Exp, scale=-0.5),
                 reads=t_big[36:38], writes=t_big[38:40])
            for c in range(8):
                tmp = 32 + 2 * (c % 2)
                P.op("dve", I("scalar_tensor_tensor",
                    out=slotF(tmp), in0=x[:, c, tsl], scalar=Amod[:, l, wh, b, c:c + 1], in1=slotF(38),
                    op0=ALU.mult, op1=ALU.mult), reads=[t_x[c][t], t_amod] + t_big[38:40], writes=t_big[tmp:tmp + 2])
                if c % 2 == 0:
                    P.op("act", I("activation",
                        out=h[:, c, :], in_=slotF(tmp), func=AF.Identity, bias=mod[:, l, sh_base + c, b:b + 1], scale=1.0),
                        reads=t_big[tmp:tmp + 2] + [t_mod], writes=[t_h])
                else:
                    P.op("pool", I("tensor_scalar", out=h[:, c, :], in0=slotF(tmp), scalar1=1.0,
                                   scalar2=mod[:, l, sh_base + c, b:b + 1], op0=ALU.mult, op1=ALU.add),
                         reads=t_big[tmp:tmp + 2] + [t_mod], writes=[t_h])

        def resid_add(l, gbase, b, t, m, pb):
            tsl = slice(t * TT, (t + 1) * TT)
            P.op("dve", I("scalar_tensor_tensor",
                out=x[:, m, tsl], in0=ps[pb][:], scalar=mod[:, l, gbase + m, b:b + 1], in1=x[:, m, tsl],
                op0=ALU.mult, op1=ALU.add), reads=[t_ps[pb], t_mod, t_x[m][t]], writes=[t_x[m][t]])

        def ffn_tile(l, b, t):
            norm_tile(l, 1, b, t)
            for pc in range(4):
                wv, wt = wload(wview(f"w1{l}", 8)[:, :, pc * 1024:(pc + 1) * 1024], 8, 1024)
                for jj in range(8):
                    j = pc * 8 + jj
                    pb = nps()
                    for k in range(8):
                        P.op("pe", I("matmul",
                            ps[pb][:], lhsT=wv[:, k, jj * 128:(jj + 1) * 128], rhs=h[:, k, :], start=(k == 0), stop=(k == 7)),
                            reads=[wt, t_h], writes=[t_ps[pb]])
                    tmp = 32 + 2 * (j % 4)
                    P.op("act", I("activation", out=slotF(tmp), in_=ps[pb][:], func=AF.Relu),
                         reads=[t_ps[pb]], writes=t_big[tmp:tmp + 2])
                    P.op("dve", I("tensor_tensor", out=slot(j), in0=slotF(tmp), in1=slotF(tmp), op=ALU.mult),
                         reads=t_big[tmp:tmp + 2], writes=[t_big[j]])
            for pc in range(4):
                wv, wt = wload(wview(f"w2{l}", 32)[:, :, pc * 256:(pc + 1) * 256], 32, 256)
                for mm in range(2):
                    m = pc * 2 + mm
                    pb = nps()
                    for j in range(32):
                        P.op("pe", I("matmul",
                            ps[pb][:], lhsT=wv[:, j, mm * 128:(mm + 1) * 128], rhs=slot(j), start=(j == 0), stop=(j == 31)),
                            reads=[wt, t_big[j]], writes=[t_ps[pb]])
                    resid_add(l, 40, b, t, m, pb)

        def rope_norm(pq, pqs, gi, j, t, dst, dst_tiles, tb):
            tsl = slice(t * TT, (t + 1) * TT)
            P.op("act", I("activation", out=slot(tb), in_=ps[pq][:], func=AF.Square), reads=[t_ps[pq]], writes=[t_big[tb]])
            pm = nps()
            P.op("pe", I("matmul", ps[pm][:], lhsT=avg_h[:], rhs=slot(tb), start=True, stop=True),
                 reads=[t_big[tb], t_const], writes=[t_ps[pm]])
            P.op("act", I("activation", out=slotF(tb + 2), in_=ps[pm][:], func=AF.Ln, bias=epst[:], scale=1.0),
                 reads=[t_ps[pm], t_const], writes=t_big[tb + 2:tb + 4])
            P.op("act", I("activation", out=slotF(tb + 4), in_=slotF(tb + 2), func=AF.Exp, scale=-0.5),
                 reads=t_big[tb + 2:tb + 4], writes=t_big[tb + 4:tb + 6])
            P.op("dve", I("scalar_tensor_tensor", out=slotF(tb + 6), in0=ps[pq][:], scalar=qkg[:, j, gi:gi + 1],
                                                         in1=axF(21), op0=ALU.mult, op1=ALU.mult),
                 reads=[t_ps[pq], t_small, t_big[tb]] + t_aux[21:23], writes=t_big[tb + 6:tb + 8])
            P.op("dve", I("scalar_tensor_tensor", out=slotF(tb + 8), in0=ps[pqs][:], scalar=qkg[:, j, gi + 1:gi + 2],
                                                         in1=axF(23), op0=ALU.mult, op1=ALU.mult),
                 reads=[t_ps[pqs], t_small] + t_aux[23:25], writes=t_big[tb + 8:tb + 10])
            P.op("dve", I("tensor_tensor", out=slotF(tb + 6), in0=slotF(tb + 6), in1=slotF(tb + 8), op=ALU.add),
                 reads=t_big[tb + 6:tb + 10], writes=t_big[tb + 6:tb + 8])
            if isinstance(dst, list):
                for (dap, q0, q1, dtl) in dst:
                    P.op("dve", I("tensor_tensor", out=dap[q0:q1, :], in0=slotF(tb + 6)[q0:q1, :], in1=slotF(tb + 4)[q0:q1, :], op=ALU.mult),
                         reads=t_big[tb + 4:tb + 8], writes=dtl)
            else:
                P.op("dve", I("tensor_tensor", out=dst, in0=slotF(tb + 6), in1=slotF(tb + 4), op=ALU.mult),
                     reads=t_big[tb + 4:tb + 8], writes=dst_tiles)

        def load_rope(t):
            tsl = slice(t * TT, (t + 1) * TT)
            P.op("sp", I("dma_start", out=axF(21), in_=d_ropeC.ap()[:, tsl]), writes=t_aux[21:23], dma="ropeC")
            P.op("sp", I("dma_start", out=axF(23), in_=d_ropeS.ap()[:, tsl]), writes=t_aux[23:25], dma="ropeS")

        kT = aux[:, 0:4096].rearrange("p (m t) -> p m t", m=2)
        Ve = aux[:, 4096:4096 + 2080].rearrange("p (h t c) -> p h t c", h=2, t=16)
        Vo = aux[:, 4096 + 2080:4096 + 2080 + 4096].rearrange("p (h t c) -> p h t c", h=2, t=16)
        t_kT = t_aux[0:8]
        t_V = t_aux[8:21]

        def attn_layer(l, b):
            j = l // 2
            P.op("pool", I("memset", Vo, 0.0), writes=t_V)
            P.op("pool", I("memset", Vo[:, :, :, 0:1], 1.0), writes=t_V)
            P.op("pool", I("memset", Ve[:, :, :, 64:65], 1.0), writes=t_V)
            for t in range(NT):
                if STG < 2.2:
                    continue
                norm_tile(l, 0, b, t)
                if STG < 2.3:
                    continue
                load_rope(t)
                wv, wt = wload(wview(f"wkv{l}", 8), 8, 768)
                for m in range(2):
                    if STG < 2.26:
                        continue
                    pq, pqs = nps(), nps()
                    for (pp, cb) in ((pq, m * 128), (pqs, 256 + m * 128)):
                        for k in range(8):
                            P.op("pe", I("matmul",
                                ps[pp][:], lhsT=wv[:, k, cb:cb + 128], rhs=h[:, k, :], start=(k == 0), stop=(k == 7)),
                                reads=[wt, t_h], writes=[t_ps[pp]])
                    if STG >= 2.28:
                        rope_norm(pq, pqs, 2, j, t, kT[:, m, t * TT:(t + 1) * TT], t_kT, 0)
                for tc in range(4):
                    if STG < 2.4:
                        continue
                    pv = nps()
                    g = t * 4 + tc
                    for k in range(8):
                        P.op("pe", I("matmul",
                            ps[pv][:, 0:256], lhsT=h[:, k, tc * 128:(tc + 1) * 128], rhs=wv[:, k, 512:768],
                            start=(k == 0), stop=(k == 7)), reads=[wt, t_h], writes=[t_ps[pv]])
                    P.op("act", I("activation",
                        out=Ve[:, :, g, 0:64], in_=ps[pv][:, 0:256].rearrange("p (h two c) -> p h two c", h=2, two=2)[:, :, 0, :],
                        func=AF.Copy), reads=[t_ps[pv]], writes=t_V)
                    P.op("dve", I("tensor_copy",
                        out=Vo[:, :, g, 64:128], in_=ps[pv][:, 0:256].rearrange("p (h two c) -> p h two c", h=2, two=2)[:, :, 1, :]),
                        reads=[t_ps[pv]], writes=t_V)
            if STG < 3:
                return
            for t in range(NT):
                norm_tile(l, 0, b, t)
                dump("h0", h[:, 0, :], [t_h], False)
                dump("kT0", kT[:, 0, 0:512], t_kT, False)
                dump("Ve0", Ve[:, 0, 0, :], t_V, False)
                dump("Vo0", Vo[:, 0, 0, :], t_V, False)
                load_rope(t)
                for c in range(8):
                    P.op("pool", I("memset", slot(32 + 2 * c)[64:128, :], 0.0), writes=[t_big[32 + 2 * c]])
                    P.op("pool", I("memset", slot(33 + 2 * c)[0:64, :], 0.0), writes=[t_big[33 + 2 * c]])
                for half in range(2):
                    wv, wt = wload(wview(f"wq{l}", 8)[:, :, half * 1024:(half + 1) * 1024], 8, 1024)
                    for cc in range(4):
                        c = half * 4 + cc
                        pq, pqs = nps(), nps()
                        for (pp, cb) in ((pq, cc * 256), (pqs, cc * 256 + 128)):
                            for k in range(8):
                                P.op("pe", I("matmul",
                                    ps[pp][:], lhsT=wv[:, k, cb:cb + 128], rhs=h[:, k, :], start=(k == 0), stop=(k == 7)),
                                    reads=[wt, t_h], writes=[t_ps[pp]])
                        rope_norm(pq, pqs, 0, j, t, [(slot(32 + 2 * c), 0, 64, [t_big[32 + 2 * c]]),
                                                      (slot(33 + 2 * c), 64, 128, [t_big[33 + 2 * c]])], None, 20)
                if STG < 4:
                    continue
                P.op("pool", I("memset", slotF(20), 0.0), writes=t_big[20:22])
                pending = [None]
                for c in range(8):
                    m = c // 4
                    for hf in range(2):
                        p0 = hf * 64
                        po = nps_acc()
                        Vl = (lambda kc: Ve[:, m, kc, :]) if hf == 0 else (lambda kc: Vo[:, m, kc, :])
                        M = 65 if hf == 0 else 128
                        sps = {}

                        def emit_s(kc):
                            sp_ = nps()
                            sps[kc] = sp_
                            P.op("pe", I("matmul",
                                ps[sp_][:], lhsT=kT[:, m, kc * 128:(kc + 1) * 128], rhs=slot(32 + 2 * c + hf),
                                start=True, stop=True), reads=t_kT + [t_big[32 + 2 * c + hf]], writes=[t_ps[sp_]])

                        def emit_pv(kc):
                            sp_ = sps[kc]
                            pslot = 16 + kc % 4
                            P.op("act", I("activation",
                                out=slot(pslot), in_=ps[sp_][:], func=AF.Exp, scale=0.125),
                                reads=[t_ps[sp_]], writes=[t_big[pslot]])
                            dump("P0", slot(16), [t_big[16]], False)
                            P.op("pe", I("matmul",
                                ps[po][0:M, :], lhsT=Vl(kc), rhs=slot(pslot), start=(kc == 0), stop=(kc == 15)),
                                reads=t_V + [t_big[pslot]], writes=[t_ps[po]])
                        emit_s(0)
                        emit_s(1)
                        for kc in range(16):
                            if kc + 2 < 16:
                                emit_s(kc + 2)
                            emit_pv(kc)
                        def mk_norm(c=c, hf=hf, p0=p0, po=po):
                            def f():
                                dp = 64 if hf == 0 else 0
                                rd = slotF(20)[dp:dp + 1, :]
                                P.op("dve", I("reciprocal", out=rd, in_=ps[po][dp:dp + 1, :]),
                                     reads=[t_ps[po]], writes=t_big[20:22])
                                dump("rd", rd, t_big[20:22], True, p0=dp)
                                pbc = nps()
                                P.op("pe", I("matmul",
                                    ps[pbc][:], lhsT=(selA if hf == 0 else selB)[:], rhs=slotF(20), start=True, stop=True),
                                    reads=t_big[20:22] + [t_const], writes=[t_ps[pbc]])
                                P.op("act", I("activation", out=slotF(22)[p0:p0 + 64, :], in_=ps[po][p0:p0 + 64, :], func=AF.Copy),
                                     reads=[t_ps[po]], writes=t_big[22:24])
                                P.op("dve", I("tensor_tensor",
                                    out=slot(8 + c)[p0:p0 + 64, :], in0=slotF(22)[p0:p0 + 64, :], in1=ps[pbc][p0:p0 + 64, :], op=ALU.mult),
                                    reads=t_big[22:24] + [t_ps[pbc]], writes=[t_big[8 + c]])
                                if hf == 1:
                                    dump("OT0", slot(8), [t_big[8]], False)
                            return f
                        if pending[0] is not None:
                            pending[0]()
                        pending[0] = mk_norm()
                if pending[0] is not None:
                    pending[0]()
                    pending[0] = None
                wv, wt = wload(wview(f"wo{l}", 8), 8, 1024)
                for mo in range(8):
                    pb = nps()
                    for c in range(8):
                        P.op("pe", I("matmul",
                            ps[pb][:], lhsT=wv[:, c, mo * 128:(mo + 1) * 128], rhs=slot(8 + c), start=(c == 0), stop=(c == 7)),
                            reads=[wt, t_big[8 + c]], writes=[t_ps[pb]])
                    resid_add(l, 16, b, t, mo, pb)
                    dump("x0", x[:, 0, 0:512], [t_x[0][0]], True)
                if STG >= 5:
                    ffn_tile(l, b, t)

        def gmlp_setup(l):
            j = l // 2
            lngF = auxF[:, 0:3072]
            P.op("sp", I("dma_start", out=lngF, in_=d_lng.ap()[j].to_broadcast([128, 3072])), writes=t_aux[0:12], dma="gs0")
            lnbF = bigF[:, 0:3072]
            P.op("sp", I("dma_start", out=lnbF, in_=d_lnb.ap()[j].to_broadcast([128, 3072])), writes=t_big[0:12], dma="gs1")
            P.op("dve", I("tensor_copy", out=aux[:, 12 * 512:18 * 512], in_=lnbF), reads=t_big[0:12], writes=t_aux[12:18])
            rtmp = bigF[0:1, 3072:3072 + 7168]
            P.op("sp", I("dma_start", out=rtmp[:, 0:3072], in_=d_binv.ap()[j]), writes=t_big[12:40], dma="gs2")
            P.op("sp", I("dma_start", out=rtmp[:, 3072:7168], in_=d_bsl.ap()[j]), writes=t_big[12:40], dma="gs3")
            P.op("dve", I("tensor_copy", out=rowb[0:1, 0:7168], in_=rtmp), reads=t_big[12:40], writes=[t_rowb])
            o, n = BLOB_OFF[f"wst{l}"]
            P.op("sp", I("dma_start", out=wst[:], in_=wb16.ap()[:, o:o + n].rearrange("p (s q) -> p s q", s=32)),
                 reads=[t_blob], writes=[t_wst], dma="gs4")

        def gmlp_tile(l, b, t):
            j = l // 2
            norm_tile(l, 0, b, t)
            for pc in range(3):
                wv, wt = wload(wview(f"gin{l}", 8)[:, :, 3072 + pc * 1024:3072 + (pc + 1) * 1024], 8, 1024)
                for tc in range(4):
                    for nt in range(2):
                        pb = nps()
                        col0 = pc * 1024 + nt * 512
                        P.op("pe", I("matmul",
                            ps[pb][:], lhsT=ones_b[:], rhs=rowb[:, col0:col0 + 512], start=True, stop=False),
                            reads=[t_const, t_rowb], writes=[t_ps[pb]])
                        for k in range(8):
                            P.op("pe", I("matmul",
                                ps[pb][:], lhsT=h[:, k, tc * 128:(tc + 1) * 128], rhs=wv[:, k, nt * 512:(nt + 1) * 512],
                                start=False, stop=(k == 7)), reads=[wt, t_h], writes=[t_ps[pb]])
                        sl = 24 + 6 * tc + pc * 2 + nt
                        P.op("act", I("activation", out=slot(sl), in_=ps[pb][:], func=AF.Gelu),
                             reads=[t_ps[pb]], writes=[t_big[sl]])

            def ln_tc(tc):
                vb = 24 + 6 * tc
                for q in range(6):
                    P.op("dve", I("bn_stats", out=stat[:, q * 6:q * 6 + 6], in_=slot(vb + q)),
                         reads=[t_big[vb + q]], writes=[t_stat])
                P.op("dve", I("bn_aggr", out=stat[:, 40:42], in_=stat[:, 0:36]), reads=[t_stat], writes=[t_stat])
                P.op("act", I("activation", out=stat[:, 42:43], in_=stat[:, 41:42], func=AF.Ln, bias=epst[:], scale=1.0),
                     reads=[t_stat, t_const], writes=[t_stat])
                P.op("act", I("activation", out=stat[:, 43:44], in_=stat[:, 42:43], func=AF.Exp, scale=-0.5),
                     reads=[t_stat], writes=[t_stat])
                P.op("dve", I("scalar_tensor_tensor", out=stat[:, 44:45], in0=stat[:, 40:41], scalar=-1.0, in1=stat[:, 43:44],
                                                             op0=ALU.mult, op1=ALU.mult), reads=[t_stat], writes=[t_stat])
                P.op("dve", I("tensor_copy", out=stat[:, 48 + 2 * tc:50 + 2 * tc], in_=stat[:, 43:45]), reads=[t_stat], writes=[t_stat2[tc]])
                for q in range(6):
                    ta = 18 + 2 * (q % 3)
                    P.op("act", I("activation",
                        out=axF(ta), in_=slot(vb + q), func=AF.Identity, bias=stat[:, 49 + 2 * tc:50 + 2 * tc], scale=stat[:, 48 + 2 * tc:49 + 2 * tc]),
                        reads=[t_big[vb + q], t_stat2[tc]], writes=t_aux[ta:ta + 2])
                    P.op("dve", I("tensor_tensor",
                        out=axF(ta), in0=axF(ta), in1=auxF[:, q * 512:(q + 1) * 512], op=ALU.mult),
                        reads=t_aux[ta:ta + 2] + t_aux[2 * q:2 * q + 2], writes=t_aux[ta:ta + 2])
                    P.op("pool", I("tensor_tensor",
                        out=slot(vb + q), in0=axF(ta), in1=aux[:, (12 + q) * 512:(13 + q) * 512], op=ALU.add),
                        reads=t_aux[ta:ta + 2] + [t_aux[12 + q]], writes=[t_big[vb + q]])
                    dump("gv0", slot(24), [t_big[24]], False)

            for pc in range(3):
                wv, wt = wload(wview(f"gin{l}", 8)[:, :, pc * 1024:(pc + 1) * 1024], 8, 1024)
                for jj in range(8):
                    ju = pc * 8 + jj
                    pb = nps()
                    for k in range(8):
                        P.op("pe", I("matmul",
                            ps[pb][:], lhsT=wv[:, k, jj * 128:(jj + 1) * 128], rhs=h[:, k, :], start=(k == 0), stop=(k == 7)),
                            reads=[wt, t_h], writes=[t_ps[pb]])
                    P.op("act", I("activation",
                        out=slot(ju), in_=ps[pb][:], func=AF.Gelu, bias=binu[:, j, ju:ju + 1], scale=1.0),
                        reads=[t_ps[pb], t_small], writes=[t_big[ju]])
                    dump("gu0", slot(0), [t_big[0]], False)
                    if ju % 6 == 2:
                        ln_tc(ju // 6)
            for tc in range(4):
                vb = 24 + 6 * tc
                vt = t_big[vb:vb + 6]
                for bk in range(8):
                    pb = nps()
                    P.op("pe", I("matmul",
                        ps[pb][:], lhsT=ones_b[:], rhs=rowb[:, 3072 + bk * 512:3072 + (bk + 1) * 512], start=True, stop=False),
                        reads=[t_const, t_rowb], writes=[t_ps[pb]])
                    for s4 in range(4):
                        s_ = bk * 4 + s4
                        jf = SLOTS[s_][0]
                        P.op("pe", I("matmul",
                            ps[pb][:, s4 * 128:(s4 + 1) * 128], lhsT=slot(vb, 6)[:, jf * 128:(jf + 1) * 128], rhs=wst[:, s_, :],
                            start=False, stop=(s4 == 3)), reads=vt + [t_wst], writes=[t_ps[pb]])
                    for s4 in range(4):
                        s_ = bk * 4 + s4
                        jf, _, p0, p1 = SLOTS[s_]
                        P.op("dve", I("tensor_tensor",
                            out=slot(jf)[p0:p1, tc * 128:(tc + 1) * 128], in0=ps[pb][p0:p1, s4 * 128:(s4 + 1) * 128],
                            in1=slot(jf)[p0:p1, tc * 128:(tc + 1) * 128], op=ALU.mult),
                            reads=[t_ps[pb], t_big[jf]], writes=[t_big[jf]])
            for pc in range(4):
                wv, wt = wload(wview(f"gout{l}", 24)[:, :, pc * 256:(pc + 1) * 256], 24, 256)
                for mm in range(2):
                    m = pc * 2 + mm
                    pb = nps()
                    for jf in range(24):
                        P.op("pe", I("matmul",
                            ps[pb][:], lhsT=wv[:, jf, mm * 128:(mm + 1) * 128], rhs=slot(jf), start=(jf == 0), stop=(jf == 23)),
                            reads=[wt, t_big[jf]], writes=[t_ps[pb]])
                    resid_add(l, 16, b, t, m, pb)
                    dump("gx0", x[:, 0, 0:512], [t_x[0][0]], True)
            ffn_tile(l, b, t)

        for b in range(nseq):
            for c in range(8):
                P.op("sp", I("dma_start", out=x[:, c, :], in_=xin.ap()[b, :, c, :]),
                     writes=t_x[c], dma=f"xin{c}")
            for l in range(nlayers):
                if STG < 2:
                    continue
                if l % 2 == 0:
                    attn_layer(l, b)
                else:
                    gmlp_setup(l)
                    for t in range(NT):
                        gmlp_tile(l, b, t)
            for c in range(8):
                P.op("sp", I("dma_start", out=yout.ap()[b, :, c, :], in_=x[:, c, :]),
                     reads=t_x[c], dma=f"out{c}")
        P.emit(final_waits=[f"out{c}" for c in range(8)] + [k for k in P.dma_cnt if k.startswith("dbg")])
    return nc


_CACHE = {}


def kernel(**inputs):
    inp = {k: np.asarray(v, dtype=np.float32) for k, v in inputs.items()}
    blob, sm = host_prep(inp)
    if "nc" not in _CACHE:
        _CACHE["nc"] = build()
    nc = _CACHE["nc"]
    x = inp["x"]
    c = inp["c"]
    in_maps = []
    for core in range(8):
        xs = x[2 * core:2 * core + 2]
        xT = np.ascontiguousarray(xs.reshape(2, SEQ, 8, 128).transpose(0, 3, 2, 1))
        cT = np.ascontiguousarray(c[2 * core:2 * core + 2].reshape(2, 8, 128).transpose(2, 1, 0))
        m = {"blob": blob, "xT": xT, "cT": cT}
        m.update(sm)
        in_maps.append(m)
    res = run_bass_kernel_spmd(nc, in_maps, core_ids=list(range(8)))
    out = np.empty((16, SEQ, D), np.float32)
    for core in range(8):
        yT = res.results[core]["yT"]
        out[2 * core:2 * core + 2] = yT.transpose(0, 3, 2, 1).reshape(2, SEQ, D)
    return out
```
